# Optimizing a Trainium2 kernel written in Bass

```python
import math
import jax, jax.numpy as jnp
from jax import lax
import numpy as np

D_MODEL = 4096
BATCH = 4
SEQ = 4096
DEPTH = 1

N_META = 16
D_FF = 11008
D_RGLRU = D_MODEL // 2
RG_HEADS = 8
RG_HEAD_DIM = D_RGLRU // RG_HEADS
CONV_WIDTH = 4
RG_C = 8.0
D_S5 = D_MODEL - D_RGLRU
S5_GROUP = 16
S5_GROUPS = D_S5 // S5_GROUP
S5_STATE = 64
DT_MIN = 0.001
DT_MAX = 0.1
D_MIX = D_RGLRU + D_S5
D_IN_PROJ = 2 * D_RGLRU + D_S5
EPS = 1e-6

kernel_name = "hymba_rglru_s5_macaron_layer"


def rmsnorm(x, g):
    xf = x.astype(jnp.float32)
    y = xf * lax.rsqrt(jnp.mean(xf * xf, axis=-1, keepdims=True) + EPS)
    return (y * g.astype(jnp.float32)).astype(x.dtype)


def swiglu(h, w_gate, w_up, w_down):
    return (jax.nn.silu(h @ w_gate) * (h @ w_up)) @ w_down


def rg_lru_mixer(u, gate, conv_w, conv_b, w_a, b_a, w_x, b_x, lam):
    bsz, t_len, _ = u.shape
    up = jnp.pad(u, ((0, 0), (CONV_WIDTH - 1, 0), (0, 0)))
    xc = conv_b + sum(up[:, k:k + t_len] * conv_w[k] for k in range(CONV_WIDTH))
    xh = xc.reshape(bsz, t_len, RG_HEADS, RG_HEAD_DIM)
    r = jax.nn.sigmoid(jnp.einsum('bthi,hij->bthj', xh, w_a) + b_a).reshape(bsz, t_len, D_RGLRU)
    i = jax.nn.sigmoid(jnp.einsum('bthi,hij->bthj', xh, w_x) + b_x).reshape(bsz, t_len, D_RGLRU)
    log_a = -RG_C * r.astype(jnp.float32) * jax.nn.softplus(-lam.astype(jnp.float32))
    a = jnp.exp(log_a)
    mult = jnp.sqrt(-jnp.expm1(2.0 * log_a))
    bx = mult * i.astype(jnp.float32) * xc.astype(jnp.float32)

    def step(h, ab):
        a_t, b_t = ab
        h = a_t * h + b_t
        return h, h

    _, hs = lax.scan(step, jnp.zeros((bsz, D_RGLRU), jnp.float32),
                     (jnp.swapaxes(a, 0, 1), jnp.swapaxes(bx, 0, 1)))
    h = jnp.swapaxes(hs, 0, 1)
    return (h * jax.nn.gelu(gate.astype(jnp.float32))).astype(u.dtype)


def s5_mixer(u, lam_re, lam_im, log_dt, b_re, b_im, c_re, c_im, d, glu_w, glu_b):
    bsz, t_len, _ = u.shape
    f32 = jnp.float32
    dt = jnp.exp(log_dt.astype(f32))[:, None]
    lam = lax.complex(lam_re.astype(f32), lam_im.astype(f32))
    lam_bar = jnp.exp(lam * dt)
    b = lax.complex(b_re.astype(f32), b_im.astype(f32))
    b_bar = ((lam_bar - 1.0) / lam)[..., None] * b
    ug = u.astype(f32).reshape(bsz, t_len, S5_GROUPS, S5_GROUP)
    bu = jnp.einsum('btgc,gnc->btgn', ug, b_bar)
    a_elems = jnp.broadcast_to(lam_bar[None, None], (1, t_len, S5_GROUPS, S5_STATE))

    def combine(e_i, e_j):
        a_i, b_i = e_i
        a_j, b_j = e_j
        return (a_j * a_i, a_j * b_i + b_j)

    _, states = lax.associative_scan(combine, (a_elems, bu), axis=1)
    c = lax.complex(c_re.astype(f32), c_im.astype(f32))
    y = jnp.einsum('btgn,gcn->btgc', states, c).real
    y = y + d.astype(f32).reshape(S5_GROUPS, S5_GROUP) * ug
    y = y.reshape(bsz, t_len, D_S5).astype(u.dtype)
    z = jax.nn.gelu(y)
    return z * jax.nn.sigmoid(z @ glu_w + glu_b)


def setup_inputs(seed: int = 0) -> dict:
    key = jax.random.key(seed)
    ks = jax.random.split(key, 40)
    f32 = jnp.float32
    nrm = lambda k, shape, s: jax.random.normal(k, shape, f32) * s
    L = DEPTH
    u = jax.random.uniform(ks[30], (L, D_RGLRU), f32, 0.9, 0.999)
    rg_lambda = jnp.log(u ** (1.0 / RG_C)) - jnp.log1p(-(u ** (1.0 / RG_C)))
    n_idx = jnp.arange(S5_STATE, dtype=f32)
    s5_lambda_re = -0.5 + nrm(ks[31], (L, S5_GROUPS, S5_STATE), 0.01)
    s5_lambda_im = math.pi * n_idx + nrm(ks[32], (L, S5_GROUPS, S5_STATE), 0.01)
    s5_log_dt = jax.random.uniform(ks[33], (L, S5_GROUPS), f32, math.log(DT_MIN), math.log(DT_MAX))
    return {
        "x": nrm(ks[0], (BATCH, SEQ, D_MODEL), 1.0),
        "meta_tokens": nrm(ks[1], (N_META, D_MODEL), 1.0),
        "ffn1_norm": 1.0 + nrm(ks[2], (L, D_MODEL), 0.02),
        "ffn1_w_gate": nrm(ks[3], (L, D_MODEL, D_FF), D_MODEL ** -0.5),
        "ffn1_w_up": nrm(ks[4], (L, D_MODEL, D_FF), D_MODEL ** -0.5),
        "ffn1_w_down": nrm(ks[5], (L, D_FF, D_MODEL), D_FF ** -0.5),
        "mix_norm": 1.0 + nrm(ks[6], (L, D_MODEL), 0.02),
        "w_in": nrm(ks[7], (L, D_MODEL, D_IN_PROJ), D_MODEL ** -0.5),
        "rg_conv_w": nrm(ks[8], (L, CONV_WIDTH, D_RGLRU), CONV_WIDTH ** -0.5),
        "rg_conv_b": nrm(ks[9], (L, D_RGLRU), 0.01),
        "rg_w_a": nrm(ks[10], (L, RG_HEADS, RG_HEAD_DIM, RG_HEAD_DIM), RG_HEAD_DIM ** -0.5),
        "rg_b_a": nrm(ks[11], (L, RG_HEADS, RG_HEAD_DIM), 0.01),
        "rg_w_x": nrm(ks[12], (L, RG_HEADS, RG_HEAD_DIM, RG_HEAD_DIM), RG_HEAD_DIM ** -0.5),
        "rg_b_x": nrm(ks[13], (L, RG_HEADS, RG_HEAD_DIM), 0.01),
        "rg_lambda": rg_lambda,
        "s5_lambda_re": s5_lambda_re,
        "s5_lambda_im": s5_lambda_im,
        "s5_log_dt": s5_log_dt,
        "s5_b_re": nrm(ks[14], (L, S5_GROUPS, S5_STATE, S5_GROUP), (2.0 * S5_GROUP) ** -0.5),
        "s5_b_im": nrm(ks[15], (L, S5_GROUPS, S5_STATE, S5_GROUP), (2.0 * S5_GROUP) ** -0.5),
        "s5_c_re": nrm(ks[16], (L, S5_GROUPS, S5_GROUP, S5_STATE), 1.0),
        "s5_c_im": nrm(ks[17], (L, S5_GROUPS, S5_GROUP, S5_STATE), 1.0),
        "s5_d": nrm(ks[18], (L, D_S5), 0.5),
        "s5_glu_w": nrm(ks[19], (L, D_S5, D_S5), D_S5 ** -0.5),
        "s5_glu_b": nrm(ks[20], (L, D_S5), 0.01),
        "rg_out_norm": 1.0 + nrm(ks[21], (L, D_RGLRU), 0.02),
        "s5_out_norm": 1.0 + nrm(ks[22], (L, D_S5), 0.02),
        "w_out": nrm(ks[23], (L, D_MIX, D_MODEL), D_MIX ** -0.5),
        "ffn2_norm": 1.0 + nrm(ks[24], (L, D_MODEL), 0.02),
        "ffn2_w_gate": nrm(ks[25], (L, D_MODEL, D_FF), D_MODEL ** -0.5),
        "ffn2_w_up": nrm(ks[26], (L, D_MODEL, D_FF), D_MODEL ** -0.5),
        "ffn2_w_down": nrm(ks[27], (L, D_FF, D_MODEL), D_FF ** -0.5),
        "final_norm": 1.0 + nrm(ks[28], (D_MODEL,), 0.02),
    }


def reference(x, meta_tokens, ffn1_norm, ffn1_w_gate, ffn1_w_up, ffn1_w_down, mix_norm, w_in,
              rg_conv_w, rg_conv_b, rg_w_a, rg_b_a, rg_w_x, rg_b_x, rg_lambda,
              s5_lambda_re, s5_lambda_im, s5_log_dt, s5_b_re, s5_b_im, s5_c_re, s5_c_im, s5_d,
              s5_glu_w, s5_glu_b, rg_out_norm, s5_out_norm, w_out,
              ffn2_norm, ffn2_w_gate, ffn2_w_up, ffn2_w_down, final_norm):
    bsz = x.shape[0]
    meta = jnp.broadcast_to(meta_tokens[None].astype(x.dtype), (bsz, N_META, D_MODEL))
    h = jnp.concatenate([meta, x], axis=1)
    for l in range(DEPTH):
        h = h + 0.5 * swiglu(rmsnorm(h, ffn1_norm[l]), ffn1_w_gate[l], ffn1_w_up[l], ffn1_w_down[l])
        proj = rmsnorm(h, mix_norm[l]) @ w_in[l]
        u_rg, g_rg, u_s5 = jnp.split(proj, [D_RGLRU, 2 * D_RGLRU], axis=-1)
        y_rg = rg_lru_mixer(u_rg, g_rg, rg_conv_w[l], rg_conv_b[l], rg_w_a[l], rg_b_a[l],
                            rg_w_x[l], rg_b_x[l], rg_lambda[l])
        y_s5 = s5_mixer(u_s5, s5_lambda_re[l], s5_lambda_im[l], s5_log_dt[l], s5_b_re[l], s5_b_im[l],
                        s5_c_re[l], s5_c_im[l], s5_d[l], s5_glu_w[l], s5_glu_b[l])
        y = jnp.concatenate([rmsnorm(y_rg, rg_out_norm[l]), rmsnorm(y_s5, s5_out_norm[l])], axis=-1)
        h = h + y @ w_out[l]
        h = h + 0.5 * swiglu(rmsnorm(h, ffn2_norm[l]), ffn2_w_gate[l], ffn2_w_up[l], ffn2_w_down[l])
    out = rmsnorm(h, final_norm)
    return out[:, N_META:]
```

```python
import math
from contextlib import ExitStack
import numpy as np
import concourse.bass as bass
import concourse.mybir as mybir
from concourse.bass_utils import run_bass_kernel_spmd

F32 = mybir.dt.float32
BF16 = mybir.dt.bfloat16
AF = mybir.ActivationFunctionType
ALU = mybir.AluOpType

EPS = 1e-6
TWO_PI = float(2.0 * math.pi)
PI_SAFE = 3.1415925
MAGIC = 12582912.0


def full_cfg():
    return dict(D=4096, DFF=11008, NTOK=4096, PRE=16, NCORES=4)


class Buf:
    __slots__ = ("name", "last_w", "readers", "dram", "sems")

    def __init__(self, name, dram=False):
        self.name = name
        self.last_w = None
        self.readers = {}
        self.dram = dram
        self.sems = {}


class DSem:
    __slots__ = ("h", "count")

    def __init__(self, h):
        self.h = h
        self.count = 0


class Op:
    __slots__ = ("eng", "fn", "deps", "kind", "sem", "val", "signal")

    def __init__(self, eng, fn, kind):
        self.eng = eng
        self.fn = fn
        self.kind = kind
        self.deps = []
        self.sem = None
        self.val = 0
        self.signal = False


ENGS = ("pe", "act", "dve", "pool", "sp")


class Prog:
    def __init__(self, nc, stack):
        self.nc = nc
        self.stack = stack
        self.ops = []
        self.nsem = 0
        self.prog_sem = {e: self._sem("prog_" + e) for e in ("pe", "act", "dve", "pool")}
        self.last_op = {e: None for e in ENGS}
        self.dsems = []
        self.pending = {e: [] for e in ENGS}

    def _sem(self, name):
        self.nsem += 1
        return self.stack.enter_context(self.nc.semaphore(name))

    def dsem(self, name):
        s = DSem(self._sem("d_" + name))
        self.dsems.append(s)
        return s

    def _track(self, op, reads, writes):
        deps = op.deps
        for b in reads:
            if b.last_w is not None:
                deps.append(b.last_w)
        for b in writes:
            if b.last_w is not None:
                deps.append(b.last_w)
            deps.extend(b.readers.values())
        key = op.eng if op.kind == "c" else ("d", id(op.sem))
        for b in reads:
            b.readers[key] = op
        for b in writes:
            b.last_w = op
            b.readers = {}
        if op.kind == "d":
            op.deps = deps = [d for d in deps if not (d.kind == "d" and d.sem is op.sem)]
        if self.pending[op.eng]:
            deps.extend(self.pending[op.eng])
            self.pending[op.eng] = []
        self.ops.append(op)
        self.last_op[op.eng] = op

    def op(self, eng, fn, reads=(), writes=()):
        o = Op(eng, fn, "c")
        self._track(o, reads, writes)
        return o

    def dma(self, queue, fn, sem, reads=(), writes=()):
        o = Op(queue, fn, "d")
        if sem is None:
            tgt = [b for b in writes if not b.dram]
            kind = "l"
            if not tgt:
                tgt = [b for b in reads if not b.dram]
                kind = "s"
            b = tgt[0]
            kind += "w" if queue == "pool" else "h"
            if kind not in b.sems:
                b.sems[kind] = self.dsem("%s_%s%d" % (kind, b.name.replace("@", "_"), self.nsem))
            sem = b.sems[kind]
        o.sem = sem
        sem.count += 16
        o.val = sem.count
        self._track(o, reads, writes)
        return o

    def barrier(self):
        lasts = []
        for e in ("pe", "act", "dve", "pool"):
            for o in reversed(self.ops):
                if o.kind == "c" and o.eng == e:
                    lasts.append(o)
                    break
        dm = []
        seen = set()
        for o in reversed(self.ops):
            if o.kind == "d" and id(o.sem) not in seen:
                seen.add(id(o.sem))
                dm.append(o)
        for e in ENGS:
            self.pending[e] = lasts + dm

    def emit(self, block):
        for o in self.ops:
            for d in o.deps:
                if d.kind == "c" and not (o.kind == "c" and d.eng == o.eng and o.eng == "pe"):
                    d.signal = True
        cnt = {e: 0 for e in ENGS}
        for o in self.ops:
            if o.kind == "c" and o.signal:
                cnt[o.eng] += 1
                o.val = cnt[o.eng]
                o.sem = self.prog_sem[o.eng]
        per = {e: [] for e in ENGS}
        for o in self.ops:
            per[o.eng].append(o)
        final_waits = [(s.h, s.count) for s in self.dsems if s.count > 0]
        final_waits += [(self.prog_sem[e], cnt[e]) for e in ("pe", "act", "dve", "pool") if cnt[e] > 0]

        def run(eng_name, e):
            waited = {}
            for o in per[eng_name]:
                need = {}
                for d in o.deps:
                    if d.kind == "c" and o.kind == "c" and d.eng == o.eng and o.eng == "pe":
                        continue
                    h = d.sem if d.kind == "c" else d.sem.h
                    k = id(h)
                    if d.val > need.get(k, (None, 0))[1]:
                        need[k] = (h, d.val)
                for k, (h, v) in need.items():
                    if waited.get(k, 0) < v:
                        e.wait_ge(h, v)
                        waited[k] = v
                ins = o.fn(e)
                if o.kind == "d":
                    ins.then_inc(o.sem.h, 16)
                elif o.signal:
                    ins.then_inc(o.sem, 1)
            if eng_name == "sp":
                for h, v in final_waits:
                    e.wait_ge(h, v)

        block.tensor(lambda e: run("pe", e))
        block.scalar(lambda e: run("act", e))
        block.vector(lambda e: run("dve", e))
        block.gpsimd(lambda e: run("pool", e))
        block.sync(lambda e: run("sp", e))


def build_program(cfg):
    D, DFF, NTOK, PRE = cfg["D"], cfg["DFF"], cfg["NTOK"], cfg["PRE"]
    DRG = D // 2
    DS5 = D // 2
    NDT = D // 128
    NFT = DFF // 128
    NFH = NFT // 2
    NCT = DRG // 128
    NHD = DRG // 256
    NG = DS5 // 16
    NPR = NG // 2
    NTL = NTOK // 512
    TT = PRE + NTOK
    W = 512 + PRE
    assert NFT % 2 == 0 and NTOK % 512 == 0 and PRE == 16

    nc = bass.Bass("TRN2", target_bir_lowering=False)
    di = lambda name, shape: nc.dram_tensor(name, list(shape), F32, kind="ExternalInput").ap()
    x_d = di("x", [TT, D])
    g_ffn1_d = di("ffn1_norm", [D]); g_mix_d = di("mix_norm", [D]); g_ffn2_d = di("ffn2_norm", [D]); g_fin_d = di("final_norm", [D])
    g_rgo_d = di("rg_out_norm", [DRG]); g_s5o_d = di("s5_out_norm", [DS5])
    w1g_d = di("ffn1_w_gate", [D, DFF]); w1u_d = di("ffn1_w_up", [D, DFF]); w1d_d = di("ffn1_w_down", [DFF, D])
    w2g_d = di("ffn2_w_gate", [D, DFF]); w2u_d = di("ffn2_w_up", [D, DFF]); w2d_d = di("ffn2_w_down", [DFF, D])
    win_d = di("w_in", [D, 3 * DRG]); wout_d = di("w_out", [D, D])
    cw_d = di("rg_conv_w", [4, DRG]); cb_d = di("rg_conv_b", [DRG])
    wa_d = di("rg_w_a", [NHD, 256, 256]); ba_d = di("rg_b_a", [DRG]); wx_d = di("rg_w_x", [NHD, 256, 256]); bx_d = di("rg_b_x", [DRG])
    lam_d = di("rg_lambda", [DRG])
    slr_d = di("s5_lambda_re", [NG, 64]); sli_d = di("s5_lambda_im", [NG, 64]); sdt_d = di("s5_log_dt", [NG])
    sbr_d = di("s5_b_re", [NG, 64, 16]); sbi_d = di("s5_b_im", [NG, 64, 16])
    scr_d = di("s5_c_re", [NG, 16, 64]); sci_d = di("s5_c_im", [NG, 16, 64])
    sd_d = di("s5_d", [DS5]); gw_d = di("s5_glu_w", [DS5, DS5]); gb_d = di("s5_glu_b", [DS5])
    out_d = nc.dram_tensor("out", [NTOK, D], F32, kind="ExternalOutput").ap()
    dint = lambda name, shape, dt=F32: nc.dram_tensor(name, list(shape), dt, kind=("ExternalOutput" if cfg.get("DEBUG") else "Internal")).ap()
    H0 = dint("H0", [D, TT]); H1 = dint("H1", [D, TT]); H2 = dint("H2", [D, TT]); H3 = dint("H3", [D, TT])
    PROJ = dint("PROJ", [3 * DRG, TT])
    YH = dint("YH", [D, NTOK], BF16)
    RSD = dint("RSD", [2, 128, NTOK])
    TABD = dint("TABD", [NPR, 128, 2, 512])
    LHD = dint("LHD", [NPR, 128, 4, 128], BF16)
    DBG = dint("DBG", [128, 4, 512 + PRE])

    stack = ExitStack()
    with stack:
        P = Prog(nc, stack)
        sb = lambda name, shape, dt=F32: stack.enter_context(nc.sbuf_tensor(name, list(shape), dt))[:]
        PS = [stack.enter_context(nc.psum_tensor("ps%d" % i, [128, 512], F32))[:] for i in range(8)]
        PSB = [Buf("ps%d" % i) for i in range(8)]
        rot = {"A": [0, 1, 2, 7], "P": [3, 4]}
        rot_i = {"A": 0, "P": 0}

        def bank(pool):
            i = rot[pool][rot_i[pool] % len(rot[pool])]
            rot_i[pool] += 1
            return i
        SSQM, SSQP, BT = 5, 6, 7

        ident = sb("ident", [128, 128]); iot_a = sb("iot_a", [128, 128]); iot_b = sb("iot_b", [128, 128])
        ones_b = sb("ones_b", [128, 128], BF16)
        iota_f = sb("iota_f", [128, 512])
        epst = sb("epst", [128, 1]);
        G1 = sb("g_ffn1", [128, NDT]); GM = sb("g_mix", [128, NDT]); G2 = sb("g_ffn2", [128, NDT]); GF = sb("g_fin", [128, NDT])
        GRO = sb("g_rgo", [128, NCT]); GSO = sb("g_s5o", [128, NCT])
        CW = sb("cw", [128, 4, NCT]); CB = sb("cb", [128, NCT]); BA = sb("ba", [128, NCT]); BX = sb("bx", [128, NCT])
        COEF = sb("coef", [128, NCT]); COEF2 = sb("coef2", [128, NCT])
        SD = sb("sd", [128, NCT]); GB = sb("gb", [128, NCT])
        HST = sb("hst", [128, NCT])
        WA = sb("wa", [128, NHD, 2, 256], BF16); WX = sb("wx", [128, NHD, 2, 256], BF16)
        RHO = sb("rho", [128, NPR]); OM = sb("om", [128, NPR])
        CL512 = sb("cl512", [128, NPR]); SL512 = sb("sl512", [128, NPR]); NSL512 = sb("nsl512", [128, NPR])
        CL16 = sb("cl16", [128, NPR]); SL16 = sb("sl16", [128, NPR]); NSL16 = sb("nsl16", [128, NPR])
        ZST = sb("zst", [128, NPR, 2])
        b_const = Buf("const")
        ARENA_B = 182 * 1024
        arena = sb("arena", [128, ARENA_B // 2], BF16)
        a_off = [0]

        def carve(shape_free, dt, nbuf=1):
            esz = 4 if dt == F32 else 2
            n = int(np.prod(shape_free))
            res = []
            for _ in range(nbuf):
                nb = (n * esz + 31) // 32 * 32
                o = a_off[0]
                assert o + nb <= ARENA_B, ("arena overflow", o + nb)
                v = arena[:, o // 2:(o + n * esz) // 2]
                if dt == F32:
                    v = v.bitcast(F32)
                if len(shape_free) == 2:
                    v = v.rearrange("p (a b) -> p a b", a=shape_free[0])
                elif len(shape_free) == 3:
                    v = v.rearrange("p (a b c) -> p a b c", a=shape_free[0], b=shape_free[1])
                a_off[0] = o + nb
                res.append((v, Buf("arena@%d" % o)))
            return res

        class Rot:
            def __init__(self, items):
                self.items = items
                self.i = 0

            def next(self):
                it = self.items[self.i % len(self.items)]
                self.i += 1
                return it

        sem_c = P.dsem("const")

        nc_allow = stack.enter_context(nc.allow_non_contiguous_dma(reason="small strided parameter loads"))

        def vec_fm(dst, src, n):
            P.dma("sp", lambda e, dst=dst, src=src: e.dma_start(out=dst[:], in_=src.rearrange("(t p) -> p t", p=128)),
                  sem_c, writes=[b_const])

        for dst, src, n in ((G1, g_ffn1_d, NDT), (GM, g_mix_d, NDT), (G2, g_ffn2_d, NDT), (GF, g_fin_d, NDT),
                            (GRO, g_rgo_d, NCT), (GSO, g_s5o_d, NCT), (CB, cb_d, NCT), (BA, ba_d, NCT), (BX, bx_d, NCT),
                            (COEF, lam_d, NCT), (SD, sd_d, NCT), (GB, gb_d, NCT)):
            vec_fm(dst, src, n)
        for k in range(4):
            P.dma("sp", lambda e, k=k: e.dma_start(out=CW[:, k, :], in_=cw_d[k, :].rearrange("(t p) -> p t", p=128)),
                  sem_c, writes=[b_const])
        sem_cw = P.dsem("constw")
        b_wax = Buf("wax")
        P.dma("pool", lambda e: e.dma_start(out=WA[:], in_=wa_d.rearrange("h (it p) j -> p h it j", p=128)), sem_cw, writes=[b_wax])
        P.dma("pool", lambda e: e.dma_start(out=WX[:], in_=wx_d.rearrange("h (it p) j -> p h it j", p=128)), sem_cw, writes=[b_wax])
        a_off[0] = 0
        (LRE, _), (LIM, _), (LDT, _), (t0, _), (t1, _), (t2, _), (t3, _), (KR, _), (KI, _) = carve([NPR], F32, 9)
        (BR0, _), (BI0, _), (BBR, _), (BBI, _), (TB, _) = carve([NPR, 16], F32, 5)
        (MRE, _), (MIM, _) = carve([NPR, 32], F32, 2)
        (CNR, _), (CNI, _) = carve([NCT, 64], F32, 2)
        (CDR, _), (CDI, _) = carve([NCT, 128], F32, 2)
        (LT, bLT), = carve([NPR, 4, 128], BF16)
        CRE = LT[:, :, 0, :]; CIN = LT[:, :, 1, :]; BRE = LT[0:32, :, 2, :]; BIM = LT[0:32, :, 3, :]
        bS = Buf("s5setup")
        cdma = lambda q, fn: P.dma(q, fn, sem_c, writes=[bS])
        cdma("sp", lambda e: e.dma_start(out=LRE, in_=slr_d.rearrange("(pr g2) n -> (g2 n) pr", g2=2)))
        cdma("sp", lambda e: e.dma_start(out=LIM, in_=sli_d.rearrange("(pr g2) n -> (g2 n) pr", g2=2)))
        for g2 in range(2):
            cdma("sp", lambda e, g2=g2: e.dma_start(out=LDT[g2 * 64:(g2 + 1) * 64, :],
                                                     in_=sdt_d.rearrange("(pr g2) -> g2 pr", g2=2)[g2:g2 + 1, :].broadcast_to([64, NPR])))
        cdma("sp", lambda e: e.dma_start(out=BR0, in_=sbr_d.rearrange("(pr g2) n c -> (g2 n) pr c", g2=2)))
        cdma("sp", lambda e: e.dma_start(out=BI0, in_=sbi_d.rearrange("(pr g2) n c -> (g2 n) pr c", g2=2)))
        cdma("sp", lambda e: e.dma_start(out=CNR, in_=scr_d.rearrange("(ct g8) c n -> (g8 c) ct n", g8=8)))
        cdma("sp", lambda e: e.dma_start(out=CNI, in_=sci_d.rearrange("(ct g8) c n -> (g8 c) ct n", g8=8)))
        P.barrier()
        P.op("pool", lambda e: e.iota(iot_a[:], [[1, 128]], base=0, channel_multiplier=0, allow_small_or_imprecise_dtypes=True), writes=[b_const])
        P.op("pool", lambda e: e.iota(iot_b[:], [[0, 128]], base=0, channel_multiplier=1, allow_small_or_imprecise_dtypes=True), writes=[b_const])
        P.op("pool", lambda e: e.iota(iota_f[:], [[1, 512]], base=0, channel_multiplier=0, allow_small_or_imprecise_dtypes=True), writes=[b_const])
        P.op("dve", lambda e: e.tensor_tensor(out=ident[:], in0=iot_a[:], in1=iot_b[:], op=ALU.is_equal), reads=[b_const], writes=[b_const])
        P.op("dve", lambda e: e.memset(ones_b[:], 1.0), writes=[b_const])
        P.op("dve", lambda e: e.memset(epst[:], EPS), writes=[b_const])
        P.op("dve", lambda e: e.memset(HST[:], 0.0), writes=[b_const])
        P.op("dve", lambda e: e.memset(ZST[:], 0.0), writes=[b_const])
        P.op("act", lambda e: e.activation(out=COEF[:], in_=COEF[:], func=AF.Exp, scale=-1.0), reads=[b_const], writes=[b_const])
        P.op("dve", lambda e: e.tensor_scalar(out=COEF[:], in0=COEF[:], scalar1=1.0, scalar2=None, op0=ALU.add), reads=[b_const], writes=[b_const])
        P.op("act", lambda e: e.activation(out=COEF[:], in_=COEF[:], func=AF.Ln), reads=[b_const], writes=[b_const])
        P.op("dve", lambda e: e.tensor_scalar(out=COEF2[:], in0=COEF[:], scalar1=-16.0, scalar2=None, op0=ALU.mult), reads=[b_const], writes=[b_const])
        P.op("dve", lambda e: e.tensor_scalar(out=COEF[:], in0=COEF[:], scalar1=-8.0, scalar2=None, op0=ALU.mult), reads=[b_const], writes=[b_const])

        S = lambda eng, fn: P.op(eng, fn, reads=[bS, b_const], writes=[bS])
        TS = lambda o, i, s1, s2, op0, op1=None: (lambda e: e.tensor_scalar(out=o, in0=i, scalar1=s1, scalar2=s2, op0=op0, op1=op1) if op1 is not None
                                                  else e.tensor_scalar(out=o, in0=i, scalar1=s1, scalar2=None, op0=op0))
        TTn = lambda o, a, b, op: (lambda e: e.tensor_tensor(out=o, in0=a, in1=b, op=op))
        ACTf = lambda o, i, f, **kw: (lambda e: e.activation(out=o, in_=i, func=f, **kw))

        def sincos(dst_c, dst_s, ang, tmp, tmp2):
            for dst, shift in ((dst_s, 0.0), (dst_c, math.pi / 2)):
                S("dve", TS(tmp2, ang, shift, None, ALU.add))
                S("dve", TS(tmp, tmp2, 1.0 / TWO_PI, MAGIC, ALU.mult, ALU.add))
                S("dve", TS(tmp, tmp, MAGIC, -TWO_PI, ALU.subtract, ALU.mult))
                S("dve", TTn(tmp, tmp, tmp2, ALU.add))
                S("dve", TS(tmp, tmp, -PI_SAFE, PI_SAFE, ALU.max, ALU.min))
                S("act", ACTf(dst, tmp, AF.Sin))

        S("act", ACTf(LDT, LDT, AF.Exp))
        S("dve", TTn(t0, LRE, LDT, ALU.mult))
        S("act", ACTf(RHO, t0, AF.Exp))
        S("dve", TTn(OM, LIM, LDT, ALU.mult))
        sincos(t0, t1, OM, t2, t3)
        S("dve", TTn(t0, t0, RHO, ALU.mult))
        S("dve", TTn(t1, t1, RHO, ALU.mult))
        S("dve", TS(t0, t0, -1.0, None, ALU.add))
        S("dve", TTn(t2, LRE, LRE, ALU.mult))
        S("dve", TTn(t3, LIM, LIM, ALU.mult))
        S("dve", TTn(t2, t2, t3, ALU.add))
        S("dve", lambda e: e.reciprocal(out=t2, in_=t2))
        S("dve", TTn(KR, t0, LRE, ALU.mult))
        S("dve", TTn(t3, t1, LIM, ALU.mult))
        S("dve", TTn(KR, KR, t3, ALU.add))
        S("dve", TTn(KR, KR, t2, ALU.mult))
        S("dve", TTn(KI, t1, LRE, ALU.mult))
        S("dve", TTn(t3, t0, LIM, ALU.mult))
        S("dve", TTn(KI, KI, t3, ALU.subtract))
        S("dve", TTn(KI, KI, t2, ALU.mult))
        for L, cl, sl, nsl in ((512.0, CL512, SL512, NSL512), (16.0, CL16, SL16, NSL16)):
            S("dve", TS(t0, OM, L, None, ALU.mult))
            sincos(cl[:], sl[:], t0, t2, t3)
            S("dve", TS(nsl[:], sl[:], -1.0, None, ALU.mult))
        kb = lambda k: k.unsqueeze(2).broadcast_to([128, NPR, 16])
        S("dve", TTn(BBR, BR0, kb(KR), ALU.mult))
        S("dve", TTn(TB, BI0, kb(KI), ALU.mult))
        S("dve", TTn(BBR, BBR, TB, ALU.subtract))
        S("dve", TTn(BBI, BI0, kb(KR), ALU.mult))
        S("dve", TTn(TB, BR0, kb(KI), ALU.mult))
        S("dve", TTn(BBI, BBI, TB, ALU.add))
        P.op("pool", lambda e: e.memset(LT, 0.0), writes=[bLT])
        for M_, BB_ in ((MRE, BBR), (MIM, BBI)):
            S("dve", lambda e, M_=M_: e.memset(M_, 0.0))
            S("dve", lambda e, M_=M_, BB_=BB_: e.tensor_copy(out=M_[0:64, :, 0:16], in_=BB_[0:64, :, :]))
            S("dve", lambda e, M_=M_, BB_=BB_: e.tensor_copy(out=M_[64:128, :, 16:32], in_=BB_[64:128, :, :]))
        for M_, dstT in ((MRE, BRE), (MIM, BIM)):
            for p0 in range(0, NPR, 4):
                np_ = min(4, NPR - p0)
                bk = bank("A")
                for j in range(np_):
                    P.op("pe", lambda e, bk=bk, j=j, M_=M_, p0=p0: e.transpose(out=PS[bk][0:32, j * 128:(j + 1) * 128], in_=M_[:, p0 + j, :], identity=ident[:]),
                         reads=[bS, b_const], writes=[PSB[bk]])
                P.op("act", lambda e, bk=bk, dstT=dstT, p0=p0, np_=np_: e.activation(
                    out=dstT[:, p0:p0 + np_, :], in_=PS[bk][0:32, 0:np_ * 128].rearrange("p (a b) -> p a b", a=np_), func=AF.Copy),
                    reads=[PSB[bk]], writes=[bLT])
        for CN_, CD_, dstC, sgn in ((CNR, CDR, CRE, 1.0), (CNI, CDI, CIN, -1.0)):
            S("dve", lambda e, CN_=CN_, CD_=CD_: e.tensor_copy(out=CD_[:, :, 0:64], in_=CN_))
            S("dve", lambda e, CN_=CN_, CD_=CD_: e.tensor_copy(out=CD_[:, :, 64:128], in_=CN_))
            for ct in range(NCT):
                bk = bank("A")
                P.op("pe", lambda e, bk=bk, CD_=CD_, ct=ct: e.transpose(out=PS[bk][:, 0:128], in_=CD_[:, ct, :], identity=ident[:]),
                     reads=[bS, b_const], writes=[PSB[bk]])
                for j4 in range(4):
                    pr = ct * 4 + j4
                    for g2 in range(2):
                        g8 = 2 * j4 + g2
                        P.op("act", lambda e, bk=bk, dstC=dstC, pr=pr, j4=j4, g2=g2, g8=g8, sgn=sgn: e.activation(
                            out=dstC[g2 * 64:(g2 + 1) * 64, pr, 32 * j4 + 16 * g2:32 * j4 + 16 * g2 + 16],
                            in_=PS[bk][g2 * 64:(g2 + 1) * 64, g8 * 16:(g8 + 1) * 16], func=AF.Copy, scale=sgn),
                            reads=[PSB[bk]], writes=[bLT])
        dL = Buf("LHD", dram=True)
        P.dma("sp", lambda e: e.dma_start(out=LHD.rearrange("pr p f c -> p pr f c"), in_=LT), None, reads=[bLT], writes=[dL])
        a_off_tab = a_off[0]
        tabs = Rot(carve([2, 512], F32, 2))
        (ANG, _), (AN2, _), (RED, _) = carve([512], F32, 3)
        bT = Buf("tabtmp")
        dT = Buf("TABD", dram=True)
        npi = sb("npi", [128, 1])
        P.op("dve", lambda e: e.memset(npi[:], -math.pi), writes=[b_const])
        for pr in range(NPR):
            tab, btab = tabs.next()
            P.op("dve", lambda e, pr=pr: e.tensor_scalar(out=ANG, in0=iota_f[:], scalar1=OM[:, pr:pr + 1], scalar2=None, op0=ALU.mult),
                 reads=[b_const, bS], writes=[bT])
            for idx, shift in ((1, 0.0), (0, math.pi / 2)):
                P.op("dve", TS(AN2, ANG, shift, None, ALU.add), reads=[bT], writes=[bT])
                P.op("dve", TS(RED, AN2, 1.0 / TWO_PI, MAGIC, ALU.mult, ALU.add), reads=[bT], writes=[bT])
                P.op("dve", TS(RED, RED, MAGIC, -TWO_PI, ALU.subtract, ALU.mult), reads=[bT], writes=[bT])
                P.op("dve", TTn(RED, RED, AN2, ALU.add), reads=[bT], writes=[bT])
                P.op("dve", TS(RED, RED, -PI_SAFE, PI_SAFE, ALU.max, ALU.min), reads=[bT], writes=[bT])
                P.op("act", lambda e, tab=tab, idx=idx: e.activation(out=tab[:, idx, :], in_=RED, func=AF.Sin), reads=[bT], writes=[btab])
            P.dma("sp", lambda e, tab=tab, pr=pr: e.dma_start(out=TABD[pr], in_=tab), None, reads=[btab], writes=[dT])
        P.barrier()

        a_off[0] = 0
        ring = carve([8192], BF16, 4)
        ringF = arena[:, 0:4 * 8192].bitcast(F32).rearrange("p (s d) -> p s d", s=4)
        ring_i = [0]
        (XN, bXN), = carve([NDT, W], BF16)
        NACT = max(NFH, cfg.get('ACT_PAD', 0))
        (ACT_, _), = carve([NACT, W], BF16)
        bACT = [Buf("act%d" % j) for j in range(NFH)]
        if NACT * W >= 2 * D:
            XPRE = ACT_.rearrange("p a b -> p (a b)")[:, 0:2 * D].bitcast(F32)
        else:
            (XPRE, _), = carve([D], F32)
        bXPRE = Buf("xpre")
        HS = Rot(carve([W], F32, 3)); OS = Rot(carve([W], F32, 3)); SG = Rot(carve([W], F32, 2)); SQ = Rot(carve([W], BF16, 2))
        (RSTD, bRSTD), = carve([W], F32)
        (RR, bRR), (RS_, bRS) = carve([512], F32, 2)
        T12 = Rot(carve([2, 512], F32, 2))
        stage13_end = a_off[0]

        def segs(t, with_pre=True):
            res = []
            if t == 0 and with_pre:
                res.append((0, PRE, 0, True))
            res.append((PRE, 512, PRE + 512 * t, False))
            return res

        def ring_next():
            i = ring_i[0] % 4
            ring_i[0] += 1
            return i

        def load_w(slot, off, src_ap, shape3):
            v, b = ring[slot]
            dst = v[:, off:off + shape3[0] * shape3[1]].rearrange("p (a b) -> p a b", a=shape3[0])
            P.dma("pool", lambda e, dst=dst, src_ap=src_ap: e.dma_start(out=dst, in_=src_ap), None, writes=[b])
            return dst

        def ssq_accum(o_ap, bo, sg, first, last, dve_sq=False):
            c0, n, _, is_pre = sg
            sq, bsq = SQ.next()
            P.op("act", lambda e, sq=sq, o_ap=o_ap: e.activation(out=sq[:, c0:c0 + n], in_=o_ap[:, c0:c0 + n], func=AF.Square), reads=[bo], writes=[bsq])
            bk = SSQP if is_pre else SSQM
            P.op("pe", lambda e, sq=sq, bk=bk: e.matmul(PS[bk][:, 0:n], lhsT=ones_b[:], rhs=sq[:, c0:c0 + n], start=first, stop=last),
                 reads=[bsq, b_const], writes=[PSB[bk]])

        def make_rstd(dst, bdst, sgl, dim, c_of=lambda sg: sg[0]):
            for sg in sgl:
                c0, n, _, is_pre = sg
                bk = SSQP if is_pre else SSQM
                cc = c_of(sg)
                P.op("act", lambda e, bk=bk, n=n, cc=cc: e.activation(out=dst[:, cc:cc + n], in_=PS[bk][:, 0:n], func=AF.Sqrt, bias=epst[:, 0:1], scale=1.0 / dim),
                     reads=[PSB[bk], b_const], writes=[bdst])
                P.op("dve", lambda e, n=n, cc=cc: e.reciprocal(out=dst[:, cc:cc + n], in_=dst[:, cc:cc + n]), reads=[bdst], writes=[bdst])

        hbuf = {}

        def hb(H, dt, t):
            k = (id(H), dt, t)
            if k not in hbuf:
                hbuf[k] = Buf("H", dram=True)
            return hbuf[k]

        def make_xn(Hsrc, G, t, sgl):
            for dt in range(NDT):
                hs, bhs = HS.next()
                for sg in sgl:
                    c0, n, d0, _ = sg
                    P.dma("sp", lambda e, hs=hs, dt=dt, c0=c0, n=n, d0=d0: e.dma_start(out=hs[:, c0:c0 + n], in_=Hsrc[dt * 128:(dt + 1) * 128, d0:d0 + n]),
                          None, reads=[hb(Hsrc, dt, t)], writes=[bhs])
                c0 = sgl[0][0]
                c1 = sgl[-1][0] + sgl[-1][1]
                P.op("dve", lambda e, hs=hs, dt=dt, c0=c0, c1=c1: e.scalar_tensor_tensor(
                    out=XN[:, dt, c0:c1], in0=hs[:, c0:c1], scalar=G[:, dt:dt + 1], in1=RSTD[:, c0:c1], op0=ALU.mult, op1=ALU.mult),
                    reads=[bhs, bRSTD, b_const], writes=[bXN])

        def ffn(t, sgl, Hsrc, Hdst, wg_d, wu_d, wd_d):
            for hf in range(2):
                for jj in range(NFH):
                    j = hf * NFH + jj
                    slot = ring_next()
                    wg = load_w(slot, 0, wg_d[:, j * 128:(j + 1) * 128].rearrange("(kt p) c -> p kt c", p=128), (NDT, 128))
                    wu = load_w(slot, NDT * 128, wu_d[:, j * 128:(j + 1) * 128].rearrange("(kt p) c -> p kt c", p=128), (NDT, 128))
                    bw = ring[slot][1]
                    accs = []
                    for w_ in (wg, wu):
                        bks = [bank("P") if sg[3] else bank("A") for sg in sgl]
                        for k in range(NDT):
                            for sg, bk in zip(sgl, bks):
                                c0, n = sg[0], sg[1]
                                P.op("pe", lambda e, bk=bk, w_=w_, k=k, c0=c0, n=n: e.matmul(PS[bk][:, 0:n], lhsT=w_[:, k, :], rhs=XN[:, k, c0:c0 + n], start=(k == 0), stop=(k == NDT - 1)),
                                     reads=[bw, bXN], writes=[PSB[bk]])
                        accs.append(bks)
                    sgt, bsg = SG.next()
                    for si, sg in enumerate(sgl):
                        c0, n = sg[0], sg[1]
                        bg, bu = accs[0][si], accs[1][si]
                        P.op("act", lambda e, sgt=sgt, bg=bg, c0=c0, n=n: e.activation(out=sgt[:, c0:c0 + n], in_=PS[bg][:, 0:n], func=AF.Silu), reads=[PSB[bg]], writes=[bsg])
                        P.op("dve", lambda e, sgt=sgt, bu=bu, jj=jj, c0=c0, n=n: e.tensor_tensor(out=ACT_[:, jj, c0:c0 + n], in0=sgt[:, c0:c0 + n], in1=PS[bu][:, 0:n], op=ALU.mult),
                             reads=[bsg, PSB[bu]], writes=[bACT[jj]])
                for m in range(NDT):
                    slot = ring_next()
                    wd = load_w(slot, 0, wd_d[hf * NFH * 128:(hf + 1) * NFH * 128, m * 128:(m + 1) * 128].rearrange("(ft p) c -> p ft c", p=128), (NFH, 128))
                    bw = ring[slot][1]
                    bks = [bank("P") if sg[3] else bank("A") for sg in sgl]
                    for ft in range(NFH):
                        for sg, bk in zip(sgl, bks):
                            c0, n = sg[0], sg[1]
                            P.op("pe", lambda e, bk=bk, wd=wd, ft=ft, c0=c0, n=n: e.matmul(PS[bk][:, 0:n], lhsT=wd[:, ft, :], rhs=ACT_[:, ft, c0:c0 + n], start=(ft == 0), stop=(ft == NFH - 1)),
                                 reads=[bw, bACT[ft]], writes=[PSB[bk]])
                    Hin = Hsrc if hf == 0 else Hdst
                    hs, bhs = HS.next()
                    o, bo = OS.next()
                    for sg, bk in zip(sgl, bks):
                        c0, n, d0, _ = sg
                        P.dma("sp", lambda e, hs=hs, m=m, c0=c0, n=n, d0=d0, Hin=Hin: e.dma_start(out=hs[:, c0:c0 + n], in_=Hin[m * 128:(m + 1) * 128, d0:d0 + n]),
                              None, reads=[hb(Hin, m, t)], writes=[bhs])
                        P.op("dve", lambda e, o=o, hs=hs, bk=bk, c0=c0, n=n: e.scalar_tensor_tensor(out=o[:, c0:c0 + n], in0=PS[bk][:, 0:n], scalar=0.5, in1=hs[:, c0:c0 + n], op0=ALU.mult, op1=ALU.add),
                             reads=[PSB[bk], bhs], writes=[bo])
                    for sg in sgl:
                        c0, n, d0, _ = sg
                        P.dma("sp", lambda e, o=o, m=m, c0=c0, n=n, d0=d0: e.dma_start(out=Hdst[m * 128:(m + 1) * 128, d0:d0 + n], in_=o[:, c0:c0 + n]),
                              None, reads=[bo], writes=[hb(Hdst, m, t)])
                        if hf == 1:
                            ssq_accum(o, bo, sg, m == 0, m == NDT - 1)

        def _stage1_tile(t):
            sgl = segs(t)
            for blk in range(4):
                v, b = ring[blk]
                r0 = PRE + 512 * t + 128 * blk
                P.dma("sp", lambda e, blk=blk, r0=r0: e.dma_start(out=ringF[:, blk, 0:D], in_=x_d[r0:r0 + 128, :]), None, writes=[b])
            if t == 0:
                P.dma("sp", lambda e: e.dma_start(out=XPRE[0:PRE, :], in_=x_d[0:PRE, :]), None, writes=[bXPRE] + bACT)
            for dt in range(NDT):
                bk = bank("A")
                for blk in range(4):
                    P.op("pe", lambda e, bk=bk, blk=blk, dt=dt: e.transpose(out=PS[bk][:, blk * 128:(blk + 1) * 128], in_=ringF[:, blk, dt * 128:(dt + 1) * 128], identity=ident[:]),
                         reads=[ring[blk][1], b_const], writes=[PSB[bk]])
                o, bo = OS.next()
                P.op("act", lambda e, o=o, bk=bk: e.activation(out=o[:, PRE:PRE + 512], in_=PS[bk][:, 0:512], func=AF.Copy), reads=[PSB[bk]], writes=[bo])
                if t == 0:
                    bp = bank("P")
                    P.op("pe", lambda e, bp=bp, dt=dt: e.transpose(out=PS[bp][:, 0:PRE], in_=XPRE[0:PRE, dt * 128:(dt + 1) * 128], identity=ident[0:PRE, 0:PRE]),
                         reads=[bXPRE, b_const], writes=[PSB[bp]])
                    P.op("act", lambda e, o=o, bp=bp: e.activation(out=o[:, 0:PRE], in_=PS[bp][:, 0:PRE], func=AF.Copy), reads=[PSB[bp]], writes=[bo])
                for sg in sgl:
                    c0, n, d0, _ = sg
                    P.dma("sp", lambda e, o=o, dt=dt, c0=c0, n=n, d0=d0: e.dma_start(out=H0[dt * 128:(dt + 1) * 128, d0:d0 + n], in_=o[:, c0:c0 + n]),
                          None, reads=[bo], writes=[hb(H0, dt, t)])
                    ssq_accum(o, bo, sg, dt == 0, dt == NDT - 1)
            if cfg.get("DEBUG") and t == 0:
                (dtmp, bdtmp), = carve([W], F32)
                P.op("act", lambda e: e.activation(out=dtmp[:, 0:16], in_=PS[SSQP][:, 0:16], func=AF.Copy), reads=[PSB[SSQP]], writes=[bdtmp])
                P.op("act", lambda e: e.activation(out=dtmp[:, 16:528], in_=PS[SSQM][:, 0:512], func=AF.Copy), reads=[PSB[SSQM]], writes=[bdtmp])
                P.dma("sp", lambda e: e.dma_start(out=DBG[:, 1, :], in_=dtmp), None, reads=[bdtmp], writes=[Buf("dbg", dram=True)])
            make_rstd(RSTD, bRSTD, sgl, D)
            if cfg.get("DEBUG") and t == 0:
                P.dma("sp", lambda e: e.dma_start(out=DBG[:, 0, :], in_=RSTD), None, reads=[bRSTD], writes=[Buf("dbg", dram=True)])
            make_xn(H0, G1, t, sgl)
            ffn(t, sgl, H0, H1, w1g_d, w1u_d, w1d_d)
            make_rstd(RSTD, bRSTD, sgl, D)
            make_xn(H1, GM, t, sgl)
            for c in range(3 * NCT):
                slot = ring_next()
                wi = load_w(slot, 0, win_d[:, c * 128:(c + 1) * 128].rearrange("(kt p) c -> p kt c", p=128), (NDT, 128))
                bw = ring[slot][1]
                bks = [bank("P") if sg[3] else bank("A") for sg in sgl]
                for k in range(NDT):
                    for sg, bk in zip(sgl, bks):
                        c0, n = sg[0], sg[1]
                        P.op("pe", lambda e, bk=bk, wi=wi, k=k, c0=c0, n=n: e.matmul(PS[bk][:, 0:n], lhsT=wi[:, k, :], rhs=XN[:, k, c0:c0 + n], start=(k == 0), stop=(k == NDT - 1)),
                             reads=[bw, bXN], writes=[PSB[bk]])
                o, bo = OS.next()
                for sg, bk in zip(sgl, bks):
                    c0, n, d0, _ = sg
                    P.op("act", lambda e, o=o, bk=bk, c0=c0, n=n: e.activation(out=o[:, c0:c0 + n], in_=PS[bk][:, 0:n], func=AF.Copy), reads=[PSB[bk]], writes=[bo])
                    P.dma("sp", lambda e, o=o, c=c, c0=c0, n=n, d0=d0: e.dma_start(out=PROJ[c * 128:(c + 1) * 128, d0:d0 + n], in_=o[:, c0:c0 + n]),
                          None, reads=[bo], writes=[hb(PROJ, c, t)])
        for t in range(NTL if cfg.get("STOP", 9) >= 1 else 0):
            _stage1_tile(t)
        P.barrier()

        a_off[0] = 0
        U2 = Rot(carve([2, 3 + W], F32, 2)); XC = Rot(carve([2, W], F32, 2)); XCB = Rot(carve([2, W], BF16, 2))
        RT = Rot(carve([W], F32, 2)); AT = Rot(carve([W], F32, 2)); MT = Rot(carve([W], F32, 2)); IT = Rot(carve([W], F32, 2))
        GT = Rot(carve([W], F32, 2)); HT = Rot(carve([W], F32, 2)); YO = Rot(carve([W], F32, 2)); YB = Rot(carve([512], BF16, 3))
        SQ2 = Rot(carve([512], BF16, 2))
        TABS = Rot(carve([2, 512], F32, 2)); UB = Rot(carve([W], BF16, 2)); LH = Rot(carve([4, 128], BF16, 2))
        TA = Rot(carve([W], F32, 2)); TBb = Rot(carve([W], F32, 2)); ZRI = Rot(carve([W], F32, 2)); ZII = Rot(carve([W], F32, 2))
        ZR = Rot(carve([W], F32, 2)); ZI = Rot(carve([W], F32, 2)); PA = Rot(carve([W], F32, 2)); PB = Rot(carve([W], F32, 2))
        XR = Rot(carve([W], BF16, 2)); XI = Rot(carve([W], BF16, 2))
        UF = Rot(carve([W], F32, 2)); YT = Rot(carve([W], F32, 2))
        (ZZ, _), = carve([NCT, 512], BF16)
        bZZ = [Buf("zz%d" % c) for c in range(NCT)]
        GLW = Rot(carve([NCT, 128], BF16, 2))
        GTT = Rot(carve([512], F32, 2)); YS = Rot(carve([512], F32, 2))
        (RO, bRO), = carve([512], F32)
        tiny = sb("tiny", [128, 8])
        btiny = Buf("tiny")
        bH = Buf("hst"); bZ = [Buf("zst%d" % p) for p in range(NPR)]

        def _stage2_tile(t):
            sgl = segs(t)
            d_lo = 0 if t == 0 else PRE + 512 * t
            c_lo = 0 if t == 0 else PRE
            ncol = (W - c_lo)
            main_d0 = PRE + 512 * t
            for hd in range(NHD):
                u2, bu2 = U2.next()
                xc, bxc = XC.next()
                xcb, bxcb = XCB.next()
                for it in range(2):
                    r0 = hd * 256 + it * 128
                    if t == 0:
                        P.op("pool", lambda e, u2=u2, it=it: e.memset(u2[:, it, 0:3], 0.0), writes=[bu2])
                        P.dma("sp", lambda e, u2=u2, it=it, r0=r0: e.dma_start(out=u2[:, it, 3:3 + W], in_=PROJ[r0:r0 + 128, 0:W]), None,
                              reads=[hb(PROJ, r0 // 128, 0)], writes=[bu2])
                    else:
                        P.dma("sp", lambda e, u2=u2, it=it, r0=r0: e.dma_start(out=u2[:, it, PRE:3 + W], in_=PROJ[r0:r0 + 128, main_d0 - 3:main_d0 + 512]), None,
                              reads=[hb(PROJ, r0 // 128, t), hb(PROJ, r0 // 128, t - 1)], writes=[bu2])
                    ct = hd * 2 + it
                    P.op("dve", lambda e, u2=u2, xc=xc, it=it, ct=ct: e.tensor_scalar(out=xc[:, it, c_lo:W], in0=u2[:, it, 3 + c_lo:3 + W], scalar1=CW[:, 3, ct:ct + 1], scalar2=CB[:, ct:ct + 1], op0=ALU.mult, op1=ALU.add),
                         reads=[bu2, b_const], writes=[bxc])
                    for k in range(3):
                        P.op("dve", lambda e, u2=u2, xc=xc, it=it, ct=ct, k=k: e.scalar_tensor_tensor(out=xc[:, it, c_lo:W], in0=u2[:, it, k + c_lo:k + W], scalar=CW[:, k, ct:ct + 1], in1=xc[:, it, c_lo:W], op0=ALU.mult, op1=ALU.add),
                             reads=[bu2, bxc, b_const], writes=[bxc])
                    P.op("act", lambda e, xc=xc, xcb=xcb, it=it: e.activation(out=xcb[:, it, c_lo:W], in_=xc[:, it, c_lo:W], func=AF.Copy), reads=[bxc], writes=[bxcb])
                for jt in range(2):
                    ct = hd * 2 + jt
                    rt, brt = RT.next(); at, bat = AT.next(); mt, bmt = MT.next(); itt, bit = IT.next()
                    gt, bgt = GT.next(); ht, bht = HT.next(); yo, byo = YO.next()
                    P.dma("sp", lambda e, gt=gt, ct=ct: e.dma_start(out=gt[:, c_lo:W], in_=PROJ[DRG + ct * 128:DRG + (ct + 1) * 128, d_lo:d_lo + ncol]), None,
                          reads=[hb(PROJ, NCT + ct, t)], writes=[bgt])
                    for sg in sgl:
                        c0, n, _, is_pre = sg
                        ba_, bx_ = (bank("P"), bank("P")) if is_pre else (bank("A"), bank("A"))
                        for Wm, bk in ((WA, ba_), (WX, bx_)):
                            for it in range(2):
                                P.op("pe", lambda e, Wm=Wm, bk=bk, it=it, jt=jt, hd=hd, xcb=xcb, c0=c0, n=n: e.matmul(PS[bk][:, 0:n], lhsT=Wm[:, hd, it, jt * 128:(jt + 1) * 128], rhs=xcb[:, it, c0:c0 + n], start=(it == 0), stop=(it == 1)),
                                     reads=[bxcb, b_const], writes=[PSB[bk]])
                        P.op("act", lambda e, rt=rt, ba_=ba_, ct=ct, c0=c0, n=n: e.activation(out=rt[:, c0:c0 + n], in_=PS[ba_][:, 0:n], func=AF.Sigmoid, bias=BA[:, ct:ct + 1], scale=1.0), reads=[PSB[ba_], b_const], writes=[brt])
                        P.op("act", lambda e, itt=itt, bx_=bx_, ct=ct, c0=c0, n=n: e.activation(out=itt[:, c0:c0 + n], in_=PS[bx_][:, 0:n], func=AF.Sigmoid, bias=BX[:, ct:ct + 1], scale=1.0), reads=[PSB[bx_], b_const], writes=[bit])
                    P.op("act", lambda e, at=at, rt=rt, ct=ct: e.activation(out=at[:, c_lo:W], in_=rt[:, c_lo:W], func=AF.Exp, scale=COEF[:, ct:ct + 1]), reads=[brt, b_const], writes=[bat])
                    P.op("act", lambda e, mt=mt, rt=rt, ct=ct: e.activation(out=mt[:, c_lo:W], in_=rt[:, c_lo:W], func=AF.Exp, scale=COEF2[:, ct:ct + 1]), reads=[brt, b_const], writes=[bmt])
                    P.op("dve", lambda e, mt=mt: e.tensor_scalar(out=mt[:, c_lo:W], in0=mt[:, c_lo:W], scalar1=-1.0, scalar2=1.0, op0=ALU.mult, op1=ALU.add), reads=[bmt], writes=[bmt])
                    P.op("act", lambda e, mt=mt: e.activation(out=mt[:, c_lo:W], in_=mt[:, c_lo:W], func=AF.Sqrt), reads=[bmt], writes=[bmt])
                    P.op("dve", lambda e, mt=mt, itt=itt: e.tensor_tensor(out=mt[:, c_lo:W], in0=mt[:, c_lo:W], in1=itt[:, c_lo:W], op=ALU.mult), reads=[bmt, bit], writes=[bmt])
                    P.op("dve", lambda e, mt=mt, xc=xc, jt=jt: e.tensor_tensor(out=mt[:, c_lo:W], in0=mt[:, c_lo:W], in1=xc[:, jt, c_lo:W], op=ALU.mult), reads=[bmt, bxc], writes=[bmt])
                    for sg in sgl:
                        c0, n, _, is_pre = sg
                        if is_pre:
                            P.op("dve", lambda e, ht=ht, at=at, mt=mt, c0=c0, n=n: e.tensor_tensor_scan(out=ht[:, c0:c0 + n], data0=at[:, c0:c0 + n], data1=mt[:, c0:c0 + n], initial=0.0, op0=ALU.mult, op1=ALU.add),
                                 reads=[bat, bmt], writes=[bht])
                        else:
                            init = ht[:, PRE - 1:PRE] if t == 0 else HST[:, ct:ct + 1]
                            P.op("dve", lambda e, ht=ht, at=at, mt=mt, c0=c0, n=n, init=init: e.tensor_tensor_scan(out=ht[:, c0:c0 + n], data0=at[:, c0:c0 + n], data1=mt[:, c0:c0 + n], initial=init, op0=ALU.mult, op1=ALU.add),
                                 reads=[bat, bmt, bH, bht], writes=[bht])
                    P.op("act", lambda e, ht=ht, ct=ct: e.activation(out=HST[:, ct:ct + 1], in_=ht[:, W - 1:W], func=AF.Copy), reads=[bht], writes=[bH])
                    P.op("act", lambda e, gt=gt: e.activation(out=gt[:, PRE:W], in_=gt[:, PRE:W], func=AF.Gelu_apprx_tanh), reads=[bgt], writes=[bgt])
                    P.op("dve", lambda e, yo=yo, ht=ht, gt=gt: e.tensor_tensor(out=yo[:, PRE:W], in0=ht[:, PRE:W], in1=gt[:, PRE:W], op=ALU.mult), reads=[bht, bgt], writes=[byo])
                    sq, bsq = SQ2.next()
                    P.op("act", lambda e, sq=sq, yo=yo: e.activation(out=sq, in_=yo[:, PRE:W], func=AF.Square), reads=[byo], writes=[bsq])
                    P.op("pe", lambda e, sq=sq, ct=ct: e.matmul(PS[SSQM][:, 0:512], lhsT=ones_b[:], rhs=sq, start=(ct == 0), stop=(ct == NCT - 1)), reads=[bsq, b_const], writes=[PSB[SSQM]])
                    yb, byb = YB.next()
                    P.op("act", lambda e, yb=yb, yo=yo, ct=ct: e.activation(out=yb, in_=yo[:, PRE:W], func=AF.Copy, scale=GRO[:, ct:ct + 1]), reads=[byo, b_const], writes=[byb])
                    P.dma("sp", lambda e, yb=yb, ct=ct: e.dma_start(out=YH[ct * 128:(ct + 1) * 128, 512 * t:512 * (t + 1)], in_=yb), None, reads=[byb], writes=[hb(YH, ct, t)])
            make_rstd(RO, bRO, [sgl[-1]], DRG, c_of=lambda sg: 0)
            P.dma("sp", lambda e: e.dma_start(out=RSD[0, :, 512 * t:512 * (t + 1)], in_=RO), None, reads=[bRO], writes=[hb(RSD, 0, t)])

            for ct in range(NCT):
                ybk = BT
                for j4 in range(4):
                    pr = ct * 4 + j4
                    tab, btab = TABS.next()
                    ub, bub = UB.next()
                    P.dma("sp", lambda e, tab=tab, pr=pr: e.dma_start(out=tab, in_=TABD[pr]), None, reads=[dT], writes=[btab])
                    lh, blh = LH.next()
                    P.dma("sp", lambda e, lh=lh, pr=pr: e.dma_start(out=lh, in_=LHD[pr]), None, reads=[dL], writes=[blh])
                    ur0 = 2 * DRG + 32 * pr
                    P.dma("pool", lambda e, ub=ub, ur0=ur0: e.dma_start(out=ub[0:32, c_lo:W], in_=PROJ[ur0:ur0 + 32, d_lo:d_lo + ncol]), None,
                          reads=[hb(PROJ, ur0 // 128, t)], writes=[bub])
                    ta, bta = TA.next(); tb, btb = TBb.next(); zri, bzri = ZRI.next(); zii, bzii = ZII.next()
                    zr, bzr = ZR.next(); zi, bzi = ZI.next(); pa, bpa = PA.next(); pb, bpb = PB.next()
                    xr, bxr = XR.next(); xi, bxi = XI.next()
                    for sg in sgl:
                        c0, n, _, is_pre = sg
                        bkr, bki = (bank("P"), bank("P")) if is_pre else (bank("A"), bank("A"))
                        P.op("pe", lambda e, bkr=bkr, lh=lh, ub=ub, c0=c0, n=n: e.matmul(PS[bkr][:, 0:n], lhsT=lh[0:32, 2, :], rhs=ub[0:32, c0:c0 + n], start=True, stop=True), reads=[bub, blh], writes=[PSB[bkr]])
                        P.op("pe", lambda e, bki=bki, lh=lh, ub=ub, c0=c0, n=n: e.matmul(PS[bki][:, 0:n], lhsT=lh[0:32, 3, :], rhs=ub[0:32, c0:c0 + n], start=True, stop=True), reads=[bub, blh], writes=[PSB[bki]])
                        Ct = tab[:, 0, 0:n]
                        St = tab[:, 1, 0:n]
                        cs = slice(c0, c0 + n)
                        P.op("dve", lambda e, ta=ta, bkr=bkr, Ct=Ct, cs=cs, n=n: e.tensor_tensor(out=ta[:, cs], in0=PS[bkr][:, 0:n], in1=Ct, op=ALU.mult), reads=[PSB[bkr], btab], writes=[bta])
                        P.op("dve", lambda e, tb=tb, bki=bki, St=St, cs=cs, n=n: e.tensor_tensor(out=tb[:, cs], in0=PS[bki][:, 0:n], in1=St, op=ALU.mult), reads=[PSB[bki], btab], writes=[btb])
                        P.op("dve", lambda e, zri=zri, ta=ta, tb=tb, cs=cs: e.tensor_tensor(out=zri[:, cs], in0=ta[:, cs], in1=tb[:, cs], op=ALU.add), reads=[bta, btb], writes=[bzri])
                        ta2, bta2 = TA.next(); tb2, btb2 = TBb.next()
                        P.op("dve", lambda e, ta2=ta2, bki=bki, Ct=Ct, cs=cs, n=n: e.tensor_tensor(out=ta2[:, cs], in0=PS[bki][:, 0:n], in1=Ct, op=ALU.mult), reads=[PSB[bki], btab], writes=[bta2])
                        P.op("dve", lambda e, tb2=tb2, bkr=bkr, St=St, cs=cs, n=n: e.tensor_tensor(out=tb2[:, cs], in0=PS[bkr][:, 0:n], in1=St, op=ALU.mult), reads=[PSB[bkr], btab], writes=[btb2])
                        P.op("dve", lambda e, zii=zii, ta2=ta2, tb2=tb2, cs=cs: e.tensor_tensor(out=zii[:, cs], in0=ta2[:, cs], in1=tb2[:, cs], op=ALU.subtract), reads=[bta2, btb2], writes=[bzii])
                        if is_pre:
                            ir, ii_ = 0.0, 0.0
                            rd = [bzri]
                        else:
                            ir, ii_ = ZST[:, pr, 0:1], ZST[:, pr, 1:2]
                            rd = [bzri, bZ[pr]]
                        rho_b = RHO[:, pr:pr + 1].broadcast_to([128, n])
                        P.op("dve", lambda e, zr=zr, zri=zri, cs=cs, ir=ir, rho_b=rho_b: e.tensor_tensor_scan(out=zr[:, cs], data0=rho_b, data1=zri[:, cs], initial=ir, op0=ALU.mult, op1=ALU.add), reads=rd + [b_const], writes=[bzr])
                        rd2 = [bzii] + rd[1:]
                        P.op("dve", lambda e, zi=zi, zii=zii, cs=cs, ii_=ii_, rho_b=rho_b: e.tensor_tensor_scan(out=zi[:, cs], data0=rho_b, data1=zii[:, cs], initial=ii_, op0=ALU.mult, op1=ALU.add), reads=rd2 + [b_const], writes=[bzi])
                        cl, sl, nsl = (CL16, SL16, NSL16) if is_pre else (CL512, SL512, NSL512)
                        e0 = c0 + n - 1
                        P.op("dve", lambda e, zr=zr, e0=e0, cl=cl, pr=pr: e.tensor_tensor(out=tiny[:, 0:1], in0=zr[:, e0:e0 + 1], in1=cl[:, pr:pr + 1], op=ALU.mult), reads=[bzr, b_const], writes=[btiny])
                        P.op("dve", lambda e, zr=zr, e0=e0, sl=sl, pr=pr: e.tensor_tensor(out=tiny[:, 1:2], in0=zr[:, e0:e0 + 1], in1=sl[:, pr:pr + 1], op=ALU.mult), reads=[bzr, b_const], writes=[btiny])
                        P.op("dve", lambda e, zi=zi, e0=e0, nsl=nsl, pr=pr: e.scalar_tensor_tensor(out=ZST[:, pr, 0:1], in0=zi[:, e0:e0 + 1], scalar=nsl[:, pr:pr + 1], in1=tiny[:, 0:1], op0=ALU.mult, op1=ALU.add), reads=[bzi, btiny, b_const], writes=[bZ[pr]])
                        P.op("dve", lambda e, zi=zi, e0=e0, cl=cl, pr=pr: e.scalar_tensor_tensor(out=ZST[:, pr, 1:2], in0=zi[:, e0:e0 + 1], scalar=cl[:, pr:pr + 1], in1=tiny[:, 1:2], op0=ALU.mult, op1=ALU.add), reads=[bzi, btiny, b_const], writes=[bZ[pr]])
                        if is_pre:
                            continue
                        eo = "pool" if pr % 3 == 2 else "dve"
                        P.op(eo, lambda e, pa=pa, zr=zr, Ct=Ct, cs=cs: e.tensor_tensor(out=pa[:, cs], in0=zr[:, cs], in1=Ct, op=ALU.mult), reads=[bzr, btab], writes=[bpa])
                        P.op(eo, lambda e, pb=pb, zi=zi, St=St, cs=cs: e.tensor_tensor(out=pb[:, cs], in0=zi[:, cs], in1=St, op=ALU.mult), reads=[bzi, btab], writes=[bpb])
                        P.op(eo, lambda e, xr=xr, pa=pa, pb=pb, cs=cs: e.tensor_tensor(out=xr[:, cs], in0=pa[:, cs], in1=pb[:, cs], op=ALU.subtract), reads=[bpa, bpb], writes=[bxr])
                        pa2, bpa2 = PA.next(); pb2, bpb2 = PB.next()
                        P.op(eo, lambda e, pa2=pa2, zr=zr, St=St, cs=cs: e.tensor_tensor(out=pa2[:, cs], in0=zr[:, cs], in1=St, op=ALU.mult), reads=[bzr, btab], writes=[bpa2])
                        P.op(eo, lambda e, pb2=pb2, zi=zi, Ct=Ct, cs=cs: e.tensor_tensor(out=pb2[:, cs], in0=zi[:, cs], in1=Ct, op=ALU.mult), reads=[bzi, btab], writes=[bpb2])
                        P.op(eo, lambda e, xi=xi, pa2=pa2, pb2=pb2, cs=cs: e.tensor_tensor(out=xi[:, cs], in0=pa2[:, cs], in1=pb2[:, cs], op=ALU.add), reads=[bpa2, bpb2], writes=[bxi])
                        P.op("pe", lambda e, ybk=ybk, lh=lh, xr=xr, cs=cs, j4=j4: e.matmul(PS[ybk][:, 0:512], lhsT=lh[:, 0, :], rhs=xr[:, cs], start=(j4 == 0), stop=False), reads=[bxr, blh], writes=[PSB[ybk]])
                        P.op("pe", lambda e, ybk=ybk, lh=lh, xi=xi, cs=cs, j4=j4: e.matmul(PS[ybk][:, 0:512], lhsT=lh[:, 1, :], rhs=xi[:, cs], start=False, stop=(j4 == 3)), reads=[bxi, blh], writes=[PSB[ybk]])
                uf, buf_ = UF.next()
                yt, byt = YT.next()
                ur = 2 * DRG + ct * 128
                P.dma("sp", lambda e, uf=uf, ur=ur: e.dma_start(out=uf[:, 0:512], in_=PROJ[ur:ur + 128, main_d0:main_d0 + 512]), None, reads=[hb(PROJ, ur // 128, t)], writes=[buf_])
                P.op("dve", lambda e, yt=yt, uf=uf, ybk=ybk, ct=ct: e.scalar_tensor_tensor(out=yt[:, 0:512], in0=uf[:, 0:512], scalar=SD[:, ct:ct + 1], in1=PS[ybk][:, 0:512], op0=ALU.mult, op1=ALU.add),
                     reads=[buf_, PSB[ybk], b_const], writes=[byt])
                P.op("act", lambda e, yt=yt, ct=ct: e.activation(out=ZZ[:, ct, :], in_=yt[:, 0:512], func=AF.Gelu_apprx_tanh), reads=[byt], writes=[bZZ[ct]])
            for cp in range(NCT):
                glw, bglw = GLW.next()
                P.dma("pool", lambda e, glw=glw, cp=cp: e.dma_start(out=glw, in_=gw_d[:, cp * 128:(cp + 1) * 128].rearrange("(kt p) c -> p kt c", p=128)), None, writes=[bglw])
                bk = bank("A")
                for ct in range(NCT):
                    P.op("pe", lambda e, bk=bk, glw=glw, ct=ct: e.matmul(PS[bk][:, 0:512], lhsT=glw[:, ct, :], rhs=ZZ[:, ct, :], start=(ct == 0), stop=(ct == NCT - 1)), reads=[bglw, bZZ[ct]], writes=[PSB[bk]])
                gtt, bgtt = GTT.next(); ys, bys = YS.next()
                P.op("act", lambda e, gtt=gtt, bk=bk, cp=cp: e.activation(out=gtt, in_=PS[bk][:, 0:512], func=AF.Sigmoid, bias=GB[:, cp:cp + 1], scale=1.0), reads=[PSB[bk], b_const], writes=[bgtt])
                P.op("dve", lambda e, ys=ys, gtt=gtt, cp=cp: e.tensor_tensor(out=ys, in0=ZZ[:, cp, :], in1=gtt, op=ALU.mult), reads=[bgtt, bZZ[cp]], writes=[bys])
                sq, bsq = SQ2.next()
                P.op("act", lambda e, sq=sq, ys=ys: e.activation(out=sq, in_=ys, func=AF.Square), reads=[bys], writes=[bsq])
                P.op("pe", lambda e, sq=sq, cp=cp: e.matmul(PS[SSQP][:, 0:512], lhsT=ones_b[:], rhs=sq, start=(cp == 0), stop=(cp == NCT - 1)), reads=[bsq, b_const], writes=[PSB[SSQP]])
                yb, byb = YB.next()
                P.op("act", lambda e, yb=yb, ys=ys, cp=cp: e.activation(out=yb, in_=ys, func=AF.Copy, scale=GSO[:, cp:cp + 1]), reads=[bys, b_const], writes=[byb])
                P.dma("sp", lambda e, yb=yb, cp=cp: e.dma_start(out=YH[DRG + cp * 128:DRG + (cp + 1) * 128, 512 * t:512 * (t + 1)], in_=yb), None, reads=[byb], writes=[hb(YH, NCT + cp, t)])
            P.op("act", lambda e: e.activation(out=RO, in_=PS[SSQP][:, 0:512], func=AF.Sqrt, bias=epst[:, 0:1], scale=1.0 / DS5), reads=[PSB[SSQP], b_const], writes=[bRO])
            P.op("dve", lambda e: e.reciprocal(out=RO, in_=RO), reads=[bRO], writes=[bRO])
            P.dma("sp", lambda e: e.dma_start(out=RSD[1, :, 512 * t:512 * (t + 1)], in_=RO), None, reads=[bRO], writes=[hb(RSD, 1, t)])
        rot["A"] = [0, 1, 2]
        for t in range(NTL if cfg.get("STOP", 9) >= 2 else 0):
            _stage2_tile(t)
        rot["A"] = [0, 1, 2, 7]
        P.barrier()

        def _stage3_tile(t):
            sgl = segs(t, with_pre=False)
            c0, n, d0, _ = sgl[0]
            P.dma("sp", lambda e: e.dma_start(out=XN[:, :, c0:c0 + 512], in_=YH[:, 512 * t:512 * (t + 1)].rearrange("(k p) c -> p k c", p=128)), None,
                  reads=[hb(YH, k, t) for k in range(NDT)], writes=[bXN])
            P.dma("sp", lambda e: e.dma_start(out=RR, in_=RSD[0, :, 512 * t:512 * (t + 1)]), None, reads=[hb(RSD, 0, t)], writes=[bRR])
            P.dma("sp", lambda e: e.dma_start(out=RS_, in_=RSD[1, :, 512 * t:512 * (t + 1)]), None, reads=[hb(RSD, 1, t)], writes=[bRS])
            for m in range(NDT):
                slot = ring_next()
                wo = load_w(slot, 0, wout_d[:, m * 128:(m + 1) * 128].rearrange("(kt p) c -> p kt c", p=128), (NDT, 128))
                bw = ring[slot][1]
                b1, b2 = bank("A"), bank("A")
                for k in range(NDT):
                    bk = b1 if k < NCT else b2
                    kk = k if k < NCT else k - NCT
                    P.op("pe", lambda e, bk=bk, wo=wo, k=k, kk=kk: e.matmul(PS[bk][:, 0:512], lhsT=wo[:, k, :], rhs=XN[:, k, c0:c0 + 512], start=(kk == 0), stop=(kk == NCT - 1)), reads=[bw, bXN], writes=[PSB[bk]])
                t12, bt12 = T12.next()
                hs, bhs = HS.next()
                o, bo = OS.next()
                P.dma("sp", lambda e, hs=hs, m=m: e.dma_start(out=hs[:, c0:c0 + 512], in_=H1[m * 128:(m + 1) * 128, d0:d0 + 512]), None, reads=[hb(H1, m, t)], writes=[bhs])
                P.op("dve", lambda e, t12=t12, b1=b1: e.tensor_tensor(out=t12[:, 0, :], in0=PS[b1][:, 0:512], in1=RR, op=ALU.mult), reads=[PSB[b1], bRR], writes=[bt12])
                P.op("dve", lambda e, t12=t12, b2=b2: e.tensor_tensor(out=t12[:, 1, :], in0=PS[b2][:, 0:512], in1=RS_, op=ALU.mult), reads=[PSB[b2], bRS], writes=[bt12])
                P.op("dve", lambda e, t12=t12: e.tensor_tensor(out=t12[:, 0, :], in0=t12[:, 0, :], in1=t12[:, 1, :], op=ALU.add), reads=[bt12], writes=[bt12])
                P.op("dve", lambda e, t12=t12, o=o, hs=hs: e.tensor_tensor(out=o[:, c0:c0 + 512], in0=t12[:, 0, :], in1=hs[:, c0:c0 + 512], op=ALU.add), reads=[bt12, bhs], writes=[bo])
                P.dma("sp", lambda e, o=o, m=m: e.dma_start(out=H2[m * 128:(m + 1) * 128, d0:d0 + 512], in_=o[:, c0:c0 + 512]), None, reads=[bo], writes=[hb(H2, m, t)])
                ssq_accum(o, bo, sgl[0], m == 0, m == NDT - 1)
            if cfg.get("STOP3", 9) < 1:
                return
            make_rstd(RSTD, bRSTD, sgl, D)
            make_xn(H2, G2, t, sgl)
            ffn(t, sgl, H2, H3, w2g_d, w2u_d, w2d_d)
            make_rstd(RSTD, bRSTD, sgl, D)
            if cfg.get("STOP3", 9) < 2:
                return
            for dt in range(NDT):
                hs, bhs = HS.next()
                o, bo = OS.next()
                P.dma("sp", lambda e, hs=hs, dt=dt: e.dma_start(out=hs[:, c0:c0 + 512], in_=H3[dt * 128:(dt + 1) * 128, d0:d0 + 512]), None, reads=[hb(H3, dt, t)], writes=[bhs])
                P.op("dve", lambda e, o=o, hs=hs, dt=dt: e.scalar_tensor_tensor(out=o[:, c0:c0 + 512], in0=hs[:, c0:c0 + 512], scalar=GF[:, dt:dt + 1], in1=RSTD[:, c0:c0 + 512], op0=ALU.mult, op1=ALU.mult),
                     reads=[bhs, bRSTD, b_const], writes=[bo])
                bk = bank("A")
                for blk in range(4):
                    P.op("pe", lambda e, bk=bk, blk=blk, o=o: e.transpose(out=PS[bk][:, blk * 128:(blk + 1) * 128], in_=o[:, c0 + blk * 128:c0 + (blk + 1) * 128], identity=ident[:]),
                         reads=[bo, b_const], writes=[PSB[bk]])
                P.op("act", lambda e, bk=bk, dt=dt: e.activation(out=ringF[:, :, dt * 128:(dt + 1) * 128], in_=PS[bk][:, 0:512].rearrange("p (a b) -> p a b", a=4), func=AF.Copy),
                     reads=[PSB[bk]], writes=[ring[i][1] for i in range(4)])
            for blk in range(4):
                r0 = 512 * t + 128 * blk
                P.dma("sp", lambda e, blk=blk, r0=r0: e.dma_start(out=out_d[r0:r0 + 128, :], in_=ringF[:, blk, 0:D]), None, reads=[ring[blk][1]])

        for t in range(NTL if cfg.get("STOP", 9) >= 3 else 0):
            _stage3_tile(t)
        with nc.Block() as block:
            P.emit(block)
    return nc


WEIGHT_KEYS = ["ffn1_norm", "ffn1_w_gate", "ffn1_w_up", "ffn1_w_down", "mix_norm", "w_in", "rg_conv_w", "rg_conv_b",
               "rg_w_a", "rg_b_a", "rg_w_x", "rg_b_x", "rg_lambda", "s5_lambda_re", "s5_lambda_im", "s5_log_dt",
               "s5_b_re", "s5_b_im", "s5_c_re", "s5_c_im", "s5_d", "s5_glu_w", "s5_glu_b", "rg_out_norm", "s5_out_norm",
               "w_out", "ffn2_norm", "ffn2_w_gate", "ffn2_w_up", "ffn2_w_down"]


def make_in_maps(cfg, inputs):
    x = np.asarray(inputs["x"], dtype=np.float32)
    meta = np.asarray(inputs["meta_tokens"], dtype=np.float32)
    shared = {}
    for k in WEIGHT_KEYS:
        a = np.asarray(inputs[k], dtype=np.float32)
        a = a.reshape(a.shape[1:])
        if k in ("rg_b_a", "rg_b_x"):
            a = a.reshape(-1)
        shared[k] = np.ascontiguousarray(a)
    shared["final_norm"] = np.ascontiguousarray(np.asarray(inputs["final_norm"], dtype=np.float32))
    maps = []
    for c in range(cfg["NCORES"]):
        m = dict(shared)
        m["x"] = np.ascontiguousarray(np.concatenate([meta, x[c]], axis=0))
        maps.append(m)
    return maps


def kernel(**inputs):
    cfg = full_cfg()
    nc = build_program(cfg)
    maps = make_in_maps(cfg, inputs)
    res = run_bass_kernel_spmd(nc, maps, core_ids=list(range(cfg["NCORES"])))
    return np.stack([np.asarray(r["out"], dtype=np.float32) for r in res.results], axis=0)
```

```python
import math
from contextlib import ExitStack
import numpy as np
import concourse.bass as bass
import concourse.mybir as mybir
from concourse.bass_utils import run_bass_kernel_spmd

F32 = mybir.dt.float32
BF16 = mybir.dt.bfloat16
AF = mybir.ActivationFunctionType
ALU = mybir.AluOpType

EPS = 1e-6
TWO_PI = float(2.0 * math.pi)
PI_SAFE = 3.1415925
MAGIC = 12582912.0


def full_cfg():
    return dict(D=4096, DFF=11008, NTOK=4096, PRE=16, NCORES=4)


class Buf:
    __slots__ = ("name", "last_w", "readers", "dram", "sems")

    def __init__(self, name, dram=False):
        self.name = name
        self.last_w = None
        self.readers = {}
        self.dram = dram
        self.sems = {}


class DSem:
    __slots__ = ("h", "count")

    def __init__(self, h):
        self.h = h
        self.count = 0


class Op:
    __slots__ = ("eng", "fn", "deps", "kind", "sem", "val", "signal")

    def __init__(self, eng, fn, kind):
        self.eng = eng
        self.fn = fn
        self.kind = kind
        self.deps = []
        self.sem = None
        self.val = 0
        self.signal = False


ENGS = ("pe", "act", "dve", "pool", "sp")


class Prog:
    def __init__(self, nc, stack):
        self.nc = nc
        self.stack = stack
        self.ops = []
        self.nsem = 0
        self.prog_sem = {e: self._sem("prog_" + e) for e in ("pe", "act", "dve", "pool")}
        self.last_op = {e: None for e in ENGS}
        self.dsems = []
        self.pending = {e: [] for e in ENGS}

    def _sem(self, name):
        self.nsem += 1
        return self.stack.enter_context(self.nc.semaphore(name))

    def dsem(self, name):
        s = DSem(self._sem("d_" + name))
        self.dsems.append(s)
        return s

    def _track(self, op, reads, writes):
        deps = op.deps
        for b in reads:
            if b.last_w is not None:
                deps.append(b.last_w)
        for b in writes:
            if b.last_w is not None:
                deps.append(b.last_w)
            deps.extend(b.readers.values())
        key = op.eng if op.kind == "c" else ("d", id(op.sem))
        for b in reads:
            b.readers[key] = op
        for b in writes:
            b.last_w = op
            b.readers = {}
        if op.kind == "d":
            op.deps = deps = [d for d in deps if not (d.kind == "d" and d.sem is op.sem)]
        if self.pending[op.eng]:
            deps.extend(self.pending[op.eng])
            self.pending[op.eng] = []
        self.ops.append(op)
        self.last_op[op.eng] = op

    def begin_capture(self):
        self._cap = []

    def end_capture(self):
        c, self._cap = self._cap, None
        return c

    def replay(self, lists):
        n = max(len(l) for l in lists)
        for i in range(n):
            for l in lists:
                if i < len(l):
                    kind, args, kw = l[i]
                    (self.op if kind == "op" else self.dma)(*args, **kw)

    def op(self, eng, fn, reads=(), writes=()):
        if getattr(self, "_cap", None) is not None:
            self._cap.append(("op", (eng, fn), dict(reads=list(reads), writes=list(writes))))
            return None
        o = Op(eng, fn, "c")
        self._track(o, reads, writes)
        return o

    def dma(self, queue, fn, sem, reads=(), writes=()):
        if getattr(self, "_cap", None) is not None:
            self._cap.append(("dma", (queue, fn, sem), dict(reads=list(reads), writes=list(writes))))
            return None
        o = Op(queue, fn, "d")
        if sem is None:
            tgt = [b for b in writes if not b.dram]
            kind = "l"
            if not tgt:
                tgt = [b for b in reads if not b.dram]
                kind = "s"
            b = tgt[0]
            kind += "w" if queue == "pool" else "h"
            if kind not in b.sems:
                b.sems[kind] = self.dsem("%s_%s%d" % (kind, b.name.replace("@", "_"), self.nsem))
            sem = b.sems[kind]
        o.sem = sem
        sem.count += 16
        o.val = sem.count
        self._track(o, reads, writes)
        return o

    def barrier(self):
        lasts = []
        for e in ("pe", "act", "dve", "pool"):
            for o in reversed(self.ops):
                if o.kind == "c" and o.eng == e:
                    lasts.append(o)
                    break
        dm = []
        seen = set()
        for o in reversed(self.ops):
            if o.kind == "d" and id(o.sem) not in seen:
                seen.add(id(o.sem))
                dm.append(o)
        for e in ENGS:
            self.pending[e] = lasts + dm

    def emit(self, block):
        for o in self.ops:
            for d in o.deps:
                if d.kind == "c" and not (o.kind == "c" and d.eng == o.eng and o.eng == "pe"):
                    d.signal = True
        cnt = {e: 0 for e in ENGS}
        for o in self.ops:
            if o.kind == "c" and o.signal:
                cnt[o.eng] += 1
                o.val = cnt[o.eng]
                o.sem = self.prog_sem[o.eng]
        per = {e: [] for e in ENGS}
        for o in self.ops:
            per[o.eng].append(o)
        final_waits = [(s.h, s.count) for s in self.dsems if s.count > 0]
        final_waits += [(self.prog_sem[e], cnt[e]) for e in ("pe", "act", "dve", "pool") if cnt[e] > 0]

        def run(eng_name, e):
            waited = {}
            for o in per[eng_name]:
                need = {}
                for d in o.deps:
                    if d.kind == "c" and o.kind == "c" and d.eng == o.eng and o.eng == "pe":
                        continue
                    h = d.sem if d.kind == "c" else d.sem.h
                    k = id(h)
                    if d.val > need.get(k, (None, 0))[1]:
                        need[k] = (h, d.val)
                for k, (h, v) in need.items():
                    if waited.get(k, 0) < v:
                        e.wait_ge(h, v)
                        waited[k] = v
                ins = o.fn(e)
                if o.kind == "d":
                    ins.then_inc(o.sem.h, 16)
                elif o.signal:
                    ins.then_inc(o.sem, 1)
            if eng_name == "sp":
                for h, v in final_waits:
                    e.wait_ge(h, v)

        block.tensor(lambda e: run("pe", e))
        block.scalar(lambda e: run("act", e))
        block.vector(lambda e: run("dve", e))
        block.gpsimd(lambda e: run("pool", e))
        block.sync(lambda e: run("sp", e))


def build_program(cfg):
    D, DFF, NTOK, PRE = cfg["D"], cfg["DFF"], cfg["NTOK"], cfg["PRE"]
    DRG = D // 2
    DS5 = D // 2
    NDT = D // 128
    NFT = DFF // 128
    NFH = NFT // 2
    NCT = DRG // 128
    NHD = DRG // 256
    NG = DS5 // 16
    NPR = NG // 2
    NTL = NTOK // 512
    TT = PRE + NTOK
    W = 512 + PRE
    assert NFT % 2 == 0 and NTOK % 512 == 0 and PRE == 16

    nc = bass.Bass("TRN2", target_bir_lowering=False)
    di = lambda name, shape: nc.dram_tensor(name, list(shape), F32, kind="ExternalInput").ap()
    x_d = di("x", [TT, D])
    g_ffn1_d = di("ffn1_norm", [D]); g_mix_d = di("mix_norm", [D]); g_ffn2_d = di("ffn2_norm", [D]); g_fin_d = di("final_norm", [D])
    g_rgo_d = di("rg_out_norm", [DRG]); g_s5o_d = di("s5_out_norm", [DS5])
    w1g_d = di("ffn1_w_gate", [D, DFF]); w1u_d = di("ffn1_w_up", [D, DFF]); w1d_d = di("ffn1_w_down", [DFF, D])
    w2g_d = di("ffn2_w_gate", [D, DFF]); w2u_d = di("ffn2_w_up", [D, DFF]); w2d_d = di("ffn2_w_down", [DFF, D])
    win_d = di("w_in", [D, 3 * DRG]); wout_d = di("w_out", [D, D])
    cw_d = di("rg_conv_w", [4, DRG]); cb_d = di("rg_conv_b", [DRG])
    wa_d = di("rg_w_a", [NHD, 256, 256]); ba_d = di("rg_b_a", [DRG]); wx_d = di("rg_w_x", [NHD, 256, 256]); bx_d = di("rg_b_x", [DRG])
    lam_d = di("rg_lambda", [DRG])
    slr_d = di("s5_lambda_re", [NG, 64]); sli_d = di("s5_lambda_im", [NG, 64]); sdt_d = di("s5_log_dt", [NG])
    sbr_d = di("s5_b_re", [NG, 64, 16]); sbi_d = di("s5_b_im", [NG, 64, 16])
    scr_d = di("s5_c_re", [NG, 16, 64]); sci_d = di("s5_c_im", [NG, 16, 64])
    sd_d = di("s5_d", [DS5]); gw_d = di("s5_glu_w", [DS5, DS5]); gb_d = di("s5_glu_b", [DS5])
    out_d = nc.dram_tensor("out", [NTOK, D], F32, kind="ExternalOutput").ap()
    dint = lambda name, shape, dt=F32: nc.dram_tensor(name, list(shape), dt, kind=("ExternalOutput" if cfg.get("DEBUG") else "Internal")).ap()
    H0 = dint("H0", [D, TT]); H1 = dint("H1", [D, TT]); H2 = dint("H2", [D, TT]); H3 = dint("H3", [D, TT])
    PROJ = dint("PROJ", [3 * DRG, TT])
    YH = dint("YH", [D, NTOK], BF16)
    RSD = dint("RSD", [2, 128, NTOK])
    TABD = dint("TABD", [NPR, 128, 2, 512])
    LHD = dint("LHD", [NPR, 128, 4, 128], BF16)
    DBG = dint("DBG", [128, 4, 512 + PRE])

    stack = ExitStack()
    with stack:
        P = Prog(nc, stack)
        sb = lambda name, shape, dt=F32: stack.enter_context(nc.sbuf_tensor(name, list(shape), dt))[:]
        PS = [stack.enter_context(nc.psum_tensor("ps%d" % i, [128, 512], F32))[:] for i in range(8)]
        PSB = [Buf("ps%d" % i) for i in range(8)]
        rot = {"A": [0, 1, 2, 7], "P": [3, 4]}
        rot_i = {"A": 0, "P": 0}

        def bank(pool):
            i = rot[pool][rot_i[pool] % len(rot[pool])]
            rot_i[pool] += 1
            return i
        SSQM, SSQP, BT = 5, 6, 7

        ident = sb("ident", [128, 128]); iot_a = sb("iot_a", [128, 128]); iot_b = sb("iot_b", [128, 128])
        ones_b = sb("ones_b", [128, 128], BF16)
        iota_f = sb("iota_f", [128, 512])
        epst = sb("epst", [128, 1]);
        G1 = sb("g_ffn1", [128, NDT]); GM = sb("g_mix", [128, NDT]); G2 = sb("g_ffn2", [128, NDT]); GF = sb("g_fin", [128, NDT])
        GRO = sb("g_rgo", [128, NCT]); GSO = sb("g_s5o", [128, NCT])
        CW = sb("cw", [128, 4, NCT]); CB = sb("cb", [128, NCT]); BA = sb("ba", [128, NCT]); BX = sb("bx", [128, NCT])
        COEF = sb("coef", [128, NCT]); COEF2 = sb("coef2", [128, NCT])
        SD = sb("sd", [128, NCT]); GB = sb("gb", [128, NCT])
        HST = sb("hst", [128, NCT])
        WA = sb("wa", [128, NHD, 2, 256], BF16); WX = sb("wx", [128, NHD, 2, 256], BF16)
        RHO = sb("rho", [128, NPR]); OM = sb("om", [128, NPR])
        CL512 = sb("cl512", [128, NPR]); SL512 = sb("sl512", [128, NPR]); NSL512 = sb("nsl512", [128, NPR])
        CL16 = sb("cl16", [128, NPR]); SL16 = sb("sl16", [128, NPR]); NSL16 = sb("nsl16", [128, NPR])
        ZST = sb("zst", [128, NPR, 2])
        b_const = Buf("const")
        ARENA_B = 182 * 1024
        arena = sb("arena", [128, ARENA_B // 2], BF16)
        a_off = [0]

        def carve(shape_free, dt, nbuf=1):
            esz = 4 if dt == F32 else 2
            n = int(np.prod(shape_free))
            res = []
            for _ in range(nbuf):
                nb = (n * esz + 31) // 32 * 32
                o = a_off[0]
                assert o + nb <= ARENA_B, ("arena overflow", o + nb)
                v = arena[:, o // 2:(o + n * esz) // 2]
                if dt == F32:
                    v = v.bitcast(F32)
                if len(shape_free) == 2:
                    v = v.rearrange("p (a b) -> p a b", a=shape_free[0])
                elif len(shape_free) == 3:
                    v = v.rearrange("p (a b c) -> p a b c", a=shape_free[0], b=shape_free[1])
                a_off[0] = o + nb
                res.append((v, Buf("arena@%d" % o)))
            return res

        class Rot:
            def __init__(self, items):
                self.items = items
                self.i = 0

            def next(self):
                it = self.items[self.i % len(self.items)]
                self.i += 1
                return it

        sem_c = P.dsem("const")

        nc_allow = stack.enter_context(nc.allow_non_contiguous_dma(reason="small strided parameter loads"))

        def vec_fm(dst, src, n):
            P.dma("sp", lambda e, dst=dst, src=src: e.dma_start(out=dst[:], in_=src.rearrange("(t p) -> p t", p=128)),
                  sem_c, writes=[b_const])

        for dst, src, n in ((G1, g_ffn1_d, NDT), (GM, g_mix_d, NDT), (G2, g_ffn2_d, NDT), (GF, g_fin_d, NDT),
                            (GRO, g_rgo_d, NCT), (GSO, g_s5o_d, NCT), (CB, cb_d, NCT), (BA, ba_d, NCT), (BX, bx_d, NCT),
                            (COEF, lam_d, NCT), (SD, sd_d, NCT), (GB, gb_d, NCT)):
            vec_fm(dst, src, n)
        for k in range(4):
            P.dma("sp", lambda e, k=k: e.dma_start(out=CW[:, k, :], in_=cw_d[k, :].rearrange("(t p) -> p t", p=128)),
                  sem_c, writes=[b_const])
        sem_cw = P.dsem("constw")
        b_wax = Buf("wax")
        P.dma("pool", lambda e: e.dma_start(out=WA[:], in_=wa_d.rearrange("h (it p) j -> p h it j", p=128)), sem_cw, writes=[b_wax])
        P.dma("pool", lambda e: e.dma_start(out=WX[:], in_=wx_d.rearrange("h (it p) j -> p h it j", p=128)), sem_cw, writes=[b_wax])
        a_off[0] = 0
        (LRE, _), (LIM, _), (LDT, _), (t0, _), (t1, _), (t2, _), (t3, _), (KR, _), (KI, _) = carve([NPR], F32, 9)
        (BR0, _), (BI0, _), (BBR, _), (BBI, _), (TB, _) = carve([NPR, 16], F32, 5)
        (MRE, _), (MIM, _) = carve([NPR, 32], F32, 2)
        (CNR, _), (CNI, _) = carve([NCT, 64], F32, 2)
        (CDR, _), (CDI, _) = carve([NCT, 128], F32, 2)
        (LT, bLT), = carve([NPR, 4, 128], BF16)
        CRE = LT[:, :, 0, :]; CIN = LT[:, :, 1, :]; BRE = LT[0:32, :, 2, :]; BIM = LT[0:32, :, 3, :]
        bS = Buf("s5setup")
        cdma = lambda q, fn: P.dma(q, fn, sem_c, writes=[bS])
        cdma("sp", lambda e: e.dma_start(out=LRE, in_=slr_d.rearrange("(pr g2) n -> (g2 n) pr", g2=2)))
        cdma("sp", lambda e: e.dma_start(out=LIM, in_=sli_d.rearrange("(pr g2) n -> (g2 n) pr", g2=2)))
        for g2 in range(2):
            cdma("sp", lambda e, g2=g2: e.dma_start(out=LDT[g2 * 64:(g2 + 1) * 64, :],
                                                     in_=sdt_d.rearrange("(pr g2) -> g2 pr", g2=2)[g2:g2 + 1, :].broadcast_to([64, NPR])))
        cdma("sp", lambda e: e.dma_start(out=BR0, in_=sbr_d.rearrange("(pr g2) n c -> (g2 n) pr c", g2=2)))
        cdma("sp", lambda e: e.dma_start(out=BI0, in_=sbi_d.rearrange("(pr g2) n c -> (g2 n) pr c", g2=2)))
        cdma("sp", lambda e: e.dma_start(out=CNR, in_=scr_d.rearrange("(ct g8) c n -> (g8 c) ct n", g8=8)))
        cdma("sp", lambda e: e.dma_start(out=CNI, in_=sci_d.rearrange("(ct g8) c n -> (g8 c) ct n", g8=8)))
        P.barrier()
        P.op("pool", lambda e: e.iota(iot_a[:], [[1, 128]], base=0, channel_multiplier=0, allow_small_or_imprecise_dtypes=True), writes=[b_const])
        P.op("pool", lambda e: e.iota(iot_b[:], [[0, 128]], base=0, channel_multiplier=1, allow_small_or_imprecise_dtypes=True), writes=[b_const])
        P.op("pool", lambda e: e.iota(iota_f[:], [[1, 512]], base=0, channel_multiplier=0, allow_small_or_imprecise_dtypes=True), writes=[b_const])
        P.op("dve", lambda e: e.tensor_tensor(out=ident[:], in0=iot_a[:], in1=iot_b[:], op=ALU.is_equal), reads=[b_const], writes=[b_const])
        P.op("dve", lambda e: e.memset(ones_b[:], 1.0), writes=[b_const])
        P.op("dve", lambda e: e.memset(epst[:], EPS), writes=[b_const])
        P.op("dve", lambda e: e.memset(HST[:], 0.0), writes=[b_const])
        P.op("dve", lambda e: e.memset(ZST[:], 0.0), writes=[b_const])
        P.op("act", lambda e: e.activation(out=COEF[:], in_=COEF[:], func=AF.Exp, scale=-1.0), reads=[b_const], writes=[b_const])
        P.op("dve", lambda e: e.tensor_scalar(out=COEF[:], in0=COEF[:], scalar1=1.0, scalar2=None, op0=ALU.add), reads=[b_const], writes=[b_const])
        P.op("act", lambda e: e.activation(out=COEF[:], in_=COEF[:], func=AF.Ln), reads=[b_const], writes=[b_const])
        P.op("dve", lambda e: e.tensor_scalar(out=COEF2[:], in0=COEF[:], scalar1=-16.0, scalar2=None, op0=ALU.mult), reads=[b_const], writes=[b_const])
        P.op("dve", lambda e: e.tensor_scalar(out=COEF[:], in0=COEF[:], scalar1=-8.0, scalar2=None, op0=ALU.mult), reads=[b_const], writes=[b_const])

        S = lambda eng, fn: P.op(eng, fn, reads=[bS, b_const], writes=[bS])
        TS = lambda o, i, s1, s2, op0, op1=None: (lambda e: e.tensor_scalar(out=o, in0=i, scalar1=s1, scalar2=s2, op0=op0, op1=op1) if op1 is not None
                                                  else e.tensor_scalar(out=o, in0=i, scalar1=s1, scalar2=None, op0=op0))
        TTn = lambda o, a, b, op: (lambda e: e.tensor_tensor(out=o, in0=a, in1=b, op=op))
        ACTf = lambda o, i, f, **kw: (lambda e: e.activation(out=o, in_=i, func=f, **kw))

        def sincos(dst_c, dst_s, ang, tmp, tmp2):
            for dst, shift in ((dst_s, 0.0), (dst_c, math.pi / 2)):
                S("dve", TS(tmp2, ang, shift, None, ALU.add))
                S("dve", TS(tmp, tmp2, 1.0 / TWO_PI, MAGIC, ALU.mult, ALU.add))
                S("dve", TS(tmp, tmp, MAGIC, -TWO_PI, ALU.subtract, ALU.mult))
                S("dve", TTn(tmp, tmp, tmp2, ALU.add))
                S("dve", TS(tmp, tmp, -PI_SAFE, PI_SAFE, ALU.max, ALU.min))
                S("act", ACTf(dst, tmp, AF.Sin))

        S("act", ACTf(LDT, LDT, AF.Exp))
        S("dve", TTn(t0, LRE, LDT, ALU.mult))
        S("act", ACTf(RHO, t0, AF.Exp))
        S("dve", TTn(OM, LIM, LDT, ALU.mult))
        sincos(t0, t1, OM, t2, t3)
        S("dve", TTn(t0, t0, RHO, ALU.mult))
        S("dve", TTn(t1, t1, RHO, ALU.mult))
        S("dve", TS(t0, t0, -1.0, None, ALU.add))
        S("dve", TTn(t2, LRE, LRE, ALU.mult))
        S("dve", TTn(t3, LIM, LIM, ALU.mult))
        S("dve", TTn(t2, t2, t3, ALU.add))
        S("dve", lambda e: e.reciprocal(out=t2, in_=t2))
        S("dve", TTn(KR, t0, LRE, ALU.mult))
        S("dve", TTn(t3, t1, LIM, ALU.mult))
        S("dve", TTn(KR, KR, t3, ALU.add))
        S("dve", TTn(KR, KR, t2, ALU.mult))
        S("dve", TTn(KI, t1, LRE, ALU.mult))
        S("dve", TTn(t3, t0, LIM, ALU.mult))
        S("dve", TTn(KI, KI, t3, ALU.subtract))
        S("dve", TTn(KI, KI, t2, ALU.mult))
        for L, cl, sl, nsl in ((512.0, CL512, SL512, NSL512), (16.0, CL16, SL16, NSL16)):
            S("dve", TS(t0, OM, L, None, ALU.mult))
            sincos(cl[:], sl[:], t0, t2, t3)
            S("dve", TS(nsl[:], sl[:], -1.0, None, ALU.mult))
        kb = lambda k: k.unsqueeze(2).broadcast_to([128, NPR, 16])
        S("dve", TTn(BBR, BR0, kb(KR), ALU.mult))
        S("dve", TTn(TB, BI0, kb(KI), ALU.mult))
        S("dve", TTn(BBR, BBR, TB, ALU.subtract))
        S("dve", TTn(BBI, BI0, kb(KR), ALU.mult))
        S("dve", TTn(TB, BR0, kb(KI), ALU.mult))
        S("dve", TTn(BBI, BBI, TB, ALU.add))
        P.op("pool", lambda e: e.memset(LT, 0.0), writes=[bLT])
        for M_, BB_ in ((MRE, BBR), (MIM, BBI)):
            S("dve", lambda e, M_=M_: e.memset(M_, 0.0))
            S("dve", lambda e, M_=M_, BB_=BB_: e.tensor_copy(out=M_[0:64, :, 0:16], in_=BB_[0:64, :, :]))
            S("dve", lambda e, M_=M_, BB_=BB_: e.tensor_copy(out=M_[64:128, :, 16:32], in_=BB_[64:128, :, :]))
        for M_, dstT in ((MRE, BRE), (MIM, BIM)):
            for p0 in range(0, NPR, 4):
                np_ = min(4, NPR - p0)
                bk = bank("A")
                for j in range(np_):
                    P.op("pe", lambda e, bk=bk, j=j, M_=M_, p0=p0: e.transpose(out=PS[bk][0:32, j * 128:(j + 1) * 128], in_=M_[:, p0 + j, :], identity=ident[:]),
                         reads=[bS, b_const], writes=[PSB[bk]])
                P.op("act", lambda e, bk=bk, dstT=dstT, p0=p0, np_=np_: e.activation(
                    out=dstT[:, p0:p0 + np_, :], in_=PS[bk][0:32, 0:np_ * 128].rearrange("p (a b) -> p a b", a=np_), func=AF.Copy),
                    reads=[PSB[bk]], writes=[bLT])
        for CN_, CD_, dstC, sgn in ((CNR, CDR, CRE, 1.0), (CNI, CDI, CIN, -1.0)):
            S("dve", lambda e, CN_=CN_, CD_=CD_: e.tensor_copy(out=CD_[:, :, 0:64], in_=CN_))
            S("dve", lambda e, CN_=CN_, CD_=CD_: e.tensor_copy(out=CD_[:, :, 64:128], in_=CN_))
            for ct in range(NCT):
                bk = bank("A")
                P.op("pe", lambda e, bk=bk, CD_=CD_, ct=ct: e.transpose(out=PS[bk][:, 0:128], in_=CD_[:, ct, :], identity=ident[:]),
                     reads=[bS, b_const], writes=[PSB[bk]])
                for j4 in range(4):
                    pr = ct * 4 + j4
                    for g2 in range(2):
                        g8 = 2 * j4 + g2
                        P.op("act", lambda e, bk=bk, dstC=dstC, pr=pr, j4=j4, g2=g2, g8=g8, sgn=sgn: e.activation(
                            out=dstC[g2 * 64:(g2 + 1) * 64, pr, 32 * j4 + 16 * g2:32 * j4 + 16 * g2 + 16],
                            in_=PS[bk][g2 * 64:(g2 + 1) * 64, g8 * 16:(g8 + 1) * 16], func=AF.Copy, scale=sgn),
                            reads=[PSB[bk]], writes=[bLT])
        dL = Buf("LHD", dram=True)
        P.dma("sp", lambda e: e.dma_start(out=LHD.rearrange("pr p f c -> p pr f c"), in_=LT), None, reads=[bLT], writes=[dL])
        a_off_tab = a_off[0]
        tabs = Rot(carve([2, 512], F32, 2))
        (ANG, _), (AN2, _), (RED, _) = carve([512], F32, 3)
        bT = Buf("tabtmp")
        dT = Buf("TABD", dram=True)
        npi = sb("npi", [128, 1])
        P.op("dve", lambda e: e.memset(npi[:], -math.pi), writes=[b_const])
        for pr in range(NPR):
            tab, btab = tabs.next()
            P.op("dve", lambda e, pr=pr: e.tensor_scalar(out=ANG, in0=iota_f[:], scalar1=OM[:, pr:pr + 1], scalar2=None, op0=ALU.mult),
                 reads=[b_const, bS], writes=[bT])
            for idx, shift in ((1, 0.0), (0, math.pi / 2)):
                P.op("dve", TS(AN2, ANG, shift, None, ALU.add), reads=[bT], writes=[bT])
                P.op("dve", TS(RED, AN2, 1.0 / TWO_PI, MAGIC, ALU.mult, ALU.add), reads=[bT], writes=[bT])
                P.op("dve", TS(RED, RED, MAGIC, -TWO_PI, ALU.subtract, ALU.mult), reads=[bT], writes=[bT])
                P.op("dve", TTn(RED, RED, AN2, ALU.add), reads=[bT], writes=[bT])
                P.op("dve", TS(RED, RED, -PI_SAFE, PI_SAFE, ALU.max, ALU.min), reads=[bT], writes=[bT])
                P.op("act", lambda e, tab=tab, idx=idx: e.activation(out=tab[:, idx, :], in_=RED, func=AF.Sin), reads=[bT], writes=[btab])
            P.dma("sp", lambda e, tab=tab, pr=pr: e.dma_start(out=TABD[pr], in_=tab), None, reads=[btab], writes=[dT])
        P.barrier()

        a_off[0] = 0
        ring = carve([8192], BF16, 4)
        ringF = arena[:, 0:4 * 8192].bitcast(F32).rearrange("p (s d) -> p s d", s=4)
        ring_i = [0]
        (XN, bXN), = carve([NDT, W], BF16)
        NACT = max(NFH, cfg.get('ACT_PAD', 0))
        (ACT_, _), = carve([NACT, W], BF16)
        bACT = [Buf("act%d" % j) for j in range(NFH)]
        if NACT * W >= 2 * D:
            XPRE = ACT_.rearrange("p a b -> p (a b)")[:, 0:2 * D].bitcast(F32)
        else:
            (XPRE, _), = carve([D], F32)
        bXPRE = Buf("xpre")
        HS = Rot(carve([W], F32, 3)); OS = Rot(carve([W], F32, 3)); SG = Rot(carve([W], F32, 2)); SQ = Rot(carve([W], BF16, 2))
        (RSTD, bRSTD), = carve([W], F32)
        (RR, bRR), (RS_, bRS) = carve([512], F32, 2)
        T12 = Rot(carve([2, 512], F32, 2))
        stage13_end = a_off[0]

        def segs(t, with_pre=True):
            res = []
            if t == 0 and with_pre:
                res.append((0, PRE, 0, True))
            res.append((PRE, 512, PRE + 512 * t, False))
            return res

        def ring_next():
            i = ring_i[0] % 4
            ring_i[0] += 1
            return i

        def load_w(slot, off, src_ap, shape3):
            v, b = ring[slot]
            dst = v[:, off:off + shape3[0] * shape3[1]].rearrange("p (a b) -> p a b", a=shape3[0])
            P.dma("pool", lambda e, dst=dst, src_ap=src_ap: e.dma_start(out=dst, in_=src_ap), None, writes=[b])
            return dst

        def ssq_accum(o_ap, bo, sg, first, last, dve_sq=False):
            c0, n, _, is_pre = sg
            sq, bsq = SQ.next()
            P.op("act", lambda e, sq=sq, o_ap=o_ap: e.activation(out=sq[:, c0:c0 + n], in_=o_ap[:, c0:c0 + n], func=AF.Square), reads=[bo], writes=[bsq])
            bk = SSQP if is_pre else SSQM
            P.op("pe", lambda e, sq=sq, bk=bk: e.matmul(PS[bk][:, 0:n], lhsT=ones_b[:], rhs=sq[:, c0:c0 + n], start=first, stop=last),
                 reads=[bsq, b_const], writes=[PSB[bk]])

        def make_rstd(dst, bdst, sgl, dim, c_of=lambda sg: sg[0]):
            for sg in sgl:
                c0, n, _, is_pre = sg
                bk = SSQP if is_pre else SSQM
                cc = c_of(sg)
                P.op("act", lambda e, bk=bk, n=n, cc=cc: e.activation(out=dst[:, cc:cc + n], in_=PS[bk][:, 0:n], func=AF.Sqrt, bias=epst[:, 0:1], scale=1.0 / dim),
                     reads=[PSB[bk], b_const], writes=[bdst])
                P.op("dve", lambda e, n=n, cc=cc: e.reciprocal(out=dst[:, cc:cc + n], in_=dst[:, cc:cc + n]), reads=[bdst], writes=[bdst])

        hbuf = {}

        def hb(H, dt, t):
            k = (id(H), dt, t)
            if k not in hbuf:
                hbuf[k] = Buf("H", dram=True)
            return hbuf[k]

        def make_xn(Hsrc, G, t, sgl):
            for dt in range(NDT):
                hs, bhs = HS.next()
                for sg in sgl:
                    c0, n, d0, _ = sg
                    P.dma("sp", lambda e, hs=hs, dt=dt, c0=c0, n=n, d0=d0: e.dma_start(out=hs[:, c0:c0 + n], in_=Hsrc[dt * 128:(dt + 1) * 128, d0:d0 + n]),
                          None, reads=[hb(Hsrc, dt, t)], writes=[bhs])
                c0 = sgl[0][0]
                c1 = sgl[-1][0] + sgl[-1][1]
                P.op("dve", lambda e, hs=hs, dt=dt, c0=c0, c1=c1: e.scalar_tensor_tensor(
                    out=XN[:, dt, c0:c1], in0=hs[:, c0:c1], scalar=G[:, dt:dt + 1], in1=RSTD[:, c0:c1], op0=ALU.mult, op1=ALU.mult),
                    reads=[bhs, bRSTD, b_const], writes=[bXN])

        def ffn(t, sgl, Hsrc, Hdst, wg_d, wu_d, wd_d):
            for hf in range(2):
                for jj in range(NFH):
                    j = hf * NFH + jj
                    slot = ring_next()
                    wg = load_w(slot, 0, wg_d[:, j * 128:(j + 1) * 128].rearrange("(kt p) c -> p kt c", p=128), (NDT, 128))
                    wu = load_w(slot, NDT * 128, wu_d[:, j * 128:(j + 1) * 128].rearrange("(kt p) c -> p kt c", p=128), (NDT, 128))
                    bw = ring[slot][1]
                    accs = []
                    for w_ in (wg, wu):
                        bks = [bank("P") if sg[3] else bank("A") for sg in sgl]
                        for k in range(NDT):
                            for sg, bk in zip(sgl, bks):
                                c0, n = sg[0], sg[1]
                                P.op("pe", lambda e, bk=bk, w_=w_, k=k, c0=c0, n=n: e.matmul(PS[bk][:, 0:n], lhsT=w_[:, k, :], rhs=XN[:, k, c0:c0 + n], start=(k == 0), stop=(k == NDT - 1)),
                                     reads=[bw, bXN], writes=[PSB[bk]])
                        accs.append(bks)
                    sgt, bsg = SG.next()
                    for si, sg in enumerate(sgl):
                        c0, n = sg[0], sg[1]
                        bg, bu = accs[0][si], accs[1][si]
                        P.op("act", lambda e, sgt=sgt, bg=bg, c0=c0, n=n: e.activation(out=sgt[:, c0:c0 + n], in_=PS[bg][:, 0:n], func=AF.Silu), reads=[PSB[bg]], writes=[bsg])
                        P.op("dve", lambda e, sgt=sgt, bu=bu, jj=jj, c0=c0, n=n: e.tensor_tensor(out=ACT_[:, jj, c0:c0 + n], in0=sgt[:, c0:c0 + n], in1=PS[bu][:, 0:n], op=ALU.mult),
                             reads=[bsg, PSB[bu]], writes=[bACT[jj]])
                for m in range(NDT):
                    slot = ring_next()
                    wd = load_w(slot, 0, wd_d[hf * NFH * 128:(hf + 1) * NFH * 128, m * 128:(m + 1) * 128].rearrange("(ft p) c -> p ft c", p=128), (NFH, 128))
                    bw = ring[slot][1]
                    bks = [bank("P") if sg[3] else bank("A") for sg in sgl]
                    for ft in range(NFH):
                        for sg, bk in zip(sgl, bks):
                            c0, n = sg[0], sg[1]
                            P.op("pe", lambda e, bk=bk, wd=wd, ft=ft, c0=c0, n=n: e.matmul(PS[bk][:, 0:n], lhsT=wd[:, ft, :], rhs=ACT_[:, ft, c0:c0 + n], start=(ft == 0), stop=(ft == NFH - 1)),
                                 reads=[bw, bACT[ft]], writes=[PSB[bk]])
                    Hin = Hsrc if hf == 0 else Hdst
                    hs, bhs = HS.next()
                    o, bo = OS.next()
                    for sg, bk in zip(sgl, bks):
                        c0, n, d0, _ = sg
                        P.dma("sp", lambda e, hs=hs, m=m, c0=c0, n=n, d0=d0, Hin=Hin: e.dma_start(out=hs[:, c0:c0 + n], in_=Hin[m * 128:(m + 1) * 128, d0:d0 + n]),
                              None, reads=[hb(Hin, m, t)], writes=[bhs])
                        P.op("dve", lambda e, o=o, hs=hs, bk=bk, c0=c0, n=n: e.scalar_tensor_tensor(out=o[:, c0:c0 + n], in0=PS[bk][:, 0:n], scalar=0.5, in1=hs[:, c0:c0 + n], op0=ALU.mult, op1=ALU.add),
                             reads=[PSB[bk], bhs], writes=[bo])
                    for sg in sgl:
                        c0, n, d0, _ = sg
                        P.dma("sp", lambda e, o=o, m=m, c0=c0, n=n, d0=d0: e.dma_start(out=Hdst[m * 128:(m + 1) * 128, d0:d0 + n], in_=o[:, c0:c0 + n]),
                              None, reads=[bo], writes=[hb(Hdst, m, t)])
                        if hf == 1:
                            ssq_accum(o, bo, sg, m == 0, m == NDT - 1)

        def _stage1_tile(t):
            sgl = segs(t)
            for blk in range(4):
                v, b = ring[blk]
                r0 = PRE + 512 * t + 128 * blk
                P.dma("sp", lambda e, blk=blk, r0=r0: e.dma_start(out=ringF[:, blk, 0:D], in_=x_d[r0:r0 + 128, :]), None, writes=[b])
            if t == 0:
                P.dma("sp", lambda e: e.dma_start(out=XPRE[0:PRE, :], in_=x_d[0:PRE, :]), None, writes=[bXPRE] + bACT)
            for dt in range(NDT):
                bk = bank("A")
                for blk in range(4):
                    P.op("pe", lambda e, bk=bk, blk=blk, dt=dt: e.transpose(out=PS[bk][:, blk * 128:(blk + 1) * 128], in_=ringF[:, blk, dt * 128:(dt + 1) * 128], identity=ident[:]),
                         reads=[ring[blk][1], b_const], writes=[PSB[bk]])
                o, bo = OS.next()
                P.op("act", lambda e, o=o, bk=bk: e.activation(out=o[:, PRE:PRE + 512], in_=PS[bk][:, 0:512], func=AF.Copy), reads=[PSB[bk]], writes=[bo])
                if t == 0:
                    bp = bank("P")
                    P.op("pe", lambda e, bp=bp, dt=dt: e.transpose(out=PS[bp][:, 0:PRE], in_=XPRE[0:PRE, dt * 128:(dt + 1) * 128], identity=ident[0:PRE, 0:PRE]),
                         reads=[bXPRE, b_const], writes=[PSB[bp]])
                    P.op("act", lambda e, o=o, bp=bp: e.activation(out=o[:, 0:PRE], in_=PS[bp][:, 0:PRE], func=AF.Copy), reads=[PSB[bp]], writes=[bo])
                for sg in sgl:
                    c0, n, d0, _ = sg
                    P.dma("sp", lambda e, o=o, dt=dt, c0=c0, n=n, d0=d0: e.dma_start(out=H0[dt * 128:(dt + 1) * 128, d0:d0 + n], in_=o[:, c0:c0 + n]),
                          None, reads=[bo], writes=[hb(H0, dt, t)])
                    ssq_accum(o, bo, sg, dt == 0, dt == NDT - 1)
            if cfg.get("DEBUG") and t == 0:
                (dtmp, bdtmp), = carve([W], F32)
                P.op("act", lambda e: e.activation(out=dtmp[:, 0:16], in_=PS[SSQP][:, 0:16], func=AF.Copy), reads=[PSB[SSQP]], writes=[bdtmp])
                P.op("act", lambda e: e.activation(out=dtmp[:, 16:528], in_=PS[SSQM][:, 0:512], func=AF.Copy), reads=[PSB[SSQM]], writes=[bdtmp])
                P.dma("sp", lambda e: e.dma_start(out=DBG[:, 1, :], in_=dtmp), None, reads=[bdtmp], writes=[Buf("dbg", dram=True)])
            make_rstd(RSTD, bRSTD, sgl, D)
            if cfg.get("DEBUG") and t == 0:
                P.dma("sp", lambda e: e.dma_start(out=DBG[:, 0, :], in_=RSTD), None, reads=[bRSTD], writes=[Buf("dbg", dram=True)])
            make_xn(H0, G1, t, sgl)
            ffn(t, sgl, H0, H1, w1g_d, w1u_d, w1d_d)
            make_rstd(RSTD, bRSTD, sgl, D)
            make_xn(H1, GM, t, sgl)
            for c in range(3 * NCT):
                slot = ring_next()
                wi = load_w(slot, 0, win_d[:, c * 128:(c + 1) * 128].rearrange("(kt p) c -> p kt c", p=128), (NDT, 128))
                bw = ring[slot][1]
                bks = [bank("P") if sg[3] else bank("A") for sg in sgl]
                for k in range(NDT):
                    for sg, bk in zip(sgl, bks):
                        c0, n = sg[0], sg[1]
                        P.op("pe", lambda e, bk=bk, wi=wi, k=k, c0=c0, n=n: e.matmul(PS[bk][:, 0:n], lhsT=wi[:, k, :], rhs=XN[:, k, c0:c0 + n], start=(k == 0), stop=(k == NDT - 1)),
                             reads=[bw, bXN], writes=[PSB[bk]])
                o, bo = OS.next()
                for sg, bk in zip(sgl, bks):
                    c0, n, d0, _ = sg
                    P.op("act", lambda e, o=o, bk=bk, c0=c0, n=n: e.activation(out=o[:, c0:c0 + n], in_=PS[bk][:, 0:n], func=AF.Copy), reads=[PSB[bk]], writes=[bo])
                    P.dma("sp", lambda e, o=o, c=c, c0=c0, n=n, d0=d0: e.dma_start(out=PROJ[c * 128:(c + 1) * 128, d0:d0 + n], in_=o[:, c0:c0 + n]),
                          None, reads=[bo], writes=[hb(PROJ, c, t)])
        for t in range(NTL if cfg.get("STOP", 9) >= 1 else 0):
            _stage1_tile(t)
        P.barrier()

        a_off[0] = 0
        U2 = Rot(carve([2, 3 + W], F32, 2)); XC = Rot(carve([2, W], F32, 2)); XCB = Rot(carve([2, W], BF16, 2))
        RT = Rot(carve([W], F32, 2)); AT = Rot(carve([W], F32, 2)); MT = Rot(carve([W], F32, 2)); IT = Rot(carve([W], F32, 2))
        GT = Rot(carve([W], F32, 2)); HT = Rot(carve([W], F32, 2)); YO = Rot(carve([W], F32, 2)); YB = Rot(carve([512], BF16, 3))
        SQ2 = Rot(carve([512], BF16, 2))
        TABS = Rot(carve([2, 512], F32, 4)); UB = Rot(carve([W], BF16, 3)); LH = Rot(carve([4, 128], BF16, 4))
        TA = Rot(carve([W], F32, 4)); TBb = Rot(carve([W], F32, 4)); ZRI = Rot(carve([W], F32, 2)); ZII = Rot(carve([W], F32, 2))
        ZR = Rot(carve([W], F32, 4)); ZI = Rot(carve([W], F32, 4)); PA = Rot(carve([W], F32, 4)); PB = Rot(carve([W], F32, 4))
        XR = Rot(carve([W], BF16, 2)); XI = Rot(carve([W], BF16, 2))
        UF = Rot(carve([W], F32, 2)); YT = Rot(carve([W], F32, 2))
        (ZZ, _), = carve([NCT, 512], BF16)
        bZZ = [Buf("zz%d" % c) for c in range(NCT)]
        GLW = Rot(carve([NCT, 128], BF16, 2))
        GTT = Rot(carve([512], F32, 2)); YS = Rot(carve([512], F32, 2))
        (RO, bRO), = carve([512], F32)
        tiny = sb("tiny", [128, 8])
        btiny2 = [Buf("tiny0"), Buf("tiny1")]
        bH = Buf("hst"); bZ = [Buf("zst%d" % p) for p in range(NPR)]

        def _stage2_tile(t):
            sgl = segs(t)
            d_lo = 0 if t == 0 else PRE + 512 * t
            c_lo = 0 if t == 0 else PRE
            ncol = (W - c_lo)
            main_d0 = PRE + 512 * t
            for hd in range(NHD):
                u2, bu2 = U2.next()
                xc, bxc = XC.next()
                xcb, bxcb = XCB.next()
                for it in range(2):
                    r0 = hd * 256 + it * 128
                    if t == 0:
                        P.op("pool", lambda e, u2=u2, it=it: e.memset(u2[:, it, 0:3], 0.0), writes=[bu2])
                        P.dma("sp", lambda e, u2=u2, it=it, r0=r0: e.dma_start(out=u2[:, it, 3:3 + W], in_=PROJ[r0:r0 + 128, 0:W]), None,
                              reads=[hb(PROJ, r0 // 128, 0)], writes=[bu2])
                    else:
                        P.dma("sp", lambda e, u2=u2, it=it, r0=r0: e.dma_start(out=u2[:, it, PRE:3 + W], in_=PROJ[r0:r0 + 128, main_d0 - 3:main_d0 + 512]), None,
                              reads=[hb(PROJ, r0 // 128, t), hb(PROJ, r0 // 128, t - 1)], writes=[bu2])
                    ct = hd * 2 + it
                    P.op("dve", lambda e, u2=u2, xc=xc, it=it, ct=ct: e.tensor_scalar(out=xc[:, it, c_lo:W], in0=u2[:, it, 3 + c_lo:3 + W], scalar1=CW[:, 3, ct:ct + 1], scalar2=CB[:, ct:ct + 1], op0=ALU.mult, op1=ALU.add),
                         reads=[bu2, b_const], writes=[bxc])
                    for k in range(3):
                        P.op("dve", lambda e, u2=u2, xc=xc, it=it, ct=ct, k=k: e.scalar_tensor_tensor(out=xc[:, it, c_lo:W], in0=u2[:, it, k + c_lo:k + W], scalar=CW[:, k, ct:ct + 1], in1=xc[:, it, c_lo:W], op0=ALU.mult, op1=ALU.add),
                             reads=[bu2, bxc, b_const], writes=[bxc])
                    P.op("act", lambda e, xc=xc, xcb=xcb, it=it: e.activation(out=xcb[:, it, c_lo:W], in_=xc[:, it, c_lo:W], func=AF.Copy), reads=[bxc], writes=[bxcb])
                for jt in range(2):
                    ct = hd * 2 + jt
                    rt, brt = RT.next(); at, bat = AT.next(); mt, bmt = MT.next(); itt, bit = IT.next()
                    gt, bgt = GT.next(); ht, bht = HT.next(); yo, byo = YO.next()
                    P.dma("sp", lambda e, gt=gt, ct=ct: e.dma_start(out=gt[:, c_lo:W], in_=PROJ[DRG + ct * 128:DRG + (ct + 1) * 128, d_lo:d_lo + ncol]), None,
                          reads=[hb(PROJ, NCT + ct, t)], writes=[bgt])
                    for sg in sgl:
                        c0, n, _, is_pre = sg
                        ba_, bx_ = (bank("P"), bank("P")) if is_pre else (bank("A"), bank("A"))
                        for Wm, bk in ((WA, ba_), (WX, bx_)):
                            for it in range(2):
                                P.op("pe", lambda e, Wm=Wm, bk=bk, it=it, jt=jt, hd=hd, xcb=xcb, c0=c0, n=n: e.matmul(PS[bk][:, 0:n], lhsT=Wm[:, hd, it, jt * 128:(jt + 1) * 128], rhs=xcb[:, it, c0:c0 + n], start=(it == 0), stop=(it == 1)),
                                     reads=[bxcb, b_const], writes=[PSB[bk]])
                        P.op("act", lambda e, rt=rt, ba_=ba_, ct=ct, c0=c0, n=n: e.activation(out=rt[:, c0:c0 + n], in_=PS[ba_][:, 0:n], func=AF.Sigmoid, bias=BA[:, ct:ct + 1], scale=1.0), reads=[PSB[ba_], b_const], writes=[brt])
                        P.op("act", lambda e, itt=itt, bx_=bx_, ct=ct, c0=c0, n=n: e.activation(out=itt[:, c0:c0 + n], in_=PS[bx_][:, 0:n], func=AF.Sigmoid, bias=BX[:, ct:ct + 1], scale=1.0), reads=[PSB[bx_], b_const], writes=[bit])
                    P.op("act", lambda e, at=at, rt=rt, ct=ct: e.activation(out=at[:, c_lo:W], in_=rt[:, c_lo:W], func=AF.Exp, scale=COEF[:, ct:ct + 1]), reads=[brt, b_const], writes=[bat])
                    P.op("act", lambda e, mt=mt, rt=rt, ct=ct: e.activation(out=mt[:, c_lo:W], in_=rt[:, c_lo:W], func=AF.Exp, scale=COEF2[:, ct:ct + 1]), reads=[brt, b_const], writes=[bmt])
                    P.op("dve", lambda e, mt=mt: e.tensor_scalar(out=mt[:, c_lo:W], in0=mt[:, c_lo:W], scalar1=-1.0, scalar2=1.0, op0=ALU.mult, op1=ALU.add), reads=[bmt], writes=[bmt])
                    P.op("act", lambda e, mt=mt: e.activation(out=mt[:, c_lo:W], in_=mt[:, c_lo:W], func=AF.Sqrt), reads=[bmt], writes=[bmt])
                    P.op("dve", lambda e, mt=mt, itt=itt: e.tensor_tensor(out=mt[:, c_lo:W], in0=mt[:, c_lo:W], in1=itt[:, c_lo:W], op=ALU.mult), reads=[bmt, bit], writes=[bmt])
                    P.op("dve", lambda e, mt=mt, xc=xc, jt=jt: e.tensor_tensor(out=mt[:, c_lo:W], in0=mt[:, c_lo:W], in1=xc[:, jt, c_lo:W], op=ALU.mult), reads=[bmt, bxc], writes=[bmt])
                    for sg in sgl:
                        c0, n, _, is_pre = sg
                        if is_pre:
                            P.op("dve", lambda e, ht=ht, at=at, mt=mt, c0=c0, n=n: e.tensor_tensor_scan(out=ht[:, c0:c0 + n], data0=at[:, c0:c0 + n], data1=mt[:, c0:c0 + n], initial=0.0, op0=ALU.mult, op1=ALU.add),
                                 reads=[bat, bmt], writes=[bht])
                        else:
                            init = ht[:, PRE - 1:PRE] if t == 0 else HST[:, ct:ct + 1]
                            P.op("dve", lambda e, ht=ht, at=at, mt=mt, c0=c0, n=n, init=init: e.tensor_tensor_scan(out=ht[:, c0:c0 + n], data0=at[:, c0:c0 + n], data1=mt[:, c0:c0 + n], initial=init, op0=ALU.mult, op1=ALU.add),
                                 reads=[bat, bmt, bH, bht], writes=[bht])
                    P.op("act", lambda e, ht=ht, ct=ct: e.activation(out=HST[:, ct:ct + 1], in_=ht[:, W - 1:W], func=AF.Copy), reads=[bht], writes=[bH])
                    P.op("act", lambda e, gt=gt: e.activation(out=gt[:, PRE:W], in_=gt[:, PRE:W], func=AF.Gelu_apprx_tanh), reads=[bgt], writes=[bgt])
                    P.op("dve", lambda e, yo=yo, ht=ht, gt=gt: e.tensor_tensor(out=yo[:, PRE:W], in0=ht[:, PRE:W], in1=gt[:, PRE:W], op=ALU.mult), reads=[bht, bgt], writes=[byo])
                    sq, bsq = SQ2.next()
                    P.op("act", lambda e, sq=sq, yo=yo: e.activation(out=sq, in_=yo[:, PRE:W], func=AF.Square), reads=[byo], writes=[bsq])
                    P.op("pe", lambda e, sq=sq, ct=ct: e.matmul(PS[SSQM][:, 0:512], lhsT=ones_b[:], rhs=sq, start=(ct == 0), stop=(ct == NCT - 1)), reads=[bsq, b_const], writes=[PSB[SSQM]])
                    yb, byb = YB.next()
                    P.op("act", lambda e, yb=yb, yo=yo, ct=ct: e.activation(out=yb, in_=yo[:, PRE:W], func=AF.Copy, scale=GRO[:, ct:ct + 1]), reads=[byo, b_const], writes=[byb])
                    P.dma("sp", lambda e, yb=yb, ct=ct: e.dma_start(out=YH[ct * 128:(ct + 1) * 128, 512 * t:512 * (t + 1)], in_=yb), None, reads=[byb], writes=[hb(YH, ct, t)])
            make_rstd(RO, bRO, [sgl[-1]], DRG, c_of=lambda sg: 0)
            P.dma("sp", lambda e: e.dma_start(out=RSD[0, :, 512 * t:512 * (t + 1)], in_=RO), None, reads=[bRO], writes=[hb(RSD, 0, t)])

            for ct in range(NCT):
                ybk = BT
                caps = []
                rot["A"] = [0, 1, 2, SSQM]
                for j4 in range(4):
                    if j4 % 2 == 0:
                        rot_i["A"] = 0
                        rot_i["P"] = 0
                    pr = ct * 4 + j4
                    tc0 = 2 * (j4 % 2)
                    btiny = btiny2[j4 % 2]
                    P.begin_capture()
                    tab, btab = TABS.next()
                    ub, bub = UB.next()
                    P.dma("sp", lambda e, tab=tab, pr=pr: e.dma_start(out=tab, in_=TABD[pr]), None, reads=[dT], writes=[btab])
                    lh, blh = LH.next()
                    P.dma("sp", lambda e, lh=lh, pr=pr: e.dma_start(out=lh, in_=LHD[pr]), None, reads=[dL], writes=[blh])
                    ur0 = 2 * DRG + 32 * pr
                    P.dma("pool", lambda e, ub=ub, ur0=ur0: e.dma_start(out=ub[0:32, c_lo:W], in_=PROJ[ur0:ur0 + 32, d_lo:d_lo + ncol]), None,
                          reads=[hb(PROJ, ur0 // 128, t)], writes=[bub])
                    ta, bta = TA.next(); tb, btb = TBb.next(); zri, bzri = ZRI.next(); zii, bzii = ZII.next()
                    zr, bzr = ZR.next(); zi, bzi = ZI.next(); pa, bpa = PA.next(); pb, bpb = PB.next()
                    xr, bxr = XR.next(); xi, bxi = XI.next()
                    for sg in sgl:
                        c0, n, _, is_pre = sg
                        if is_pre:
                            bkr = bki = bank("P")
                            oki = PRE
                        else:
                            bkr, bki = bank("A"), bank("A")
                            oki = 0
                        P.op("pe", lambda e, bkr=bkr, lh=lh, ub=ub, c0=c0, n=n: e.matmul(PS[bkr][:, 0:n], lhsT=lh[0:32, 2, :], rhs=ub[0:32, c0:c0 + n], start=True, stop=True), reads=[bub, blh], writes=[PSB[bkr]])
                        P.op("pe", lambda e, bki=bki, lh=lh, ub=ub, c0=c0, n=n, oki=oki: e.matmul(PS[bki][:, oki:oki + n], lhsT=lh[0:32, 3, :], rhs=ub[0:32, c0:c0 + n], start=True, stop=True), reads=[bub, blh], writes=[PSB[bki]])
                        Ct = tab[:, 0, 0:n]
                        St = tab[:, 1, 0:n]
                        cs = slice(c0, c0 + n)
                        P.op("dve", lambda e, ta=ta, bkr=bkr, Ct=Ct, cs=cs, n=n: e.tensor_tensor(out=ta[:, cs], in0=PS[bkr][:, 0:n], in1=Ct, op=ALU.mult), reads=[PSB[bkr], btab], writes=[bta])
                        P.op("dve", lambda e, tb=tb, bki=bki, St=St, cs=cs, n=n, oki=oki: e.tensor_tensor(out=tb[:, cs], in0=PS[bki][:, oki:oki + n], in1=St, op=ALU.mult), reads=[PSB[bki], btab], writes=[btb])
                        P.op("dve", lambda e, zri=zri, ta=ta, tb=tb, cs=cs: e.tensor_tensor(out=zri[:, cs], in0=ta[:, cs], in1=tb[:, cs], op=ALU.add), reads=[bta, btb], writes=[bzri])
                        ta2, bta2 = TA.next(); tb2, btb2 = TBb.next()
                        P.op("dve", lambda e, ta2=ta2, bki=bki, Ct=Ct, cs=cs, n=n, oki=oki: e.tensor_tensor(out=ta2[:, cs], in0=PS[bki][:, oki:oki + n], in1=Ct, op=ALU.mult), reads=[PSB[bki], btab], writes=[bta2])
                        P.op("dve", lambda e, tb2=tb2, bkr=bkr, St=St, cs=cs, n=n: e.tensor_tensor(out=tb2[:, cs], in0=PS[bkr][:, 0:n], in1=St, op=ALU.mult), reads=[PSB[bkr], btab], writes=[btb2])
                        P.op("dve", lambda e, zii=zii, ta2=ta2, tb2=tb2, cs=cs: e.tensor_tensor(out=zii[:, cs], in0=ta2[:, cs], in1=tb2[:, cs], op=ALU.subtract), reads=[bta2, btb2], writes=[bzii])
                        if is_pre:
                            ir, ii_ = 0.0, 0.0
                            rd = [bzri]
                        else:
                            ir, ii_ = ZST[:, pr, 0:1], ZST[:, pr, 1:2]
                            rd = [bzri, bZ[pr]]
                        rho_b = RHO[:, pr:pr + 1].broadcast_to([128, n])
                        P.op("dve", lambda e, zr=zr, zri=zri, cs=cs, ir=ir, rho_b=rho_b: e.tensor_tensor_scan(out=zr[:, cs], data0=rho_b, data1=zri[:, cs], initial=ir, op0=ALU.mult, op1=ALU.add), reads=rd + [b_const], writes=[bzr])
                        rd2 = [bzii] + rd[1:]
                        P.op("dve", lambda e, zi=zi, zii=zii, cs=cs, ii_=ii_, rho_b=rho_b: e.tensor_tensor_scan(out=zi[:, cs], data0=rho_b, data1=zii[:, cs], initial=ii_, op0=ALU.mult, op1=ALU.add), reads=rd2 + [b_const], writes=[bzi])
                        cl, sl, nsl = (CL16, SL16, NSL16) if is_pre else (CL512, SL512, NSL512)
                        e0 = c0 + n - 1
                        P.op("dve", lambda e, zr=zr, e0=e0, cl=cl, pr=pr, tc0=tc0: e.tensor_tensor(out=tiny[:, tc0:tc0 + 1], in0=zr[:, e0:e0 + 1], in1=cl[:, pr:pr + 1], op=ALU.mult), reads=[bzr, b_const], writes=[btiny])
                        P.op("dve", lambda e, zr=zr, e0=e0, sl=sl, pr=pr, tc0=tc0: e.tensor_tensor(out=tiny[:, tc0 + 1:tc0 + 2], in0=zr[:, e0:e0 + 1], in1=sl[:, pr:pr + 1], op=ALU.mult), reads=[bzr, b_const], writes=[btiny])
                        P.op("dve", lambda e, zi=zi, e0=e0, nsl=nsl, pr=pr, tc0=tc0: e.scalar_tensor_tensor(out=ZST[:, pr, 0:1], in0=zi[:, e0:e0 + 1], scalar=nsl[:, pr:pr + 1], in1=tiny[:, tc0:tc0 + 1], op0=ALU.mult, op1=ALU.add), reads=[bzi, btiny, b_const], writes=[bZ[pr]])
                        P.op("dve", lambda e, zi=zi, e0=e0, cl=cl, pr=pr, tc0=tc0: e.scalar_tensor_tensor(out=ZST[:, pr, 1:2], in0=zi[:, e0:e0 + 1], scalar=cl[:, pr:pr + 1], in1=tiny[:, tc0 + 1:tc0 + 2], op0=ALU.mult, op1=ALU.add), reads=[bzi, btiny, b_const], writes=[bZ[pr]])
                        if is_pre:
                            continue
                        eo = "pool" if pr % 3 == 2 else "dve"
                        P.op(eo, lambda e, pa=pa, zr=zr, Ct=Ct, cs=cs: e.tensor_tensor(out=pa[:, cs], in0=zr[:, cs], in1=Ct, op=ALU.mult), reads=[bzr, btab], writes=[bpa])
                        P.op(eo, lambda e, pb=pb, zi=zi, St=St, cs=cs: e.tensor_tensor(out=pb[:, cs], in0=zi[:, cs], in1=St, op=ALU.mult), reads=[bzi, btab], writes=[bpb])
                        P.op(eo, lambda e, xr=xr, pa=pa, pb=pb, cs=cs: e.tensor_tensor(out=xr[:, cs], in0=pa[:, cs], in1=pb[:, cs], op=ALU.subtract), reads=[bpa, bpb], writes=[bxr])
                        pa2, bpa2 = PA.next(); pb2, bpb2 = PB.next()
                        P.op(eo, lambda e, pa2=pa2, zr=zr, St=St, cs=cs: e.tensor_tensor(out=pa2[:, cs], in0=zr[:, cs], in1=St, op=ALU.mult), reads=[bzr, btab], writes=[bpa2])
                        P.op(eo, lambda e, pb2=pb2, zi=zi, Ct=Ct, cs=cs: e.tensor_tensor(out=pb2[:, cs], in0=zi[:, cs], in1=Ct, op=ALU.mult), reads=[bzi, btab], writes=[bpb2])
                        P.op(eo, lambda e, xi=xi, pa2=pa2, pb2=pb2, cs=cs: e.tensor_tensor(out=xi[:, cs], in0=pa2[:, cs], in1=pb2[:, cs], op=ALU.add), reads=[bpa2, bpb2], writes=[bxi])
                        P.op("pe", lambda e, ybk=ybk, lh=lh, xr=xr, cs=cs, j4=j4: e.matmul(PS[ybk][:, 0:512], lhsT=lh[:, 0, :], rhs=xr[:, cs], start=(j4 == 0), stop=False), reads=[bxr, blh], writes=[PSB[ybk]])
                        P.op("pe", lambda e, ybk=ybk, lh=lh, xi=xi, cs=cs, j4=j4: e.matmul(PS[ybk][:, 0:512], lhsT=lh[:, 1, :], rhs=xi[:, cs], start=False, stop=(j4 == 3)), reads=[bxi, blh], writes=[PSB[ybk]])
                    caps.append(P.end_capture())
                P.replay([caps[0], caps[1]])
                P.replay([caps[2], caps[3]])
                rot["A"] = [0, 1, 2]
                uf, buf_ = UF.next()
                yt, byt = YT.next()
                ur = 2 * DRG + ct * 128
                P.dma("sp", lambda e, uf=uf, ur=ur: e.dma_start(out=uf[:, 0:512], in_=PROJ[ur:ur + 128, main_d0:main_d0 + 512]), None, reads=[hb(PROJ, ur // 128, t)], writes=[buf_])
                P.op("dve", lambda e, yt=yt, uf=uf, ybk=ybk, ct=ct: e.scalar_tensor_tensor(out=yt[:, 0:512], in0=uf[:, 0:512], scalar=SD[:, ct:ct + 1], in1=PS[ybk][:, 0:512], op0=ALU.mult, op1=ALU.add),
                     reads=[buf_, PSB[ybk], b_const], writes=[byt])
                P.op("act", lambda e, yt=yt, ct=ct: e.activation(out=ZZ[:, ct, :], in_=yt[:, 0:512], func=AF.Gelu_apprx_tanh), reads=[byt], writes=[bZZ[ct]])
            for cp in range(NCT):
                glw, bglw = GLW.next()
                P.dma("pool", lambda e, glw=glw, cp=cp: e.dma_start(out=glw, in_=gw_d[:, cp * 128:(cp + 1) * 128].rearrange("(kt p) c -> p kt c", p=128)), None, writes=[bglw])
                bk = bank("A")
                for ct in range(NCT):
                    P.op("pe", lambda e, bk=bk, glw=glw, ct=ct: e.matmul(PS[bk][:, 0:512], lhsT=glw[:, ct, :], rhs=ZZ[:, ct, :], start=(ct == 0), stop=(ct == NCT - 1)), reads=[bglw, bZZ[ct]], writes=[PSB[bk]])
                gtt, bgtt = GTT.next(); ys, bys = YS.next()
                P.op("act", lambda e, gtt=gtt, bk=bk, cp=cp: e.activation(out=gtt, in_=PS[bk][:, 0:512], func=AF.Sigmoid, bias=GB[:, cp:cp + 1], scale=1.0), reads=[PSB[bk], b_const], writes=[bgtt])
                P.op("dve", lambda e, ys=ys, gtt=gtt, cp=cp: e.tensor_tensor(out=ys, in0=ZZ[:, cp, :], in1=gtt, op=ALU.mult), reads=[bgtt, bZZ[cp]], writes=[bys])
                sq, bsq = SQ2.next()
                P.op("act", lambda e, sq=sq, ys=ys: e.activation(out=sq, in_=ys, func=AF.Square), reads=[bys], writes=[bsq])
                P.op("pe", lambda e, sq=sq, cp=cp: e.matmul(PS[SSQP][:, 0:512], lhsT=ones_b[:], rhs=sq, start=(cp == 0), stop=(cp == NCT - 1)), reads=[bsq, b_const], writes=[PSB[SSQP]])
                yb, byb = YB.next()
                P.op("act", lambda e, yb=yb, ys=ys, cp=cp: e.activation(out=yb, in_=ys, func=AF.Copy, scale=GSO[:, cp:cp + 1]), reads=[bys, b_const], writes=[byb])
                P.dma("sp", lambda e, yb=yb, cp=cp: e.dma_start(out=YH[DRG + cp * 128:DRG + (cp + 1) * 128, 512 * t:512 * (t + 1)], in_=yb), None, reads=[byb], writes=[hb(YH, NCT + cp, t)])
            P.op("act", lambda e: e.activation(out=RO, in_=PS[SSQP][:, 0:512], func=AF.Sqrt, bias=epst[:, 0:1], scale=1.0 / DS5), reads=[PSB[SSQP], b_const], writes=[bRO])
            P.op("dve", lambda e: e.reciprocal(out=RO, in_=RO), reads=[bRO], writes=[bRO])
            P.dma("sp", lambda e: e.dma_start(out=RSD[1, :, 512 * t:512 * (t + 1)], in_=RO), None, reads=[bRO], writes=[hb(RSD, 1, t)])
        rot["A"] = [0, 1, 2]
        for t in range(NTL if cfg.get("STOP", 9) >= 2 else 0):
            _stage2_tile(t)
        rot["A"] = [0, 1, 2, 7]
        P.barrier()

        def _stage3_tile(t):
            sgl = segs(t, with_pre=False)
            c0, n, d0, _ = sgl[0]
            P.dma("sp", lambda e: e.dma_start(out=XN[:, :, c0:c0 + 512], in_=YH[:, 512 * t:512 * (t + 1)].rearrange("(k p) c -> p k c", p=128)), None,
                  reads=[hb(YH, k, t) for k in range(NDT)], writes=[bXN])
            P.dma("sp", lambda e: e.dma_start(out=RR, in_=RSD[0, :, 512 * t:512 * (t + 1)]), None, reads=[hb(RSD, 0, t)], writes=[bRR])
            P.dma("sp", lambda e: e.dma_start(out=RS_, in_=RSD[1, :, 512 * t:512 * (t + 1)]), None, reads=[hb(RSD, 1, t)], writes=[bRS])
            for m in range(NDT):
                slot = ring_next()
                wo = load_w(slot, 0, wout_d[:, m * 128:(m + 1) * 128].rearrange("(kt p) c -> p kt c", p=128), (NDT, 128))
                bw = ring[slot][1]
                b1, b2 = bank("A"), bank("A")
                for k in range(NDT):
                    bk = b1 if k < NCT else b2
                    kk = k if k < NCT else k - NCT
                    P.op("pe", lambda e, bk=bk, wo=wo, k=k, kk=kk: e.matmul(PS[bk][:, 0:512], lhsT=wo[:, k, :], rhs=XN[:, k, c0:c0 + 512], start=(kk == 0), stop=(kk == NCT - 1)), reads=[bw, bXN], writes=[PSB[bk]])
                t12, bt12 = T12.next()
                hs, bhs = HS.next()
                o, bo = OS.next()
                P.dma("sp", lambda e, hs=hs, m=m: e.dma_start(out=hs[:, c0:c0 + 512], in_=H1[m * 128:(m + 1) * 128, d0:d0 + 512]), None, reads=[hb(H1, m, t)], writes=[bhs])
                P.op("dve", lambda e, t12=t12, b1=b1: e.tensor_tensor(out=t12[:, 0, :], in0=PS[b1][:, 0:512], in1=RR, op=ALU.mult), reads=[PSB[b1], bRR], writes=[bt12])
                P.op("dve", lambda e, t12=t12, b2=b2: e.tensor_tensor(out=t12[:, 1, :], in0=PS[b2][:, 0:512], in1=RS_, op=ALU.mult), reads=[PSB[b2], bRS], writes=[bt12])
                P.op("dve", lambda e, t12=t12: e.tensor_tensor(out=t12[:, 0, :], in0=t12[:, 0, :], in1=t12[:, 1, :], op=ALU.add), reads=[bt12], writes=[bt12])
                P.op("dve", lambda e, t12=t12, o=o, hs=hs: e.tensor_tensor(out=o[:, c0:c0 + 512], in0=t12[:, 0, :], in1=hs[:, c0:c0 + 512], op=ALU.add), reads=[bt12, bhs], writes=[bo])
                P.dma("sp", lambda e, o=o, m=m: e.dma_start(out=H2[m * 128:(m + 1) * 128, d0:d0 + 512], in_=o[:, c0:c0 + 512]), None, reads=[bo], writes=[hb(H2, m, t)])
                ssq_accum(o, bo, sgl[0], m == 0, m == NDT - 1)
            if cfg.get("STOP3", 9) < 1:
                return
            make_rstd(RSTD, bRSTD, sgl, D)
            make_xn(H2, G2, t, sgl)
            ffn(t, sgl, H2, H3, w2g_d, w2u_d, w2d_d)
            make_rstd(RSTD, bRSTD, sgl, D)
            if cfg.get("STOP3", 9) < 2:
                return
            for dt in range(NDT):
                hs, bhs = HS.next()
                o, bo = OS.next()
                P.dma("sp", lambda e, hs=hs, dt=dt: e.dma_start(out=hs[:, c0:c0 + 512], in_=H3[dt * 128:(dt + 1) * 128, d0:d0 + 512]), None, reads=[hb(H3, dt, t)], writes=[bhs])
                P.op("dve", lambda e, o=o, hs=hs, dt=dt: e.scalar_tensor_tensor(out=o[:, c0:c0 + 512], in0=hs[:, c0:c0 + 512], scalar=GF[:, dt:dt + 1], in1=RSTD[:, c0:c0 + 512], op0=ALU.mult, op1=ALU.mult),
                     reads=[bhs, bRSTD, b_const], writes=[bo])
                bk = bank("A")
                for blk in range(4):
                    P.op("pe", lambda e, bk=bk, blk=blk, o=o: e.transpose(out=PS[bk][:, blk * 128:(blk + 1) * 128], in_=o[:, c0 + blk * 128:c0 + (blk + 1) * 128], identity=ident[:]),
                         reads=[bo, b_const], writes=[PSB[bk]])
                P.op("act", lambda e, bk=bk, dt=dt: e.activation(out=ringF[:, :, dt * 128:(dt + 1) * 128], in_=PS[bk][:, 0:512].rearrange("p (a b) -> p a b", a=4), func=AF.Copy),
                     reads=[PSB[bk]], writes=[ring[i][1] for i in range(4)])
            for blk in range(4):
                r0 = 512 * t + 128 * blk
                P.dma("sp", lambda e, blk=blk, r0=r0: e.dma_start(out=out_d[r0:r0 + 128, :], in_=ringF[:, blk, 0:D]), None, reads=[ring[blk][1]])

        for t in range(NTL if cfg.get("STOP", 9) >= 3 else 0):
            _stage3_tile(t)
        with nc.Block() as block:
            P.emit(block)
    return nc


WEIGHT_KEYS = ["ffn1_norm", "ffn1_w_gate", "ffn1_w_up", "ffn1_w_down", "mix_norm", "w_in", "rg_conv_w", "rg_conv_b",
               "rg_w_a", "rg_b_a", "rg_w_x", "rg_b_x", "rg_lambda", "s5_lambda_re", "s5_lambda_im", "s5_log_dt",
               "s5_b_re", "s5_b_im", "s5_c_re", "s5_c_im", "s5_d", "s5_glu_w", "s5_glu_b", "rg_out_norm", "s5_out_norm",
               "w_out", "ffn2_norm", "ffn2_w_gate", "ffn2_w_up", "ffn2_w_down"]


def make_in_maps(cfg, inputs):
    x = np.asarray(inputs["x"], dtype=np.float32)
    meta = np.asarray(inputs["meta_tokens"], dtype=np.float32)
    shared = {}
    for k in WEIGHT_KEYS:
        a = np.asarray(inputs[k], dtype=np.float32)
        a = a.reshape(a.shape[1:])
        if k in ("rg_b_a", "rg_b_x"):
            a = a.reshape(-1)
        shared[k] = np.ascontiguousarray(a)
    shared["final_norm"] = np.ascontiguousarray(np.asarray(inputs["final_norm"], dtype=np.float32))
    maps = []
    for c in range(cfg["NCORES"]):
        m = dict(shared)
        m["x"] = np.ascontiguousarray(np.concatenate([meta, x[c]], axis=0))
        maps.append(m)
    return maps


def kernel(**inputs):
    cfg = full_cfg()
    nc = build_program(cfg)
    maps = make_in_maps(cfg, inputs)
    res = run_bass_kernel_spmd(nc, maps, core_ids=list(range(cfg["NCORES"])))
    return np.stack([np.asarray(r["out"], dtype=np.float32) for r in res.results], axis=0)
```

```python
import math
from contextlib import ExitStack
import numpy as np
import concourse.bass as bass
import concourse.mybir as mybir
from concourse.bass_utils import run_bass_kernel_spmd

F32 = mybir.dt.float32
BF16 = mybir.dt.bfloat16
AF = mybir.ActivationFunctionType
ALU = mybir.AluOpType

EPS = 1e-6
TWO_PI = float(2.0 * math.pi)
PI_SAFE = 3.1415925
MAGIC = 12582912.0


def full_cfg():
    return dict(D=4096, DFF=11008, NTOK=4096, PRE=16, NCORES=4)


class Buf:
    __slots__ = ("name", "last_w", "readers", "dram", "sems")

    def __init__(self, name, dram=False):
        self.name = name
        self.last_w = None
        self.readers = {}
        self.dram = dram
        self.sems = {}


class DSem:
    __slots__ = ("h", "count")

    def __init__(self, h):
        self.h = h
        self.count = 0


class Op:
    __slots__ = ("eng", "fn", "deps", "kind", "sem", "val", "signal")

    def __init__(self, eng, fn, kind):
        self.eng = eng
        self.fn = fn
        self.kind = kind
        self.deps = []
        self.sem = None
        self.val = 0
        self.signal = False


ENGS = ("pe", "act", "dve", "pool", "sp")


class Prog:
    def __init__(self, nc, stack):
        self.nc = nc
        self.stack = stack
        self.ops = []
        self.nsem = 0
        self.prog_sem = {e: self._sem("prog_" + e) for e in ("pe", "act", "dve", "pool")}
        self.last_op = {e: None for e in ENGS}
        self.dsems = []
        self.pending = {e: [] for e in ENGS}

    def _sem(self, name):
        self.nsem += 1
        return self.stack.enter_context(self.nc.semaphore(name))

    def dsem(self, name):
        s = DSem(self._sem("d_" + name))
        self.dsems.append(s)
        return s

    def _track(self, op, reads, writes):
        deps = op.deps
        for b in reads:
            if b.last_w is not None:
                deps.append(b.last_w)
        for b in writes:
            if b.last_w is not None:
                deps.append(b.last_w)
            deps.extend(b.readers.values())
        key = op.eng if op.kind == "c" else ("d", id(op.sem))
        for b in reads:
            b.readers[key] = op
        for b in writes:
            b.last_w = op
            b.readers = {}
        if op.kind == "d":
            op.deps = deps = [d for d in deps if not (d.kind == "d" and d.sem is op.sem)]
        if self.pending[op.eng]:
            deps.extend(self.pending[op.eng])
            self.pending[op.eng] = []
        self.ops.append(op)
        self.last_op[op.eng] = op

    def begin_capture(self):
        self._cap = []

    def end_capture(self):
        c, self._cap = self._cap, None
        return c

    def replay(self, lists):
        n = max(len(l) for l in lists)
        for i in range(n):
            for l in lists:
                if i < len(l):
                    kind, args, kw = l[i]
                    (self.op if kind == "op" else self.dma)(*args, **kw)

    def op(self, eng, fn, reads=(), writes=()):
        if getattr(self, "_cap", None) is not None:
            self._cap.append(("op", (eng, fn), dict(reads=list(reads), writes=list(writes))))
            return None
        o = Op(eng, fn, "c")
        self._track(o, reads, writes)
        return o

    def dma(self, queue, fn, sem, reads=(), writes=()):
        if getattr(self, "_cap", None) is not None:
            self._cap.append(("dma", (queue, fn, sem), dict(reads=list(reads), writes=list(writes))))
            return None
        o = Op(queue, fn, "d")
        if sem is None:
            tgt = [b for b in writes if not b.dram]
            kind = "l"
            if not tgt:
                tgt = [b for b in reads if not b.dram]
                kind = "s"
            b = tgt[0]
            kind += "w" if queue == "pool" else "h"
            if kind not in b.sems:
                b.sems[kind] = self.dsem("%s_%s%d" % (kind, b.name.replace("@", "_"), self.nsem))
            sem = b.sems[kind]
        o.sem = sem
        sem.count += 16
        o.val = sem.count
        self._track(o, reads, writes)
        return o

    def barrier(self):
        lasts = []
        for e in ("pe", "act", "dve", "pool"):
            for o in reversed(self.ops):
                if o.kind == "c" and o.eng == e:
                    lasts.append(o)
                    break
        dm = []
        seen = set()
        for o in reversed(self.ops):
            if o.kind == "d" and id(o.sem) not in seen:
                seen.add(id(o.sem))
                dm.append(o)
        for e in ENGS:
            self.pending[e] = lasts + dm

    def emit(self, block):
        for o in self.ops:
            for d in o.deps:
                if d.kind == "c" and not (o.kind == "c" and d.eng == o.eng and o.eng == "pe"):
                    d.signal = True
        cnt = {e: 0 for e in ENGS}
        for o in self.ops:
            if o.kind == "c" and o.signal:
                cnt[o.eng] += 1
                o.val = cnt[o.eng]
                o.sem = self.prog_sem[o.eng]
        per = {e: [] for e in ENGS}
        for o in self.ops:
            per[o.eng].append(o)
        final_waits = [(s.h, s.count) for s in self.dsems if s.count > 0]
        final_waits += [(self.prog_sem[e], cnt[e]) for e in ("pe", "act", "dve", "pool") if cnt[e] > 0]

        def run(eng_name, e):
            waited = {}
            for o in per[eng_name]:
                need = {}
                for d in o.deps:
                    if d.kind == "c" and o.kind == "c" and d.eng == o.eng and o.eng == "pe":
                        continue
                    h = d.sem if d.kind == "c" else d.sem.h
                    k = id(h)
                    if d.val > need.get(k, (None, 0))[1]:
                        need[k] = (h, d.val)
                for k, (h, v) in need.items():
                    if waited.get(k, 0) < v:
                        e.wait_ge(h, v)
                        waited[k] = v
                ins = o.fn(e)
                if o.kind == "d":
                    ins.then_inc(o.sem.h, 16)
                elif o.signal:
                    ins.then_inc(o.sem, 1)
            if eng_name == "sp":
                for h, v in final_waits:
                    e.wait_ge(h, v)

        block.tensor(lambda e: run("pe", e))
        block.scalar(lambda e: run("act", e))
        block.vector(lambda e: run("dve", e))
        block.gpsimd(lambda e: run("pool", e))
        block.sync(lambda e: run("sp", e))


def build_program(cfg):
    D, DFF, NTOK, PRE = cfg["D"], cfg["DFF"], cfg["NTOK"], cfg["PRE"]
    DRG = D // 2
    DS5 = D // 2
    NDT = D // 128
    NFT = DFF // 128
    NFH = NFT // 2
    NCT = DRG // 128
    NHD = DRG // 256
    NG = DS5 // 16
    NPR = NG // 2
    NTL = NTOK // 512
    TT = PRE + NTOK
    W = 512 + PRE
    assert NFT % 2 == 0 and NTOK % 512 == 0 and PRE == 16

    nc = bass.Bass("TRN2", target_bir_lowering=False)
    di = lambda name, shape: nc.dram_tensor(name, list(shape), F32, kind="ExternalInput").ap()
    x_d = di("x", [TT, D])
    g_ffn1_d = di("ffn1_norm", [D]); g_mix_d = di("mix_norm", [D]); g_ffn2_d = di("ffn2_norm", [D]); g_fin_d = di("final_norm", [D])
    g_rgo_d = di("rg_out_norm", [DRG]); g_s5o_d = di("s5_out_norm", [DS5])
    w1g_d = di("ffn1_w_gate", [D, DFF]); w1u_d = di("ffn1_w_up", [D, DFF]); w1d_d = di("ffn1_w_down", [DFF, D])
    w2g_d = di("ffn2_w_gate", [D, DFF]); w2u_d = di("ffn2_w_up", [D, DFF]); w2d_d = di("ffn2_w_down", [DFF, D])
    win_d = di("w_in", [D, 3 * DRG]); wout_d = di("w_out", [D, D])
    cw_d = di("rg_conv_w", [4, DRG]); cb_d = di("rg_conv_b", [DRG])
    wa_d = di("rg_w_a", [NHD, 256, 256]); ba_d = di("rg_b_a", [DRG]); wx_d = di("rg_w_x", [NHD, 256, 256]); bx_d = di("rg_b_x", [DRG])
    lam_d = di("rg_lambda", [DRG])
    slr_d = di("s5_lambda_re", [NG, 64]); sli_d = di("s5_lambda_im", [NG, 64]); sdt_d = di("s5_log_dt", [NG])
    sbr_d = di("s5_b_re", [NG, 64, 16]); sbi_d = di("s5_b_im", [NG, 64, 16])
    scr_d = di("s5_c_re", [NG, 16, 64]); sci_d = di("s5_c_im", [NG, 16, 64])
    sd_d = di("s5_d", [DS5]); gw_d = di("s5_glu_w", [DS5, DS5]); gb_d = di("s5_glu_b", [DS5])
    out_d = nc.dram_tensor("out", [NTOK, D], F32, kind="ExternalOutput").ap()
    dint = lambda name, shape, dt=F32: nc.dram_tensor(name, list(shape), dt, kind=("ExternalOutput" if cfg.get("DEBUG") else "Internal")).ap()
    H0 = dint("H0", [D, TT]); H1 = dint("H1", [D, TT]); H2 = dint("H2", [D, TT]); H3 = dint("H3", [D, TT])
    PROJ = dint("PROJ", [3 * DRG, TT])
    YH = dint("YH", [D, NTOK], BF16)
    RSD = dint("RSD", [2, 128, NTOK])
    TABD = dint("TABD", [NPR, 128, 2, 512])
    LHD = dint("LHD", [NPR, 128, 4, 128], BF16)
    DBG = dint("DBG", [128, 4, 512 + PRE])

    stack = ExitStack()
    with stack:
        P = Prog(nc, stack)
        sb = lambda name, shape, dt=F32: stack.enter_context(nc.sbuf_tensor(name, list(shape), dt))[:]
        PS = [stack.enter_context(nc.psum_tensor("ps%d" % i, [128, 512], F32))[:] for i in range(8)]
        PSB = [Buf("ps%d" % i) for i in range(8)]
        rot = {"A": [0, 1, 2, 7], "P": [3, 4]}
        rot_i = {"A": 0, "P": 0}

        def bank(pool):
            i = rot[pool][rot_i[pool] % len(rot[pool])]
            rot_i[pool] += 1
            return i
        SSQM, SSQP, BT = 5, 6, 7

        ident = sb("ident", [128, 128]); iot_a = sb("iot_a", [128, 128]); iot_b = sb("iot_b", [128, 128])
        ones_b = sb("ones_b", [128, 128], BF16)
        iota_f = sb("iota_f", [128, 512])
        epst = sb("epst", [128, 1]);
        G1 = sb("g_ffn1", [128, NDT]); GM = sb("g_mix", [128, NDT]); G2 = sb("g_ffn2", [128, NDT]); GF = sb("g_fin", [128, NDT])
        GRO = sb("g_rgo", [128, NCT]); GSO = sb("g_s5o", [128, NCT])
        CW = sb("cw", [128, 4, NCT]); CB = sb("cb", [128, NCT]); BA = sb("ba", [128, NCT]); BX = sb("bx", [128, NCT])
        COEF = sb("coef", [128, NCT]); COEF2 = sb("coef2", [128, NCT])
        SD = sb("sd", [128, NCT]); GB = sb("gb", [128, NCT])
        HST = sb("hst", [128, NCT])
        WA = sb("wa", [128, NHD, 2, 256], BF16); WX = sb("wx", [128, NHD, 2, 256], BF16)
        RHO = sb("rho", [128, NPR]); OM = sb("om", [128, NPR])
        CL512 = sb("cl512", [128, NPR]); SL512 = sb("sl512", [128, NPR]); NSL512 = sb("nsl512", [128, NPR])
        CL16 = sb("cl16", [128, NPR]); SL16 = sb("sl16", [128, NPR]); NSL16 = sb("nsl16", [128, NPR])
        ZST = sb("zst", [128, NPR, 2])
        b_const = Buf("const")
        ARENA_B = 182 * 1024
        arena = sb("arena", [128, ARENA_B // 2], BF16)
        a_off = [0]

        def carve(shape_free, dt, nbuf=1):
            esz = 4 if dt == F32 else 2
            n = int(np.prod(shape_free))
            res = []
            for _ in range(nbuf):
                nb = (n * esz + 31) // 32 * 32
                o = a_off[0]
                assert o + nb <= ARENA_B, ("arena overflow", o + nb)
                v = arena[:, o // 2:(o + n * esz) // 2]
                if dt == F32:
                    v = v.bitcast(F32)
                if len(shape_free) == 2:
                    v = v.rearrange("p (a b) -> p a b", a=shape_free[0])
                elif len(shape_free) == 3:
                    v = v.rearrange("p (a b c) -> p a b c", a=shape_free[0], b=shape_free[1])
                a_off[0] = o + nb
                res.append((v, Buf("arena@%d" % o)))
            return res

        class Rot:
            def __init__(self, items):
                self.items = items
                self.i = 0

            def next(self):
                it = self.items[self.i % len(self.items)]
                self.i += 1
                return it

        sem_c = P.dsem("const")

        nc_allow = stack.enter_context(nc.allow_non_contiguous_dma(reason="small strided parameter loads"))

        def vec_fm(dst, src, n):
            P.dma("sp", lambda e, dst=dst, src=src: e.dma_start(out=dst[:], in_=src.rearrange("(t p) -> p t", p=128)),
                  sem_c, writes=[b_const])

        for dst, src, n in ((G1, g_ffn1_d, NDT), (GM, g_mix_d, NDT), (G2, g_ffn2_d, NDT), (GF, g_fin_d, NDT),
                            (GRO, g_rgo_d, NCT), (GSO, g_s5o_d, NCT), (CB, cb_d, NCT), (BA, ba_d, NCT), (BX, bx_d, NCT),
                            (COEF, lam_d, NCT), (SD, sd_d, NCT), (GB, gb_d, NCT)):
            vec_fm(dst, src, n)
        for k in range(4):
            P.dma("sp", lambda e, k=k: e.dma_start(out=CW[:, k, :], in_=cw_d[k, :].rearrange("(t p) -> p t", p=128)),
                  sem_c, writes=[b_const])
        sem_cw = P.dsem("constw")
        b_wax = Buf("wax")
        P.dma("pool", lambda e: e.dma_start(out=WA[:], in_=wa_d.rearrange("h (it p) j -> p h it j", p=128)), sem_cw, writes=[b_wax])
        P.dma("pool", lambda e: e.dma_start(out=WX[:], in_=wx_d.rearrange("h (it p) j -> p h it j", p=128)), sem_cw, writes=[b_wax])
        a_off[0] = 0
        (LRE, _), (LIM, _), (LDT, _), (t0, _), (t1, _), (t2, _), (t3, _), (KR, _), (KI, _) = carve([NPR], F32, 9)
        (BR0, _), (BI0, _), (BBR, _), (BBI, _), (TB, _) = carve([NPR, 16], F32, 5)
        (MRE, _), (MIM, _) = carve([NPR, 32], F32, 2)
        (CNR, _), (CNI, _) = carve([NCT, 64], F32, 2)
        (CDR, _), (CDI, _) = carve([NCT, 128], F32, 2)
        (LT, bLT), = carve([NPR, 4, 128], BF16)
        CRE = LT[:, :, 0, :]; CIN = LT[:, :, 1, :]; BRE = LT[0:32, :, 2, :]; BIM = LT[0:32, :, 3, :]
        bS = Buf("s5setup")
        cdma = lambda q, fn: P.dma(q, fn, sem_c, writes=[bS])
        cdma("sp", lambda e: e.dma_start(out=LRE, in_=slr_d.rearrange("(pr g2) n -> (g2 n) pr", g2=2)))
        cdma("sp", lambda e: e.dma_start(out=LIM, in_=sli_d.rearrange("(pr g2) n -> (g2 n) pr", g2=2)))
        for g2 in range(2):
            cdma("sp", lambda e, g2=g2: e.dma_start(out=LDT[g2 * 64:(g2 + 1) * 64, :],
                                                     in_=sdt_d.rearrange("(pr g2) -> g2 pr", g2=2)[g2:g2 + 1, :].broadcast_to([64, NPR])))
        cdma("sp", lambda e: e.dma_start(out=BR0, in_=sbr_d.rearrange("(pr g2) n c -> (g2 n) pr c", g2=2)))
        cdma("sp", lambda e: e.dma_start(out=BI0, in_=sbi_d.rearrange("(pr g2) n c -> (g2 n) pr c", g2=2)))
        cdma("sp", lambda e: e.dma_start(out=CNR, in_=scr_d.rearrange("(ct g8) c n -> (g8 c) ct n", g8=8)))
        cdma("sp", lambda e: e.dma_start(out=CNI, in_=sci_d.rearrange("(ct g8) c n -> (g8 c) ct n", g8=8)))
        P.barrier()
        P.op("pool", lambda e: e.iota(iot_a[:], [[1, 128]], base=0, channel_multiplier=0, allow_small_or_imprecise_dtypes=True), writes=[b_const])
        P.op("pool", lambda e: e.iota(iot_b[:], [[0, 128]], base=0, channel_multiplier=1, allow_small_or_imprecise_dtypes=True), writes=[b_const])
        P.op("pool", lambda e: e.iota(iota_f[:], [[1, 512]], base=0, channel_multiplier=0, allow_small_or_imprecise_dtypes=True), writes=[b_const])
        P.op("dve", lambda e: e.tensor_tensor(out=ident[:], in0=iot_a[:], in1=iot_b[:], op=ALU.is_equal), reads=[b_const], writes=[b_const])
        P.op("dve", lambda e: e.memset(ones_b[:], 1.0), writes=[b_const])
        P.op("dve", lambda e: e.memset(epst[:], EPS), writes=[b_const])
        P.op("dve", lambda e: e.memset(HST[:], 0.0), writes=[b_const])
        P.op("dve", lambda e: e.memset(ZST[:], 0.0), writes=[b_const])
        P.op("act", lambda e: e.activation(out=COEF[:], in_=COEF[:], func=AF.Exp, scale=-1.0), reads=[b_const], writes=[b_const])
        P.op("dve", lambda e: e.tensor_scalar(out=COEF[:], in0=COEF[:], scalar1=1.0, scalar2=None, op0=ALU.add), reads=[b_const], writes=[b_const])
        P.op("act", lambda e: e.activation(out=COEF[:], in_=COEF[:], func=AF.Ln), reads=[b_const], writes=[b_const])
        P.op("dve", lambda e: e.tensor_scalar(out=COEF2[:], in0=COEF[:], scalar1=-16.0, scalar2=None, op0=ALU.mult), reads=[b_const], writes=[b_const])
        P.op("dve", lambda e: e.tensor_scalar(out=COEF[:], in0=COEF[:], scalar1=-8.0, scalar2=None, op0=ALU.mult), reads=[b_const], writes=[b_const])

        S = lambda eng, fn: P.op(eng, fn, reads=[bS, b_const], writes=[bS])
        TS = lambda o, i, s1, s2, op0, op1=None: (lambda e: e.tensor_scalar(out=o, in0=i, scalar1=s1, scalar2=s2, op0=op0, op1=op1) if op1 is not None
                                                  else e.tensor_scalar(out=o, in0=i, scalar1=s1, scalar2=None, op0=op0))
        TTn = lambda o, a, b, op: (lambda e: e.tensor_tensor(out=o, in0=a, in1=b, op=op))
        ACTf = lambda o, i, f, **kw: (lambda e: e.activation(out=o, in_=i, func=f, **kw))

        def sincos(dst_c, dst_s, ang, tmp, tmp2):
            for dst, shift in ((dst_s, 0.0), (dst_c, math.pi / 2)):
                S("dve", TS(tmp2, ang, shift, None, ALU.add))
                S("dve", TS(tmp, tmp2, 1.0 / TWO_PI, MAGIC, ALU.mult, ALU.add))
                S("dve", TS(tmp, tmp, MAGIC, -TWO_PI, ALU.subtract, ALU.mult))
                S("dve", TTn(tmp, tmp, tmp2, ALU.add))
                S("dve", TS(tmp, tmp, -PI_SAFE, PI_SAFE, ALU.max, ALU.min))
                S("act", ACTf(dst, tmp, AF.Sin))

        S("act", ACTf(LDT, LDT, AF.Exp))
        S("dve", TTn(t0, LRE, LDT, ALU.mult))
        S("act", ACTf(RHO, t0, AF.Exp))
        S("dve", TTn(OM, LIM, LDT, ALU.mult))
        sincos(t0, t1, OM, t2, t3)
        S("dve", TTn(t0, t0, RHO, ALU.mult))
        S("dve", TTn(t1, t1, RHO, ALU.mult))
        S("dve", TS(t0, t0, -1.0, None, ALU.add))
        S("dve", TTn(t2, LRE, LRE, ALU.mult))
        S("dve", TTn(t3, LIM, LIM, ALU.mult))
        S("dve", TTn(t2, t2, t3, ALU.add))
        S("dve", lambda e: e.reciprocal(out=t2, in_=t2))
        S("dve", TTn(KR, t0, LRE, ALU.mult))
        S("dve", TTn(t3, t1, LIM, ALU.mult))
        S("dve", TTn(KR, KR, t3, ALU.add))
        S("dve", TTn(KR, KR, t2, ALU.mult))
        S("dve", TTn(KI, t1, LRE, ALU.mult))
        S("dve", TTn(t3, t0, LIM, ALU.mult))
        S("dve", TTn(KI, KI, t3, ALU.subtract))
        S("dve", TTn(KI, KI, t2, ALU.mult))
        for L, cl, sl, nsl in ((512.0, CL512, SL512, NSL512), (16.0, CL16, SL16, NSL16)):
            S("dve", TS(t0, OM, L, None, ALU.mult))
            sincos(cl[:], sl[:], t0, t2, t3)
            S("dve", TS(nsl[:], sl[:], -1.0, None, ALU.mult))
        kb = lambda k: k.unsqueeze(2).broadcast_to([128, NPR, 16])
        S("dve", TTn(BBR, BR0, kb(KR), ALU.mult))
        S("dve", TTn(TB, BI0, kb(KI), ALU.mult))
        S("dve", TTn(BBR, BBR, TB, ALU.subtract))
        S("dve", TTn(BBI, BI0, kb(KR), ALU.mult))
        S("dve", TTn(TB, BR0, kb(KI), ALU.mult))
        S("dve", TTn(BBI, BBI, TB, ALU.add))
        P.op("pool", lambda e: e.memset(LT, 0.0), writes=[bLT])
        for M_, BB_ in ((MRE, BBR), (MIM, BBI)):
            S("dve", lambda e, M_=M_: e.memset(M_, 0.0))
            S("dve", lambda e, M_=M_, BB_=BB_: e.tensor_copy(out=M_[0:64, :, 0:16], in_=BB_[0:64, :, :]))
            S("dve", lambda e, M_=M_, BB_=BB_: e.tensor_copy(out=M_[64:128, :, 16:32], in_=BB_[64:128, :, :]))
        for M_, dstT in ((MRE, BRE), (MIM, BIM)):
            for p0 in range(0, NPR, 4):
                np_ = min(4, NPR - p0)
                bk = bank("A")
                for j in range(np_):
                    P.op("pe", lambda e, bk=bk, j=j, M_=M_, p0=p0: e.transpose(out=PS[bk][0:32, j * 128:(j + 1) * 128], in_=M_[:, p0 + j, :], identity=ident[:]),
                         reads=[bS, b_const], writes=[PSB[bk]])
                P.op("act", lambda e, bk=bk, dstT=dstT, p0=p0, np_=np_: e.activation(
                    out=dstT[:, p0:p0 + np_, :], in_=PS[bk][0:32, 0:np_ * 128].rearrange("p (a b) -> p a b", a=np_), func=AF.Copy),
                    reads=[PSB[bk]], writes=[bLT])
        for CN_, CD_, dstC, sgn in ((CNR, CDR, CRE, 1.0), (CNI, CDI, CIN, -1.0)):
            S("dve", lambda e, CN_=CN_, CD_=CD_: e.tensor_copy(out=CD_[:, :, 0:64], in_=CN_))
            S("dve", lambda e, CN_=CN_, CD_=CD_: e.tensor_copy(out=CD_[:, :, 64:128], in_=CN_))
            for ct in range(NCT):
                bk = bank("A")
                P.op("pe", lambda e, bk=bk, CD_=CD_, ct=ct: e.transpose(out=PS[bk][:, 0:128], in_=CD_[:, ct, :], identity=ident[:]),
                     reads=[bS, b_const], writes=[PSB[bk]])
                for j4 in range(4):
                    pr = ct * 4 + j4
                    for g2 in range(2):
                        g8 = 2 * j4 + g2
                        P.op("act", lambda e, bk=bk, dstC=dstC, pr=pr, j4=j4, g2=g2, g8=g8, sgn=sgn: e.activation(
                            out=dstC[g2 * 64:(g2 + 1) * 64, pr, 32 * j4 + 16 * g2:32 * j4 + 16 * g2 + 16],
                            in_=PS[bk][g2 * 64:(g2 + 1) * 64, g8 * 16:(g8 + 1) * 16], func=AF.Copy, scale=sgn),
                            reads=[PSB[bk]], writes=[bLT])
        dL = Buf("LHD", dram=True)
        P.dma("sp", lambda e: e.dma_start(out=LHD.rearrange("pr p f c -> p pr f c"), in_=LT), None, reads=[bLT], writes=[dL])
        a_off_tab = a_off[0]
        tabs = Rot(carve([2, 512], F32, 2))
        (ANG, _), (AN2, _), (RED, _) = carve([512], F32, 3)
        bT = Buf("tabtmp")
        dT = Buf("TABD", dram=True)
        npi = sb("npi", [128, 1])
        P.op("dve", lambda e: e.memset(npi[:], -math.pi), writes=[b_const])
        for pr in range(NPR):
            tab, btab = tabs.next()
            P.op("dve", lambda e, pr=pr: e.tensor_scalar(out=ANG, in0=iota_f[:], scalar1=OM[:, pr:pr + 1], scalar2=None, op0=ALU.mult),
                 reads=[b_const, bS], writes=[bT])
            for idx, shift in ((1, 0.0), (0, math.pi / 2)):
                P.op("dve", TS(AN2, ANG, shift, None, ALU.add), reads=[bT], writes=[bT])
                P.op("dve", TS(RED, AN2, 1.0 / TWO_PI, MAGIC, ALU.mult, ALU.add), reads=[bT], writes=[bT])
                P.op("dve", TS(RED, RED, MAGIC, -TWO_PI, ALU.subtract, ALU.mult), reads=[bT], writes=[bT])
                P.op("dve", TTn(RED, RED, AN2, ALU.add), reads=[bT], writes=[bT])
                P.op("dve", TS(RED, RED, -PI_SAFE, PI_SAFE, ALU.max, ALU.min), reads=[bT], writes=[bT])
                P.op("act", lambda e, tab=tab, idx=idx: e.activation(out=tab[:, idx, :], in_=RED, func=AF.Sin), reads=[bT], writes=[btab])
            P.dma("sp", lambda e, tab=tab, pr=pr: e.dma_start(out=TABD[pr], in_=tab), None, reads=[btab], writes=[dT])
        P.barrier()

        a_off[0] = 0
        ring = carve([8192], BF16, 4)
        ringF = arena[:, 0:4 * 8192].bitcast(F32).rearrange("p (s d) -> p s d", s=4)
        ring_i = [0]
        (XN, bXN), = carve([NDT, W], BF16)
        NACT = max(NFH, cfg.get('ACT_PAD', 0))
        (ACT_, _), = carve([NACT, W], BF16)
        bACT = [Buf("act%d" % j) for j in range(NFH)]
        if NACT * W >= 2 * D:
            XPRE = ACT_.rearrange("p a b -> p (a b)")[:, 0:2 * D].bitcast(F32)
        else:
            (XPRE, _), = carve([D], F32)
        bXPRE = Buf("xpre")
        HS = Rot(carve([W], F32, 3)); OS = Rot(carve([W], F32, 3)); SG = Rot(carve([W], F32, 2)); SQ = Rot(carve([W], BF16, 2))
        (RSTD, bRSTD), = carve([W], F32)
        (RR, bRR), (RS_, bRS) = carve([512], F32, 2)
        T12 = Rot(carve([2, 512], F32, 2))
        stage13_end = a_off[0]

        def segs(t, with_pre=True):
            res = []
            if t == 0 and with_pre:
                res.append((0, PRE, 0, True))
            res.append((PRE, 512, PRE + 512 * t, False))
            return res

        def ring_next():
            i = ring_i[0] % 4
            ring_i[0] += 1
            return i

        def load_w(slot, off, src_ap, shape3):
            v, b = ring[slot]
            dst = v[:, off:off + shape3[0] * shape3[1]].rearrange("p (a b) -> p a b", a=shape3[0])
            P.dma("pool", lambda e, dst=dst, src_ap=src_ap: e.dma_start(out=dst, in_=src_ap), None, writes=[b])
            return dst

        def ssq_accum(o_ap, bo, sg, first, last, dve_sq=False):
            c0, n, _, is_pre = sg
            sq, bsq = SQ.next()
            P.op("act", lambda e, sq=sq, o_ap=o_ap: e.activation(out=sq[:, c0:c0 + n], in_=o_ap[:, c0:c0 + n], func=AF.Square), reads=[bo], writes=[bsq])
            bk = SSQP if is_pre else SSQM
            P.op("pe", lambda e, sq=sq, bk=bk: e.matmul(PS[bk][:, 0:n], lhsT=ones_b[:], rhs=sq[:, c0:c0 + n], start=first, stop=last),
                 reads=[bsq, b_const], writes=[PSB[bk]])

        def make_rstd(dst, bdst, sgl, dim, c_of=lambda sg: sg[0]):
            for sg in sgl:
                c0, n, _, is_pre = sg
                bk = SSQP if is_pre else SSQM
                cc = c_of(sg)
                P.op("act", lambda e, bk=bk, n=n, cc=cc: e.activation(out=dst[:, cc:cc + n], in_=PS[bk][:, 0:n], func=AF.Sqrt, bias=epst[:, 0:1], scale=1.0 / dim),
                     reads=[PSB[bk], b_const], writes=[bdst])
                P.op("dve", lambda e, n=n, cc=cc: e.reciprocal(out=dst[:, cc:cc + n], in_=dst[:, cc:cc + n]), reads=[bdst], writes=[bdst])

        hbuf = {}

        def hb(H, dt, t):
            k = (id(H), dt, t)
            if k not in hbuf:
                hbuf[k] = Buf("H", dram=True)
            return hbuf[k]

        def make_xn(Hsrc, G, t, sgl):
            for dt in range(NDT):
                hs, bhs = HS.next()
                for sg in sgl:
                    c0, n, d0, _ = sg
                    P.dma("sp", lambda e, hs=hs, dt=dt, c0=c0, n=n, d0=d0: e.dma_start(out=hs[:, c0:c0 + n], in_=Hsrc[dt * 128:(dt + 1) * 128, d0:d0 + n]),
                          None, reads=[hb(Hsrc, dt, t)], writes=[bhs])
                c0 = sgl[0][0]
                c1 = sgl[-1][0] + sgl[-1][1]
                P.op("dve", lambda e, hs=hs, dt=dt, c0=c0, c1=c1: e.scalar_tensor_tensor(
                    out=XN[:, dt, c0:c1], in0=hs[:, c0:c1], scalar=G[:, dt:dt + 1], in1=RSTD[:, c0:c1], op0=ALU.mult, op1=ALU.mult),
                    reads=[bhs, bRSTD, b_const], writes=[bXN])

        def ffn(t, sgl, Hsrc, Hdst, wg_d, wu_d, wd_d):
            for hf in range(2):
                for jj in range(NFH):
                    j = hf * NFH + jj
                    slot = ring_next()
                    wg = load_w(slot, 0, wg_d[:, j * 128:(j + 1) * 128].rearrange("(kt p) c -> p kt c", p=128), (NDT, 128))
                    wu = load_w(slot, NDT * 128, wu_d[:, j * 128:(j + 1) * 128].rearrange("(kt p) c -> p kt c", p=128), (NDT, 128))
                    bw = ring[slot][1]
                    accs = []
                    for w_ in (wg, wu):
                        bks = [bank("P") if sg[3] else bank("A") for sg in sgl]
                        for k in range(NDT):
                            for sg, bk in zip(sgl, bks):
                                c0, n = sg[0], sg[1]
                                P.op("pe", lambda e, bk=bk, w_=w_, k=k, c0=c0, n=n: e.matmul(PS[bk][:, 0:n], lhsT=w_[:, k, :], rhs=XN[:, k, c0:c0 + n], start=(k == 0), stop=(k == NDT - 1)),
                                     reads=[bw, bXN], writes=[PSB[bk]])
                        accs.append(bks)
                    sgt, bsg = SG.next()
                    for si, sg in enumerate(sgl):
                        c0, n = sg[0], sg[1]
                        bg, bu = accs[0][si], accs[1][si]
                        P.op("act", lambda e, sgt=sgt, bg=bg, c0=c0, n=n: e.activation(out=sgt[:, c0:c0 + n], in_=PS[bg][:, 0:n], func=AF.Silu), reads=[PSB[bg]], writes=[bsg])
                        P.op("dve", lambda e, sgt=sgt, bu=bu, jj=jj, c0=c0, n=n: e.tensor_tensor(out=ACT_[:, jj, c0:c0 + n], in0=sgt[:, c0:c0 + n], in1=PS[bu][:, 0:n], op=ALU.mult),
                             reads=[bsg, PSB[bu]], writes=[bACT[jj]])
                for m in range(NDT):
                    slot = ring_next()
                    wd = load_w(slot, 0, wd_d[hf * NFH * 128:(hf + 1) * NFH * 128, m * 128:(m + 1) * 128].rearrange("(ft p) c -> p ft c", p=128), (NFH, 128))
                    bw = ring[slot][1]
                    bks = [bank("P") if sg[3] else bank("A") for sg in sgl]
                    for ft in range(NFH):
                        for sg, bk in zip(sgl, bks):
                            c0, n = sg[0], sg[1]
                            P.op("pe", lambda e, bk=bk, wd=wd, ft=ft, c0=c0, n=n: e.matmul(PS[bk][:, 0:n], lhsT=wd[:, ft, :], rhs=ACT_[:, ft, c0:c0 + n], start=(ft == 0), stop=(ft == NFH - 1)),
                                 reads=[bw, bACT[ft]], writes=[PSB[bk]])
                    Hin = Hsrc if hf == 0 else Hdst
                    hs, bhs = HS.next()
                    o, bo = OS.next()
                    for sg, bk in zip(sgl, bks):
                        c0, n, d0, _ = sg
                        P.dma("sp", lambda e, hs=hs, m=m, c0=c0, n=n, d0=d0, Hin=Hin: e.dma_start(out=hs[:, c0:c0 + n], in_=Hin[m * 128:(m + 1) * 128, d0:d0 + n]),
                              None, reads=[hb(Hin, m, t)], writes=[bhs])
                        P.op("dve", lambda e, o=o, hs=hs, bk=bk, c0=c0, n=n: e.scalar_tensor_tensor(out=o[:, c0:c0 + n], in0=PS[bk][:, 0:n], scalar=0.5, in1=hs[:, c0:c0 + n], op0=ALU.mult, op1=ALU.add),
                             reads=[PSB[bk], bhs], writes=[bo])
                    for sg in sgl:
                        c0, n, d0, _ = sg
                        P.dma("sp", lambda e, o=o, m=m, c0=c0, n=n, d0=d0: e.dma_start(out=Hdst[m * 128:(m + 1) * 128, d0:d0 + n], in_=o[:, c0:c0 + n]),
                              None, reads=[bo], writes=[hb(Hdst, m, t)])
                        if hf == 1:
                            ssq_accum(o, bo, sg, m == 0, m == NDT - 1)

        def _stage1_tile(t):
            sgl = segs(t)
            for blk in range(4):
                v, b = ring[blk]
                r0 = PRE + 512 * t + 128 * blk
                P.dma("sp", lambda e, blk=blk, r0=r0: e.dma_start(out=ringF[:, blk, 0:D], in_=x_d[r0:r0 + 128, :]), None, writes=[b])
            if t == 0:
                P.dma("sp", lambda e: e.dma_start(out=XPRE[0:PRE, :], in_=x_d[0:PRE, :]), None, writes=[bXPRE] + bACT)
            for dt in range(NDT):
                bk = bank("A")
                for blk in range(4):
                    P.op("pe", lambda e, bk=bk, blk=blk, dt=dt: e.transpose(out=PS[bk][:, blk * 128:(blk + 1) * 128], in_=ringF[:, blk, dt * 128:(dt + 1) * 128], identity=ident[:]),
                         reads=[ring[blk][1], b_const], writes=[PSB[bk]])
                o, bo = OS.next()
                P.op("act", lambda e, o=o, bk=bk: e.activation(out=o[:, PRE:PRE + 512], in_=PS[bk][:, 0:512], func=AF.Copy), reads=[PSB[bk]], writes=[bo])
                if t == 0:
                    bp = bank("P")
                    P.op("pe", lambda e, bp=bp, dt=dt: e.transpose(out=PS[bp][:, 0:PRE], in_=XPRE[0:PRE, dt * 128:(dt + 1) * 128], identity=ident[0:PRE, 0:PRE]),
                         reads=[bXPRE, b_const], writes=[PSB[bp]])
                    P.op("act", lambda e, o=o, bp=bp: e.activation(out=o[:, 0:PRE], in_=PS[bp][:, 0:PRE], func=AF.Copy), reads=[PSB[bp]], writes=[bo])
                for sg in sgl:
                    c0, n, d0, _ = sg
                    P.dma("sp", lambda e, o=o, dt=dt, c0=c0, n=n, d0=d0: e.dma_start(out=H0[dt * 128:(dt + 1) * 128, d0:d0 + n], in_=o[:, c0:c0 + n]),
                          None, reads=[bo], writes=[hb(H0, dt, t)])
                    ssq_accum(o, bo, sg, dt == 0, dt == NDT - 1)
            if cfg.get("DEBUG") and t == 0:
                (dtmp, bdtmp), = carve([W], F32)
                P.op("act", lambda e: e.activation(out=dtmp[:, 0:16], in_=PS[SSQP][:, 0:16], func=AF.Copy), reads=[PSB[SSQP]], writes=[bdtmp])
                P.op("act", lambda e: e.activation(out=dtmp[:, 16:528], in_=PS[SSQM][:, 0:512], func=AF.Copy), reads=[PSB[SSQM]], writes=[bdtmp])
                P.dma("sp", lambda e: e.dma_start(out=DBG[:, 1, :], in_=dtmp), None, reads=[bdtmp], writes=[Buf("dbg", dram=True)])
            make_rstd(RSTD, bRSTD, sgl, D)
            if cfg.get("DEBUG") and t == 0:
                P.dma("sp", lambda e: e.dma_start(out=DBG[:, 0, :], in_=RSTD), None, reads=[bRSTD], writes=[Buf("dbg", dram=True)])
            make_xn(H0, G1, t, sgl)
            ffn(t, sgl, H0, H1, w1g_d, w1u_d, w1d_d)
            make_rstd(RSTD, bRSTD, sgl, D)
            make_xn(H1, GM, t, sgl)
            for c in range(3 * NCT):
                slot = ring_next()
                wi = load_w(slot, 0, win_d[:, c * 128:(c + 1) * 128].rearrange("(kt p) c -> p kt c", p=128), (NDT, 128))
                bw = ring[slot][1]
                bks = [bank("P") if sg[3] else bank("A") for sg in sgl]
                for k in range(NDT):
                    for sg, bk in zip(sgl, bks):
                        c0, n = sg[0], sg[1]
                        P.op("pe", lambda e, bk=bk, wi=wi, k=k, c0=c0, n=n: e.matmul(PS[bk][:, 0:n], lhsT=wi[:, k, :], rhs=XN[:, k, c0:c0 + n], start=(k == 0), stop=(k == NDT - 1)),
                             reads=[bw, bXN], writes=[PSB[bk]])
                o, bo = OS.next()
                for sg, bk in zip(sgl, bks):
                    c0, n, d0, _ = sg
                    P.op("act", lambda e, o=o, bk=bk, c0=c0, n=n: e.activation(out=o[:, c0:c0 + n], in_=PS[bk][:, 0:n], func=AF.Copy), reads=[PSB[bk]], writes=[bo])
                    P.dma("sp", lambda e, o=o, c=c, c0=c0, n=n, d0=d0: e.dma_start(out=PROJ[c * 128:(c + 1) * 128, d0:d0 + n], in_=o[:, c0:c0 + n]),
                          None, reads=[bo], writes=[hb(PROJ, c, t)])
        for t in range(NTL if cfg.get("STOP", 9) >= 1 else 0):
            _stage1_tile(t)
        P.barrier()

        a_off[0] = 0
        U2 = Rot(carve([2, 3 + W], F32, 2)); XC = Rot(carve([2, W], F32, 2)); XCB = Rot(carve([2, W], BF16, 2))
        RT = Rot(carve([W], F32, 2)); AT = Rot(carve([W], F32, 2)); MT = Rot(carve([W], F32, 2)); IT = Rot(carve([W], F32, 2))
        GT = Rot(carve([W], F32, 2)); HT = Rot(carve([W], F32, 2)); YO = Rot(carve([W], F32, 2)); YB = Rot(carve([512], BF16, 3))
        SQ2 = Rot(carve([512], BF16, 2))
        TABS = Rot(carve([2, 512], F32, 4)); UB = Rot(carve([W], BF16, 3)); LH = Rot(carve([4, 128], BF16, 4))
        TA = Rot(carve([W], F32, 4)); TBb = Rot(carve([W], F32, 4)); ZRI = Rot(carve([W], F32, 2)); ZII = Rot(carve([W], F32, 2))
        ZR = Rot(carve([W], F32, 4)); ZI = Rot(carve([W], F32, 4)); PA = Rot(carve([W], F32, 4)); PB = Rot(carve([W], F32, 4))
        XR = Rot(carve([W], BF16, 2)); XI = Rot(carve([W], BF16, 2))
        UF = Rot(carve([W], F32, 2)); YT = Rot(carve([W], F32, 2))
        (ZZ, _), = carve([NCT, 512], BF16)
        bZZ = [Buf("zz%d" % c) for c in range(NCT)]
        GLW = Rot(carve([NCT, 128], BF16, 2))
        GTT = Rot(carve([512], F32, 2)); YS = Rot(carve([512], F32, 2))
        (RO, bRO), = carve([512], F32)
        tiny = sb("tiny", [128, 8])
        btiny2 = [Buf("tiny0"), Buf("tiny1")]
        bH = Buf("hst"); bZ = [Buf("zst%d" % p) for p in range(NPR)]

        def _stage2_tile(t):
            sgl = segs(t)
            d_lo = 0 if t == 0 else PRE + 512 * t
            c_lo = 0 if t == 0 else PRE
            ncol = (W - c_lo)
            main_d0 = PRE + 512 * t
            for hd in range(NHD):
                u2, bu2 = U2.next()
                xc, bxc = XC.next()
                xcb, bxcb = XCB.next()
                for it in range(2):
                    r0 = hd * 256 + it * 128
                    if t == 0:
                        P.op("pool", lambda e, u2=u2, it=it: e.memset(u2[:, it, 0:3], 0.0), writes=[bu2])
                        P.dma("sp", lambda e, u2=u2, it=it, r0=r0: e.dma_start(out=u2[:, it, 3:3 + W], in_=PROJ[r0:r0 + 128, 0:W]), None,
                              reads=[hb(PROJ, r0 // 128, 0)], writes=[bu2])
                    else:
                        P.dma("sp", lambda e, u2=u2, it=it, r0=r0: e.dma_start(out=u2[:, it, PRE:3 + W], in_=PROJ[r0:r0 + 128, main_d0 - 3:main_d0 + 512]), None,
                              reads=[hb(PROJ, r0 // 128, t), hb(PROJ, r0 // 128, t - 1)], writes=[bu2])
                    ct = hd * 2 + it
                    P.op("dve", lambda e, u2=u2, xc=xc, it=it, ct=ct: e.tensor_scalar(out=xc[:, it, c_lo:W], in0=u2[:, it, 3 + c_lo:3 + W], scalar1=CW[:, 3, ct:ct + 1], scalar2=CB[:, ct:ct + 1], op0=ALU.mult, op1=ALU.add),
                         reads=[bu2, b_const], writes=[bxc])
                    for k in range(3):
                        P.op("dve", lambda e, u2=u2, xc=xc, it=it, ct=ct, k=k: e.scalar_tensor_tensor(out=xc[:, it, c_lo:W], in0=u2[:, it, k + c_lo:k + W], scalar=CW[:, k, ct:ct + 1], in1=xc[:, it, c_lo:W], op0=ALU.mult, op1=ALU.add),
                             reads=[bu2, bxc, b_const], writes=[bxc])
                    P.op("act", lambda e, xc=xc, xcb=xcb, it=it: e.activation(out=xcb[:, it, c_lo:W], in_=xc[:, it, c_lo:W], func=AF.Copy), reads=[bxc], writes=[bxcb])
                for jt in range(2):
                    ct = hd * 2 + jt
                    rt, brt = RT.next(); at, bat = AT.next(); mt, bmt = MT.next(); itt, bit = IT.next()
                    gt, bgt = GT.next(); ht, bht = HT.next(); yo, byo = YO.next()
                    P.dma("sp", lambda e, gt=gt, ct=ct: e.dma_start(out=gt[:, c_lo:W], in_=PROJ[DRG + ct * 128:DRG + (ct + 1) * 128, d_lo:d_lo + ncol]), None,
                          reads=[hb(PROJ, NCT + ct, t)], writes=[bgt])
                    for sg in sgl:
                        c0, n, _, is_pre = sg
                        ba_, bx_ = (bank("P"), bank("P")) if is_pre else (bank("A"), bank("A"))
                        for Wm, bk in ((WA, ba_), (WX, bx_)):
                            for it in range(2):
                                P.op("pe", lambda e, Wm=Wm, bk=bk, it=it, jt=jt, hd=hd, xcb=xcb, c0=c0, n=n: e.matmul(PS[bk][:, 0:n], lhsT=Wm[:, hd, it, jt * 128:(jt + 1) * 128], rhs=xcb[:, it, c0:c0 + n], start=(it == 0), stop=(it == 1)),
                                     reads=[bxcb, b_const], writes=[PSB[bk]])
                        P.op("act", lambda e, rt=rt, ba_=ba_, ct=ct, c0=c0, n=n: e.activation(out=rt[:, c0:c0 + n], in_=PS[ba_][:, 0:n], func=AF.Sigmoid, bias=BA[:, ct:ct + 1], scale=1.0), reads=[PSB[ba_], b_const], writes=[brt])
                        P.op("act", lambda e, itt=itt, bx_=bx_, ct=ct, c0=c0, n=n: e.activation(out=itt[:, c0:c0 + n], in_=PS[bx_][:, 0:n], func=AF.Sigmoid, bias=BX[:, ct:ct + 1], scale=1.0), reads=[PSB[bx_], b_const], writes=[bit])
                    P.op("act", lambda e, at=at, rt=rt, ct=ct: e.activation(out=at[:, c_lo:W], in_=rt[:, c_lo:W], func=AF.Exp, scale=COEF[:, ct:ct + 1]), reads=[brt, b_const], writes=[bat])
                    P.op("act", lambda e, mt=mt, rt=rt, ct=ct: e.activation(out=mt[:, c_lo:W], in_=rt[:, c_lo:W], func=AF.Exp, scale=COEF2[:, ct:ct + 1]), reads=[brt, b_const], writes=[bmt])
                    P.op("dve", lambda e, mt=mt: e.tensor_scalar(out=mt[:, c_lo:W], in0=mt[:, c_lo:W], scalar1=-1.0, scalar2=1.0, op0=ALU.mult, op1=ALU.add), reads=[bmt], writes=[bmt])
                    P.op("act", lambda e, mt=mt: e.activation(out=mt[:, c_lo:W], in_=mt[:, c_lo:W], func=AF.Sqrt), reads=[bmt], writes=[bmt])
                    P.op("dve", lambda e, mt=mt, itt=itt: e.tensor_tensor(out=mt[:, c_lo:W], in0=mt[:, c_lo:W], in1=itt[:, c_lo:W], op=ALU.mult), reads=[bmt, bit], writes=[bmt])
                    P.op("dve", lambda e, mt=mt, xc=xc, jt=jt: e.tensor_tensor(out=mt[:, c_lo:W], in0=mt[:, c_lo:W], in1=xc[:, jt, c_lo:W], op=ALU.mult), reads=[bmt, bxc], writes=[bmt])
                    for sg in sgl:
                        c0, n, _, is_pre = sg
                        if is_pre:
                            P.op("dve", lambda e, ht=ht, at=at, mt=mt, c0=c0, n=n: e.tensor_tensor_scan(out=ht[:, c0:c0 + n], data0=at[:, c0:c0 + n], data1=mt[:, c0:c0 + n], initial=0.0, op0=ALU.mult, op1=ALU.add),
                                 reads=[bat, bmt], writes=[bht])
                        else:
                            init = ht[:, PRE - 1:PRE] if t == 0 else HST[:, ct:ct + 1]
                            P.op("dve", lambda e, ht=ht, at=at, mt=mt, c0=c0, n=n, init=init: e.tensor_tensor_scan(out=ht[:, c0:c0 + n], data0=at[:, c0:c0 + n], data1=mt[:, c0:c0 + n], initial=init, op0=ALU.mult, op1=ALU.add),
                                 reads=[bat, bmt, bH, bht], writes=[bht])
                    P.op("act", lambda e, ht=ht, ct=ct: e.activation(out=HST[:, ct:ct + 1], in_=ht[:, W - 1:W], func=AF.Copy), reads=[bht], writes=[bH])
                    P.op("act", lambda e, gt=gt: e.activation(out=gt[:, PRE:W], in_=gt[:, PRE:W], func=AF.Gelu_apprx_tanh), reads=[bgt], writes=[bgt])
                    P.op("dve", lambda e, yo=yo, ht=ht, gt=gt: e.tensor_tensor(out=yo[:, PRE:W], in0=ht[:, PRE:W], in1=gt[:, PRE:W], op=ALU.mult), reads=[bht, bgt], writes=[byo])
                    sq, bsq = SQ2.next()
                    P.op("act", lambda e, sq=sq, yo=yo: e.activation(out=sq, in_=yo[:, PRE:W], func=AF.Square), reads=[byo], writes=[bsq])
                    P.op("pe", lambda e, sq=sq, ct=ct: e.matmul(PS[SSQM][:, 0:512], lhsT=ones_b[:], rhs=sq, start=(ct == 0), stop=(ct == NCT - 1)), reads=[bsq, b_const], writes=[PSB[SSQM]])
                    yb, byb = YB.next()
                    P.op("act", lambda e, yb=yb, yo=yo, ct=ct: e.activation(out=yb, in_=yo[:, PRE:W], func=AF.Copy, scale=GRO[:, ct:ct + 1]), reads=[byo, b_const], writes=[byb])
                    P.dma("sp", lambda e, yb=yb, ct=ct: e.dma_start(out=YH[ct * 128:(ct + 1) * 128, 512 * t:512 * (t + 1)], in_=yb), None, reads=[byb], writes=[hb(YH, ct, t)])
            make_rstd(RO, bRO, [sgl[-1]], DRG, c_of=lambda sg: 0)
            P.dma("sp", lambda e: e.dma_start(out=RSD[0, :, 512 * t:512 * (t + 1)], in_=RO), None, reads=[bRO], writes=[hb(RSD, 0, t)])

            for ct in range(NCT):
                ybk = BT
                caps = []
                rot["A"] = [0, 1, 2, SSQM]
                for j4 in range(4):
                    if j4 % 2 == 0:
                        rot_i["A"] = 0
                        rot_i["P"] = 0
                    pr = ct * 4 + j4
                    tc0 = 2 * (j4 % 2)
                    btiny = btiny2[j4 % 2]
                    P.begin_capture()
                    tab, btab = TABS.next()
                    ub, bub = UB.next()
                    P.dma("sp", lambda e, tab=tab, pr=pr: e.dma_start(out=tab, in_=TABD[pr]), None, reads=[dT], writes=[btab])
                    lh, blh = LH.next()
                    P.dma("sp", lambda e, lh=lh, pr=pr: e.dma_start(out=lh, in_=LHD[pr]), None, reads=[dL], writes=[blh])
                    ur0 = 2 * DRG + 32 * pr
                    P.dma("pool", lambda e, ub=ub, ur0=ur0: e.dma_start(out=ub[0:32, c_lo:W], in_=PROJ[ur0:ur0 + 32, d_lo:d_lo + ncol]), None,
                          reads=[hb(PROJ, ur0 // 128, t)], writes=[bub])
                    ta, bta = TA.next(); tb, btb = TBb.next(); zri, bzri = ZRI.next(); zii, bzii = ZII.next()
                    zr, bzr = ZR.next(); zi, bzi = ZI.next(); pa, bpa = PA.next(); pb, bpb = PB.next()
                    xr, bxr = XR.next(); xi, bxi = XI.next()
                    for sg in sgl:
                        c0, n, _, is_pre = sg
                        if is_pre:
                            bkr = bki = bank("P")
                            oki = PRE
                        else:
                            bkr, bki = bank("A"), bank("A")
                            oki = 0
                        P.op("pe", lambda e, bkr=bkr, lh=lh, ub=ub, c0=c0, n=n: e.matmul(PS[bkr][:, 0:n], lhsT=lh[0:32, 2, :], rhs=ub[0:32, c0:c0 + n], start=True, stop=True), reads=[bub, blh], writes=[PSB[bkr]])
                        P.op("pe", lambda e, bki=bki, lh=lh, ub=ub, c0=c0, n=n, oki=oki: e.matmul(PS[bki][:, oki:oki + n], lhsT=lh[0:32, 3, :], rhs=ub[0:32, c0:c0 + n], start=True, stop=True), reads=[bub, blh], writes=[PSB[bki]])
                        Ct = tab[:, 0, 0:n]
                        St = tab[:, 1, 0:n]
                        cs = slice(c0, c0 + n)
                        P.op("dve", lambda e, ta=ta, bkr=bkr, Ct=Ct, cs=cs, n=n: e.tensor_tensor(out=ta[:, cs], in0=PS[bkr][:, 0:n], in1=Ct, op=ALU.mult), reads=[PSB[bkr], btab], writes=[bta])
                        P.op("dve", lambda e, tb=tb, bki=bki, St=St, cs=cs, n=n, oki=oki: e.tensor_tensor(out=tb[:, cs], in0=PS[bki][:, oki:oki + n], in1=St, op=ALU.mult), reads=[PSB[bki], btab], writes=[btb])
                        P.op("dve", lambda e, zri=zri, ta=ta, tb=tb, cs=cs: e.tensor_tensor(out=zri[:, cs], in0=ta[:, cs], in1=tb[:, cs], op=ALU.add), reads=[bta, btb], writes=[bzri])
                        ta2, bta2 = TA.next(); tb2, btb2 = TBb.next()
                        P.op("dve", lambda e, ta2=ta2, bki=bki, Ct=Ct, cs=cs, n=n, oki=oki: e.tensor_tensor(out=ta2[:, cs], in0=PS[bki][:, oki:oki + n], in1=Ct, op=ALU.mult), reads=[PSB[bki], btab], writes=[bta2])
                        P.op("dve", lambda e, tb2=tb2, bkr=bkr, St=St, cs=cs, n=n: e.tensor_tensor(out=tb2[:, cs], in0=PS[bkr][:, 0:n], in1=St, op=ALU.mult), reads=[PSB[bkr], btab], writes=[btb2])
                        P.op("dve", lambda e, zii=zii, ta2=ta2, tb2=tb2, cs=cs: e.tensor_tensor(out=zii[:, cs], in0=ta2[:, cs], in1=tb2[:, cs], op=ALU.subtract), reads=[bta2, btb2], writes=[bzii])
                        if is_pre:
                            ir, ii_ = 0.0, 0.0
                            rd = [bzri]
                        else:
                            ir, ii_ = ZST[:, pr, 0:1], ZST[:, pr, 1:2]
                            rd = [bzri, bZ[pr]]
                        rho_b = RHO[:, pr:pr + 1].broadcast_to([128, n])
                        P.op("dve", lambda e, zr=zr, zri=zri, cs=cs, ir=ir, rho_b=rho_b: e.tensor_tensor_scan(out=zr[:, cs], data0=rho_b, data1=zri[:, cs], initial=ir, op0=ALU.mult, op1=ALU.add), reads=rd + [b_const], writes=[bzr])
                        rd2 = [bzii] + rd[1:]
                        P.op("dve", lambda e, zi=zi, zii=zii, cs=cs, ii_=ii_, rho_b=rho_b: e.tensor_tensor_scan(out=zi[:, cs], data0=rho_b, data1=zii[:, cs], initial=ii_, op0=ALU.mult, op1=ALU.add), reads=rd2 + [b_const], writes=[bzi])
                        cl, sl, nsl = (CL16, SL16, NSL16) if is_pre else (CL512, SL512, NSL512)
                        e0 = c0 + n - 1
                        P.op("dve", lambda e, zr=zr, e0=e0, cl=cl, pr=pr, tc0=tc0: e.tensor_tensor(out=tiny[:, tc0:tc0 + 1], in0=zr[:, e0:e0 + 1], in1=cl[:, pr:pr + 1], op=ALU.mult), reads=[bzr, b_const], writes=[btiny])
                        P.op("dve", lambda e, zr=zr, e0=e0, sl=sl, pr=pr, tc0=tc0: e.tensor_tensor(out=tiny[:, tc0 + 1:tc0 + 2], in0=zr[:, e0:e0 + 1], in1=sl[:, pr:pr + 1], op=ALU.mult), reads=[bzr, b_const], writes=[btiny])
                        P.op("dve", lambda e, zi=zi, e0=e0, nsl=nsl, pr=pr, tc0=tc0: e.scalar_tensor_tensor(out=ZST[:, pr, 0:1], in0=zi[:, e0:e0 + 1], scalar=nsl[:, pr:pr + 1], in1=tiny[:, tc0:tc0 + 1], op0=ALU.mult, op1=ALU.add), reads=[bzi, btiny, b_const], writes=[bZ[pr]])
                        P.op("dve", lambda e, zi=zi, e0=e0, cl=cl, pr=pr, tc0=tc0: e.scalar_tensor_tensor(out=ZST[:, pr, 1:2], in0=zi[:, e0:e0 + 1], scalar=cl[:, pr:pr + 1], in1=tiny[:, tc0 + 1:tc0 + 2], op0=ALU.mult, op1=ALU.add), reads=[bzi, btiny, b_const], writes=[bZ[pr]])
                        if is_pre:
                            continue
                        eo = "pool" if j4 % 2 == 1 else "dve"
                        P.op(eo, lambda e, pa=pa, zr=zr, Ct=Ct, cs=cs: e.tensor_tensor(out=pa[:, cs], in0=zr[:, cs], in1=Ct, op=ALU.mult), reads=[bzr, btab], writes=[bpa])
                        P.op(eo, lambda e, pb=pb, zi=zi, St=St, cs=cs: e.tensor_tensor(out=pb[:, cs], in0=zi[:, cs], in1=St, op=ALU.mult), reads=[bzi, btab], writes=[bpb])
                        P.op(eo, lambda e, xr=xr, pa=pa, pb=pb, cs=cs: e.tensor_tensor(out=xr[:, cs], in0=pa[:, cs], in1=pb[:, cs], op=ALU.subtract), reads=[bpa, bpb], writes=[bxr])
                        pa2, bpa2 = PA.next(); pb2, bpb2 = PB.next()
                        P.op(eo, lambda e, pa2=pa2, zr=zr, St=St, cs=cs: e.tensor_tensor(out=pa2[:, cs], in0=zr[:, cs], in1=St, op=ALU.mult), reads=[bzr, btab], writes=[bpa2])
                        P.op(eo, lambda e, pb2=pb2, zi=zi, Ct=Ct, cs=cs: e.tensor_tensor(out=pb2[:, cs], in0=zi[:, cs], in1=Ct, op=ALU.mult), reads=[bzi, btab], writes=[bpb2])
                        P.op(eo, lambda e, xi=xi, pa2=pa2, pb2=pb2, cs=cs: e.tensor_tensor(out=xi[:, cs], in0=pa2[:, cs], in1=pb2[:, cs], op=ALU.add), reads=[bpa2, bpb2], writes=[bxi])
                        P.op("pe", lambda e, ybk=ybk, lh=lh, xr=xr, cs=cs, j4=j4: e.matmul(PS[ybk][:, 0:512], lhsT=lh[:, 0, :], rhs=xr[:, cs], start=(j4 == 0), stop=False), reads=[bxr, blh], writes=[PSB[ybk]])
                        P.op("pe", lambda e, ybk=ybk, lh=lh, xi=xi, cs=cs, j4=j4: e.matmul(PS[ybk][:, 0:512], lhsT=lh[:, 1, :], rhs=xi[:, cs], start=False, stop=(j4 == 3)), reads=[bxi, blh], writes=[PSB[ybk]])
                    caps.append(P.end_capture())
                P.replay([caps[0], caps[1]])
                P.replay([caps[2], caps[3]])
                rot["A"] = [0, 1, 2]
                uf, buf_ = UF.next()
                yt, byt = YT.next()
                ur = 2 * DRG + ct * 128
                P.dma("sp", lambda e, uf=uf, ur=ur: e.dma_start(out=uf[:, 0:512], in_=PROJ[ur:ur + 128, main_d0:main_d0 + 512]), None, reads=[hb(PROJ, ur // 128, t)], writes=[buf_])
                P.op("dve", lambda e, yt=yt, uf=uf, ybk=ybk, ct=ct: e.scalar_tensor_tensor(out=yt[:, 0:512], in0=uf[:, 0:512], scalar=SD[:, ct:ct + 1], in1=PS[ybk][:, 0:512], op0=ALU.mult, op1=ALU.add),
                     reads=[buf_, PSB[ybk], b_const], writes=[byt])
                P.op("act", lambda e, yt=yt, ct=ct: e.activation(out=ZZ[:, ct, :], in_=yt[:, 0:512], func=AF.Gelu_apprx_tanh), reads=[byt], writes=[bZZ[ct]])
            for cp in range(NCT):
                glw, bglw = GLW.next()
                P.dma("pool", lambda e, glw=glw, cp=cp: e.dma_start(out=glw, in_=gw_d[:, cp * 128:(cp + 1) * 128].rearrange("(kt p) c -> p kt c", p=128)), None, writes=[bglw])
                bk = bank("A")
                for ct in range(NCT):
                    P.op("pe", lambda e, bk=bk, glw=glw, ct=ct: e.matmul(PS[bk][:, 0:512], lhsT=glw[:, ct, :], rhs=ZZ[:, ct, :], start=(ct == 0), stop=(ct == NCT - 1)), reads=[bglw, bZZ[ct]], writes=[PSB[bk]])
                gtt, bgtt = GTT.next(); ys, bys = YS.next()
                P.op("act", lambda e, gtt=gtt, bk=bk, cp=cp: e.activation(out=gtt, in_=PS[bk][:, 0:512], func=AF.Sigmoid, bias=GB[:, cp:cp + 1], scale=1.0), reads=[PSB[bk], b_const], writes=[bgtt])
                P.op("dve", lambda e, ys=ys, gtt=gtt, cp=cp: e.tensor_tensor(out=ys, in0=ZZ[:, cp, :], in1=gtt, op=ALU.mult), reads=[bgtt, bZZ[cp]], writes=[bys])
                sq, bsq = SQ2.next()
                P.op("act", lambda e, sq=sq, ys=ys: e.activation(out=sq, in_=ys, func=AF.Square), reads=[bys], writes=[bsq])
                P.op("pe", lambda e, sq=sq, cp=cp: e.matmul(PS[SSQP][:, 0:512], lhsT=ones_b[:], rhs=sq, start=(cp == 0), stop=(cp == NCT - 1)), reads=[bsq, b_const], writes=[PSB[SSQP]])
                yb, byb = YB.next()
                P.op("act", lambda e, yb=yb, ys=ys, cp=cp: e.activation(out=yb, in_=ys, func=AF.Copy, scale=GSO[:, cp:cp + 1]), reads=[bys, b_const], writes=[byb])
                P.dma("sp", lambda e, yb=yb, cp=cp: e.dma_start(out=YH[DRG + cp * 128:DRG + (cp + 1) * 128, 512 * t:512 * (t + 1)], in_=yb), None, reads=[byb], writes=[hb(YH, NCT + cp, t)])
            P.op("act", lambda e: e.activation(out=RO, in_=PS[SSQP][:, 0:512], func=AF.Sqrt, bias=epst[:, 0:1], scale=1.0 / DS5), reads=[PSB[SSQP], b_const], writes=[bRO])
            P.op("dve", lambda e: e.reciprocal(out=RO, in_=RO), reads=[bRO], writes=[bRO])
            P.dma("sp", lambda e: e.dma_start(out=RSD[1, :, 512 * t:512 * (t + 1)], in_=RO), None, reads=[bRO], writes=[hb(RSD, 1, t)])
        rot["A"] = [0, 1, 2]
        for t in range(NTL if cfg.get("STOP", 9) >= 2 else 0):
            _stage2_tile(t)
        rot["A"] = [0, 1, 2, 7]
        P.barrier()

        def _stage3_tile(t):
            sgl = segs(t, with_pre=False)
            c0, n, d0, _ = sgl[0]
            P.dma("sp", lambda e: e.dma_start(out=XN[:, :, c0:c0 + 512], in_=YH[:, 512 * t:512 * (t + 1)].rearrange("(k p) c -> p k c", p=128)), None,
                  reads=[hb(YH, k, t) for k in range(NDT)], writes=[bXN])
            P.dma("sp", lambda e: e.dma_start(out=RR, in_=RSD[0, :, 512 * t:512 * (t + 1)]), None, reads=[hb(RSD, 0, t)], writes=[bRR])
            P.dma("sp", lambda e: e.dma_start(out=RS_, in_=RSD[1, :, 512 * t:512 * (t + 1)]), None, reads=[hb(RSD, 1, t)], writes=[bRS])
            for m in range(NDT):
                slot = ring_next()
                wo = load_w(slot, 0, wout_d[:, m * 128:(m + 1) * 128].rearrange("(kt p) c -> p kt c", p=128), (NDT, 128))
                bw = ring[slot][1]
                b1, b2 = bank("A"), bank("A")
                for k in range(NDT):
                    bk = b1 if k < NCT else b2
                    kk = k if k < NCT else k - NCT
                    P.op("pe", lambda e, bk=bk, wo=wo, k=k, kk=kk: e.matmul(PS[bk][:, 0:512], lhsT=wo[:, k, :], rhs=XN[:, k, c0:c0 + 512], start=(kk == 0), stop=(kk == NCT - 1)), reads=[bw, bXN], writes=[PSB[bk]])
                t12, bt12 = T12.next()
                hs, bhs = HS.next()
                o, bo = OS.next()
                P.dma("sp", lambda e, hs=hs, m=m: e.dma_start(out=hs[:, c0:c0 + 512], in_=H1[m * 128:(m + 1) * 128, d0:d0 + 512]), None, reads=[hb(H1, m, t)], writes=[bhs])
                P.op("dve", lambda e, t12=t12, b1=b1: e.tensor_tensor(out=t12[:, 0, :], in0=PS[b1][:, 0:512], in1=RR, op=ALU.mult), reads=[PSB[b1], bRR], writes=[bt12])
                P.op("dve", lambda e, t12=t12, b2=b2: e.tensor_tensor(out=t12[:, 1, :], in0=PS[b2][:, 0:512], in1=RS_, op=ALU.mult), reads=[PSB[b2], bRS], writes=[bt12])
                P.op("dve", lambda e, t12=t12: e.tensor_tensor(out=t12[:, 0, :], in0=t12[:, 0, :], in1=t12[:, 1, :], op=ALU.add), reads=[bt12], writes=[bt12])
                P.op("dve", lambda e, t12=t12, o=o, hs=hs: e.tensor_tensor(out=o[:, c0:c0 + 512], in0=t12[:, 0, :], in1=hs[:, c0:c0 + 512], op=ALU.add), reads=[bt12, bhs], writes=[bo])
                P.dma("sp", lambda e, o=o, m=m: e.dma_start(out=H2[m * 128:(m + 1) * 128, d0:d0 + 512], in_=o[:, c0:c0 + 512]), None, reads=[bo], writes=[hb(H2, m, t)])
                ssq_accum(o, bo, sgl[0], m == 0, m == NDT - 1)
            if cfg.get("STOP3", 9) < 1:
                return
            make_rstd(RSTD, bRSTD, sgl, D)
            make_xn(H2, G2, t, sgl)
            ffn(t, sgl, H2, H3, w2g_d, w2u_d, w2d_d)
            make_rstd(RSTD, bRSTD, sgl, D)
            if cfg.get("STOP3", 9) < 2:
                return
            for dt in range(NDT):
                hs, bhs = HS.next()
                o, bo = OS.next()
                P.dma("sp", lambda e, hs=hs, dt=dt: e.dma_start(out=hs[:, c0:c0 + 512], in_=H3[dt * 128:(dt + 1) * 128, d0:d0 + 512]), None, reads=[hb(H3, dt, t)], writes=[bhs])
                P.op("dve", lambda e, o=o, hs=hs, dt=dt: e.scalar_tensor_tensor(out=o[:, c0:c0 + 512], in0=hs[:, c0:c0 + 512], scalar=GF[:, dt:dt + 1], in1=RSTD[:, c0:c0 + 512], op0=ALU.mult, op1=ALU.mult),
                     reads=[bhs, bRSTD, b_const], writes=[bo])
                bk = bank("A")
                for blk in range(4):
                    P.op("pe", lambda e, bk=bk, blk=blk, o=o: e.transpose(out=PS[bk][:, blk * 128:(blk + 1) * 128], in_=o[:, c0 + blk * 128:c0 + (blk + 1) * 128], identity=ident[:]),
                         reads=[bo, b_const], writes=[PSB[bk]])
                P.op("act", lambda e, bk=bk, dt=dt: e.activation(out=ringF[:, :, dt * 128:(dt + 1) * 128], in_=PS[bk][:, 0:512].rearrange("p (a b) -> p a b", a=4), func=AF.Copy),
                     reads=[PSB[bk]], writes=[ring[i][1] for i in range(4)])
            for blk in range(4):
                r0 = 512 * t + 128 * blk
                P.dma("sp", lambda e, blk=blk, r0=r0: e.dma_start(out=out_d[r0:r0 + 128, :], in_=ringF[:, blk, 0:D]), None, reads=[ring[blk][1]])

        for t in range(NTL if cfg.get("STOP", 9) >= 3 else 0):
            _stage3_tile(t)
        with nc.Block() as block:
            P.emit(block)
    return nc


WEIGHT_KEYS = ["ffn1_norm", "ffn1_w_gate", "ffn1_w_up", "ffn1_w_down", "mix_norm", "w_in", "rg_conv_w", "rg_conv_b",
               "rg_w_a", "rg_b_a", "rg_w_x", "rg_b_x", "rg_lambda", "s5_lambda_re", "s5_lambda_im", "s5_log_dt",
               "s5_b_re", "s5_b_im", "s5_c_re", "s5_c_im", "s5_d", "s5_glu_w", "s5_glu_b", "rg_out_norm", "s5_out_norm",
               "w_out", "ffn2_norm", "ffn2_w_gate", "ffn2_w_up", "ffn2_w_down"]


def make_in_maps(cfg, inputs):
    x = np.asarray(inputs["x"], dtype=np.float32)
    meta = np.asarray(inputs["meta_tokens"], dtype=np.float32)
    shared = {}
    for k in WEIGHT_KEYS:
        a = np.asarray(inputs[k], dtype=np.float32)
        a = a.reshape(a.shape[1:])
        if k in ("rg_b_a", "rg_b_x"):
            a = a.reshape(-1)
        shared[k] = np.ascontiguousarray(a)
    shared["final_norm"] = np.ascontiguousarray(np.asarray(inputs["final_norm"], dtype=np.float32))
    maps = []
    for c in range(cfg["NCORES"]):
        m = dict(shared)
        m["x"] = np.ascontiguousarray(np.concatenate([meta, x[c]], axis=0))
        maps.append(m)
    return maps


def kernel(**inputs):
    cfg = full_cfg()
    nc = build_program(cfg)
    maps = make_in_maps(cfg, inputs)
    res = run_bass_kernel_spmd(nc, maps, core_ids=list(range(cfg["NCORES"])))
    return np.stack([np.asarray(r["out"], dtype=np.float32) for r in res.results], axis=0)
```
